# Optimizing a Trainium2 kernel written in Bass

```python
import math
import jax
import jax.numpy as jnp
from jax import lax
import numpy as np

D_MODEL = 2048
BATCH = 32
SEQ = 256
DEPTH = 2
DEC_BATCH = 2
DEC_SEQ = 4096
PAST_LEN = 512

GRID_W = 64
H_A = D_MODEL // 256
DH_A = 64
DV_A = 2 * DH_A
W_A = H_A * DV_A
C_B = D_MODEL // 2
CONV_W = 31
N_C = 64
H_C = D_MODEL // (2 * N_C)
C_C = H_C * N_C
LORA_W = 64
LORA_A = 64
N_BRANCH = 3
Q_BLOCK = 128
ROPE_BASE = 10000.0
EPS = 1e-6
LN_EPS = 1e-5
GN_EPS = 64e-5
IN_SPLITS = (H_A * 2 * DH_A, H_A * 2 * DH_A, W_A, W_A, 2 * C_B, C_B, C_C, C_C, C_C, C_C,
             2 * LORA_W, 2 * LORA_A, N_BRANCH * D_MODEL)
IN_COLS = sum(IN_SPLITS)

kernel_name = 'hybrid_diffusion_diffattn_conformer_rwkv7_step'


def rms_norm(x, w):
    xf = x.astype(jnp.float32)
    y = xf * lax.rsqrt(jnp.mean(xf * xf, axis=-1, keepdims=True) + EPS)
    return (y * w.astype(jnp.float32)).astype(x.dtype)


def split_in(p):
    idx, acc = [], 0
    for s in IN_SPLITS[:-1]:
        acc += s
        idx.append(acc)
    return jnp.split(p, idx, axis=-1)


def axial_rope(x, row, col):
    half = DH_A // 2
    quarter = half // 2
    inv = ROPE_BASE ** (-jnp.arange(quarter, dtype=jnp.float32) / quarter)

    def rot(xp, pos):
        ang = pos[:, None] * inv[None, :]
        cos = jnp.cos(ang)[None, :, None, None, :]
        sin = jnp.sin(ang)[None, :, None, None, :]
        x1, x2 = xp[..., :quarter], xp[..., quarter:]
        return jnp.concatenate([x1 * cos - x2 * sin, x2 * cos + x1 * sin], axis=-1)

    xf = x.astype(jnp.float32)
    return jnp.concatenate([rot(xf[..., :half], row), rot(xf[..., half:], col)], axis=-1).astype(x.dtype)


def diff_attention(q, k, v, lam):
    B, L = q.shape[:2]
    nb = L // Q_BLOCK
    qb = jnp.moveaxis(q.reshape(B, nb, Q_BLOCK, H_A, 2, DH_A), 1, 0)
    scale = DH_A ** -0.5

    def block(qblk):
        s = jnp.einsum('bqhmd,bkhmd->bhmqk', qblk, k).astype(jnp.float32) * scale
        p = jax.nn.softmax(s, axis=-1)
        p = p[:, :, 0] - lam * p[:, :, 1]
        return jnp.einsum('bhqk,bkhe->bqhe', p.astype(v.dtype), v)

    o = lax.map(block, qb)
    return jnp.moveaxis(o, 0, 1).reshape(B, L, H_A, DV_A)


def conformer_conv(u, conv_w, conv_b, ln_w, ln_b):
    a, g = jnp.split(u, 2, axis=-1)
    z = a * jax.nn.sigmoid(g)
    z = lax.conv_general_dilated(z, conv_w[:, None, :], window_strides=(1,),
                                 padding=[(CONV_W // 2, CONV_W // 2)],
                                 dimension_numbers=('NWC', 'WIO', 'NWC'),
                                 feature_group_count=C_B) + conv_b
    zf = z.astype(jnp.float32)
    mu = jnp.mean(zf, axis=-1, keepdims=True)
    var = jnp.mean(jnp.square(zf - mu), axis=-1, keepdims=True)
    zn = (zf - mu) * lax.rsqrt(var + LN_EPS) * ln_w.astype(jnp.float32) + ln_b.astype(jnp.float32)
    return jax.nn.silu(zn).astype(u.dtype)


def rwkv7_bidir(r, k, v, xw, xa, w0, w_up, a0, a_up, k_k, k_a, r_k, gn_w, gn_b, s0):
    f32 = jnp.float32
    B, L, _ = r.shape
    r, k, v, xw, xa = (t.astype(f32) for t in (r, k, v, xw, xa))
    w_log = -jax.nn.softplus(-(w0 + jnp.einsum('bldr,drc->bldc', jnp.tanh(xw), w_up))) - 0.5
    decay = jnp.exp(-jnp.exp(w_log))
    a = jax.nn.sigmoid(a0 + jnp.einsum('bldr,drc->bldc', xa, a_up))
    kk = (k * k_k).reshape(B, L, H_C, N_C)
    kk = kk / jnp.maximum(jnp.linalg.norm(kk, axis=-1, keepdims=True), 1e-12)
    kk = kk.reshape(B, L, C_C)
    k_dir = k[:, :, None, :] * (1.0 + (a - 1.0) * k_a)
    b_dir = kk[:, :, None, :] * a

    def heads(t):
        return t.reshape(B, L, H_C, N_C).transpose(1, 0, 2, 3)

    r_s, v_s, kk_s = heads(r), heads(v), heads(kk)

    def step(S, inp):
        r_t, w_t, k_t, v_t, kk_t, b_t = inp
        sa = jnp.einsum('bhij,bhj->bhi', S, -kk_t)
        S = S * w_t[:, :, None, :] + sa[..., None] * b_t[:, :, None, :] + v_t[..., None] * k_t[:, :, None, :]
        return S, jnp.einsum('bhij,bhj->bhi', S, r_t)

    finals = []
    y = 0.0
    for d, rev in ((0, False), (1, True)):
        s_fin, ys = lax.scan(step, s0[:, d].astype(f32),
                             (r_s, heads(decay[:, :, d]), heads(k_dir[:, :, d]), v_s, kk_s,
                              heads(b_dir[:, :, d])), reverse=rev)
        finals.append(s_fin)
        y = y + ys
    y = y.transpose(1, 0, 2, 3)
    mu = jnp.mean(y, axis=-1, keepdims=True)
    var = jnp.mean(jnp.square(y - mu), axis=-1, keepdims=True)
    y = ((y - mu) * lax.rsqrt(var + GN_EPS)).reshape(B, L, C_C) * gn_w + gn_b
    bonus = jnp.sum(r.reshape(B, L, H_C, N_C) * (k_dir[:, :, 0] + k_dir[:, :, 1]).reshape(B, L, H_C, N_C)
                    * r_k, axis=-1, keepdims=True) * v.reshape(B, L, H_C, N_C)
    y = y + bonus.reshape(B, L, C_C)
    return y, jnp.stack(finals, axis=1)


def mixer(h, lp, lam_init, pos, ctx_k, ctx_v, s0):
    B, L, _ = h.shape
    (q, k, v, g_a, glu, g_b, r_c, k_c, v_c, g_c, xw, xa, mg) = split_in(h @ lp['w_in'])
    q = q.reshape(B, L, H_A, 2, DH_A)
    k = k.reshape(B, L, H_A, 2, DH_A)
    v = v.reshape(B, L, H_A, DV_A)
    if pos is not None:
        q = axial_rope(q, pos[0], pos[1])
        k = axial_rope(k, pos[0], pos[1])
    if ctx_k is None:
        keys, vals = k, v
    else:
        keys = jnp.concatenate([ctx_k.astype(k.dtype), k], axis=1)
        vals = jnp.concatenate([ctx_v.astype(v.dtype), v], axis=1)
    f32 = jnp.float32
    lam = (jnp.exp(jnp.sum(lp['lq1'].astype(f32) * lp['lk1'].astype(f32)))
           - jnp.exp(jnp.sum(lp['lq2'].astype(f32) * lp['lk2'].astype(f32))) + lam_init)
    o = diff_attention(q, keys, vals, lam)
    o = rms_norm(o, lp['subln_w']) * (1.0 - lam_init)
    y_a = (o.reshape(B, L, W_A) * jax.nn.silu(g_a)) @ lp['w_br_a']
    z = conformer_conv(glu, lp['conv_w'], lp['conv_b'], lp['conv_ln_w'], lp['conv_ln_b'])
    y_b = (z * jax.nn.silu(g_b)) @ lp['w_br_b']
    if s0 is None:
        s0 = jnp.zeros((B, 2, H_C, N_C, N_C), jnp.float32)
    yc, s_fin = rwkv7_bidir(r_c, k_c, v_c, xw.reshape(B, L, 2, LORA_W), xa.reshape(B, L, 2, LORA_A),
                            lp['w0'], lp['w_up'], lp['a0'], lp['a_up'], lp['k_k'], lp['k_a'],
                            lp['r_k'], lp['gn_w'], lp['gn_b'], s0)
    y_c = (yc.astype(h.dtype) * jax.nn.silu(g_c)) @ lp['w_br_c']
    gates = jax.nn.sigmoid(mg.astype(f32)).reshape(B, L, N_BRANCH, D_MODEL).astype(h.dtype)
    merged = gates[:, :, 0] * y_a + gates[:, :, 1] * y_b + gates[:, :, 2] * y_c
    return merged @ lp['w_out'], k, v, s_fin


def trunk_layer(x, cond, lp, lam_init, pos, ctx_k, ctx_v, s0):
    shift, scale, gate = jnp.split(jax.nn.silu(cond) @ lp['w_ada'] + lp['b_ada'], 3, axis=-1)
    h = rms_norm(x, lp['norm_w']) * (1.0 + scale[:, None]) + shift[:, None]
    out, k, v, s_fin = mixer(h, lp, lam_init, pos, ctx_k, ctx_v, s0)
    return x + gate[:, None] * out, k, v, s_fin


def setup_inputs(seed: int = 0) -> dict:
    key = jax.random.key(seed)
    ks = iter(jax.random.split(key, 40))
    f32 = jnp.float32

    def nrm(shape, s):
        return s * jax.random.normal(next(ks), shape, f32)

    D = D_MODEL
    return {
        'x_prompt': nrm((BATCH, SEQ, D), 1.0),
        'x_sample': nrm((DEC_BATCH, DEC_SEQ, D), 1.0),
        'cache_k': nrm((DEC_BATCH, DEPTH, PAST_LEN, H_A, 2, DH_A), 1.0),
        'cache_v': nrm((DEC_BATCH, DEPTH, PAST_LEN, H_A, DV_A), 1.0),
        'state_rwkv': nrm((DEC_BATCH, DEPTH, 2, H_C, N_C, N_C), 0.3),
        'c': nrm((DEC_BATCH, D), 1.0),
        'c_ctx': nrm((D,), 1.0),
        'w_ada': nrm((DEPTH, D, 3 * D), 0.5 * D ** -0.5),
        'b_ada': nrm((DEPTH, 3 * D), 0.02),
        'norm_w': 1.0 + nrm((DEPTH, D), 0.02),
        'w_in': nrm((DEPTH, D, IN_COLS), D ** -0.5),
        'lambda_q1': nrm((DEPTH, DH_A), 0.1),
        'lambda_k1': nrm((DEPTH, DH_A), 0.1),
        'lambda_q2': nrm((DEPTH, DH_A), 0.1),
        'lambda_k2': nrm((DEPTH, DH_A), 0.1),
        'subln_w': 1.0 + nrm((DEPTH, DV_A), 0.02),
        'conv_w': nrm((DEPTH, CONV_W, C_B), CONV_W ** -0.5),
        'conv_b': nrm((DEPTH, C_B), 0.02),
        'conv_ln_w': 1.0 + nrm((DEPTH, C_B), 0.02),
        'conv_ln_b': nrm((DEPTH, C_B), 0.02),
        'rwkv_w0': -3.5 + nrm((DEPTH, 2, C_C), 1.5),
        'rwkv_w_up': nrm((DEPTH, 2, LORA_W, C_C), 0.1),
        'rwkv_a0': nrm((DEPTH, 2, C_C), 0.5),
        'rwkv_a_up': nrm((DEPTH, 2, LORA_A, C_C), 0.1),
        'rwkv_k_k': 0.85 + nrm((DEPTH, C_C), 0.05),
        'rwkv_k_a': 1.0 + nrm((DEPTH, C_C), 0.05),
        'rwkv_r_k': nrm((DEPTH, H_C, N_C), 0.1),
        'rwkv_gn_w': 1.0 + nrm((DEPTH, C_C), 0.02),
        'rwkv_gn_b': nrm((DEPTH, C_C), 0.02),
        'w_br_a': nrm((DEPTH, W_A, D), W_A ** -0.5),
        'w_br_b': nrm((DEPTH, C_B, D), C_B ** -0.5),
        'w_br_c': nrm((DEPTH, C_C, D), C_C ** -0.5),
        'w_out': nrm((DEPTH, D, D), D ** -0.5),
        'final_norm_w': 1.0 + nrm((D,), 0.02),
    }


def reference(x_prompt, x_sample, cache_k, cache_v, state_rwkv, c, c_ctx, w_ada, b_ada, norm_w, w_in,
              lambda_q1, lambda_k1, lambda_q2, lambda_k2, subln_w, conv_w, conv_b, conv_ln_w, conv_ln_b,
              rwkv_w0, rwkv_w_up, rwkv_a0, rwkv_a_up, rwkv_k_k, rwkv_k_a, rwkv_r_k, rwkv_gn_w, rwkv_gn_b,
              w_br_a, w_br_b, w_br_c, w_out, final_norm_w):
    n_lat = x_sample.shape[1]
    rows = n_lat // GRID_W
    row = jnp.repeat(jnp.arange(rows, dtype=jnp.float32), GRID_W)
    col = jnp.tile(jnp.arange(GRID_W, dtype=jnp.float32), rows)
    pos = (row, col)

    xp, xs = x_prompt, x_sample
    ks, vs, ss = [], [], []
    for l in range(DEPTH):
        lp = dict(w_ada=w_ada[l], b_ada=b_ada[l], norm_w=norm_w[l], w_in=w_in[l],
                  lq1=lambda_q1[l], lk1=lambda_k1[l], lq2=lambda_q2[l], lk2=lambda_k2[l],
                  subln_w=subln_w[l], conv_w=conv_w[l], conv_b=conv_b[l],
                  conv_ln_w=conv_ln_w[l], conv_ln_b=conv_ln_b[l],
                  w0=rwkv_w0[l], w_up=rwkv_w_up[l], a0=rwkv_a0[l], a_up=rwkv_a_up[l],
                  k_k=rwkv_k_k[l], k_a=rwkv_k_a[l], r_k=rwkv_r_k[l],
                  gn_w=rwkv_gn_w[l], gn_b=rwkv_gn_b[l],
                  w_br_a=w_br_a[l], w_br_b=w_br_b[l], w_br_c=w_br_c[l], w_out=w_out[l])
        lam_init = 0.8 - 0.6 * math.exp(-0.3 * l)
        xp, k_l, v_l, s_l = trunk_layer(xp, c_ctx[None], lp, lam_init, None, None, None, None)
        ks.append(k_l)
        vs.append(v_l)
        ss.append(s_l.astype(x_prompt.dtype))
        xs, _, _, _ = trunk_layer(xs, c, lp, lam_init, pos, cache_k[:, l], cache_v[:, l], state_rwkv[:, l])

    y_prompt = rms_norm(xp, final_norm_w)
    y_sample = rms_norm(xs, final_norm_w)
    new_cache_k = jnp.stack(ks, axis=1)
    new_cache_v = jnp.stack(vs, axis=1)
    new_state_rwkv = jnp.stack(ss, axis=1)
    return (y_prompt, y_sample, new_cache_k, new_cache_v, new_state_rwkv)
```

```python
import math
from contextlib import ExitStack
import numpy as np
import concourse.bass as bass
import concourse.mybir as mybir
from concourse.bass_utils import run_bass_kernel_spmd

F32 = mybir.dt.float32
BF16 = mybir.dt.bfloat16
AF = mybir.ActivationFunctionType
ALU = mybir.AluOpType
AX = mybir.AxisListType

D = 2048
KT = 16
INC = 17664
DEPTH = 2
C0 = math.exp(-0.5)
EPS = 1e-6
LN_EPS = 1e-5
GN_EPS = 64e-5
NPC = 9216
NPT = 8448


class Trk:
    def __init__(s, nc, es):
        s.nc = nc
        s.es = es
        s.eng = {'p': nc.tensor, 'v': nc.vector, 'a': nc.scalar, 'g': nc.gpsimd, 's': nc.sync}
        s.sem = {}
        s.cnt = {}
        s.waited = {}
        s.lastw = {}
        s.readers = {}
        s.nins = 0

    def _sem(s, key):
        if key not in s.sem:
            s.sem[key] = s.es.enter_context(s.nc.semaphore("sm%d" % len(s.sem)))
            s.cnt[key] = 0
        return s.sem[key]

    def _deps(s, R, W):
        d = {}
        for b in R:
            t = s.lastw.get(b)
            if t is not None and d.get(t[0], 0) < t[1]:
                d[t[0]] = t[1]
        for b in W:
            t = s.lastw.get(b)
            if t is not None and d.get(t[0], 0) < t[1]:
                d[t[0]] = t[1]
            rd = s.readers.get(b)
            if rd:
                for k, v in rd.items():
                    if d.get(k, 0) < v:
                        d[k] = v
        return d

    def _wait(s, e, d, skip_self=False):
        for k, v in d.items():
            if skip_self and k == e:
                continue
            if s.waited.get((e, k), 0) < v:
                s.eng[e].wait_ge(s._sem(k), v)
                s.waited[(e, k)] = v
                s.nins += 1

    def _record(s, tok, R, W):
        for b in W:
            s.lastw[b] = tok
            s.readers[b] = {}
        for b in R:
            rd = s.readers.setdefault(b, {})
            if rd.get(tok[0], 0) < tok[1]:
                rd[tok[0]] = tok[1]

    def op(s, e, fn, R=(), W=()):
        s._wait(e, s._deps(R, W), skip_self=(e == 'p'))
        ins = fn()
        sem = s._sem(e)
        s.cnt[e] += 1
        ins.then_inc(sem, 1)
        s._record((e, s.cnt[e]), R, W)
        s.nins += 1

    def dma(s, q, out, in_, stream, R=(), W=()):
        s._wait(q, s._deps(R, W))
        sem = s._sem(stream)
        s.cnt[stream] += 16
        s.eng[q].dma_start(out=out, in_=in_).then_inc(sem, 16)
        s._record((stream, s.cnt[stream]), R, W)
        s.nins += 1

    def barrier(s):
        keys = list(s.cnt.keys())
        for e in ('p', 'v', 'a', 'g', 's'):
            for k in keys:
                v = s.cnt[k]
                if v > 0 and s.waited.get((e, k), 0) < v:
                    s.eng[e].wait_ge(s._sem(k), v)
                    s.waited[(e, k)] = v
        s.lastw = {}
        s.readers = {}


class Ctx:
    pass


def build(cfg):
    Ls, Lp, NP, PAST = cfg['Ls'], cfg['Lp'], cfg['NP'], cfg['PAST']
    NT = Ls + NP * Lp
    assert Ls % 512 == 0 and (NP * Lp) % 512 == 0 and Lp % 256 == 0 and PAST % 128 == 0
    nc = bass.Bass("TRN2", target_bir_lowering=False)
    K = Ctx()
    K.nc = nc
    K.cfg = cfg
    K.NT = NT

    def din(name, shape, dt=F32):
        return nc.dram_tensor(name, list(shape), dt, kind="ExternalInput").ap()

    def dout(name, shape):
        return nc.dram_tensor(name, list(shape), F32, kind="ExternalOutput").ap()

    def dscr(name, shape, dt=F32):
        return nc.dram_tensor(name, list(shape), dt, kind="Internal").ap()

    I = {}
    I['x_all'] = din('x_all', [NT, D])
    I['cvecT'] = din('cvecT', [128, 32])
    I['ck'] = din('ck', [DEPTH, PAST, 1024])
    I['cv'] = din('cv', [DEPTH, PAST, 1024])
    I['st0T'] = din('st0T', [DEPTH, 2, 16, 64, 64])
    I['w_ada'] = din('w_ada', [DEPTH, D, 3 * D])
    I['b_ada'] = din('b_ada', [DEPTH, 3 * D])
    I['norm_w'] = din('norm_w', [DEPTH, D])
    I['w_in'] = din('w_in', [DEPTH, D, INC])
    I['lam4'] = din('lam4', [DEPTH, 256])
    I['subln_w'] = din('subln_w', [DEPTH, 128])
    I['conv_wT'] = din('conv_wT', [DEPTH, 1024, 31])
    I['conv_bT'] = din('conv_bT', [DEPTH, 128, 8])
    I['conv_ln_w'] = din('conv_ln_w', [DEPTH, 1024])
    I['conv_ln_b'] = din('conv_ln_b', [DEPTH, 1024])
    I['rwkv_w0'] = din('rwkv_w0', [DEPTH, 2, 1024])
    I['rwkv_w_up'] = din('rwkv_w_up', [DEPTH, 2, 64, 1024])
    I['rwkv_a0'] = din('rwkv_a0', [DEPTH, 2, 1024])
    I['rwkv_a_up'] = din('rwkv_a_up', [DEPTH, 2, 64, 1024])
    I['rwkv_k_k'] = din('rwkv_k_k', [DEPTH, 1024])
    I['rwkv_k_a'] = din('rwkv_k_a', [DEPTH, 1024])
    I['rwkv_r_k'] = din('rwkv_r_k', [DEPTH, 1024])
    I['rwkv_gn_w'] = din('rwkv_gn_w', [DEPTH, 1024])
    I['rwkv_gn_b'] = din('rwkv_gn_b', [DEPTH, 1024])
    I['w_br_a'] = din('w_br_a', [DEPTH, 1024, D])
    I['w_br_b'] = din('w_br_b', [DEPTH, 1024, D])
    I['w_br_c'] = din('w_br_c', [DEPTH, 1024, D])
    I['w_out'] = din('w_out', [DEPTH, D, D])
    I['final_norm_w'] = din('final_norm_w', [1, D])
    I['ident'] = din('ident', [128, 128])
    I['ropetab'] = din('ropetab', [Ls, 128])
    I['maskX'] = din('maskX', [2, 128, 384])
    I['maskY'] = din('maskY', [2, 128, 256])
    I['tri'] = din('tri', [2, 128, 128])
    K.I = I
    O = {}
    O['y_all'] = dout('y_all', [NT, D])
    O['out_k'] = dout('out_k', [NP, DEPTH, Lp, 1024])
    O['out_v'] = dout('out_v', [NP, DEPTH, Lp, 1024])
    O['out_st'] = dout('out_st', [NP, DEPTH, 2, 16, 64, 64])
    K.O = O
    S = {}
    S['X'] = dscr('X', [NT, D])
    S['P'] = dscr('P', [NT, NPC])
    S['PT'] = dscr('PT', [NPT, NT])
    S['Wb'] = dscr('Wb', [DEPTH, D, INC], BF16)
    S['Wa'] = dscr('Wba', [DEPTH, 1024, D], BF16)
    S['Wbb'] = dscr('Wbb', [DEPTH, 1024, D], BF16)
    S['Wc'] = dscr('Wbc', [DEPTH, 1024, D], BF16)
    S['Wo'] = dscr('Wbo', [DEPTH, D, D], BF16)
    S['ACTA'] = dscr('ACTA', [NT, 1024])
    S['ACTB'] = dscr('ACTB', [NT, 1024])
    S['ACTC'] = dscr('ACTC', [NT, 1024])
    S['ZC'] = dscr('ZC', [NT, 1024])
    S['YC0'] = dscr('YC0', [NT, 1024])
    S['YC1'] = dscr('YC1', [NT, 1024])
    S['BON'] = dscr('BON', [2, NT, 16])
    S['ADAd'] = dscr('ADAd', [2, 128, 3 * D])
    if cfg.get('debug'):
        S['Hdbg'] = dscr('Hdbg', [NT, D])
        S['HTdbg'] = dscr('HTdbg', [NT // 512, 128, KT * 512], BF16)
    K.S = S

    seqs = [dict(r0=0, L=Ls, ctx=PAST, rope=True, g=0, pi=None)]
    for i in range(NP):
        seqs.append(dict(r0=Ls + i * Lp, L=Lp, ctx=0, rope=False, g=1, pi=i))
    K.seqs = seqs

    with ExitStack() as es:
        T = Trk(nc, es)
        K.T = T
        K.uid = 0

        def sb(stack, shape, dt=F32, nm="t"):
            K.uid += 1
            return stack.enter_context(nc.sbuf_tensor("%s%d" % (nm, K.uid), list(shape), dt))

        def pb(stack, shape=(128, 512), dt=F32, nm="ps"):
            K.uid += 1
            return stack.enter_context(nc.psum_tensor("%s%d" % (nm, K.uid), list(shape), dt))

        K.sb = sb
        K.pb = pb
        K.ident = sb(es, [128, 128])
        T.dma('s', K.ident[:], I['ident'][:, :], 'c0')
        T.barrier()

        ORD = '0aABCDEZ'
        stop = ORD.index(cfg.get('stop', 'Z'))
        nl = cfg.get('nl', DEPTH)
        phase_convert(K)
        T.dma('s', S['X'][:, :], I['x_all'][:, :], 'c0')
        T.barrier()
        for l in range(nl):
            if stop >= ORD.index('a'):
                phase_ada(K, l)
            if stop >= ORD.index('A'):
                phase_A(K, l)
            if stop >= ORD.index('B'):
                phase_B(K, l)
            if stop >= ORD.index('C'):
                phase_C(K, l)
            if stop >= ORD.index('D'):
                phase_D(K, l)
            if stop >= ORD.index('E'):
                phase_E(K, l)
        T.barrier()
        for nm in cfg.get('taps', ()):
            src = S[nm]
            shp = list(src.shape)
            dst = nc.dram_tensor('tap_' + nm, shp, src.dtype, kind="ExternalOutput").ap()
            T.dma('s', dst, src, 'c0')
        T.barrier()
        print("instructions:", T.nins, "sems:", len(T.sem))
    return nc


def phase_convert(K):
    nc, T, I, S = K.nc, K.T, K.I, K.S
    with ExitStack() as ph:
        NS = 4
        CW = 2048
        fin = [K.sb(ph, [128, CW]) for _ in range(NS)]
        fout = [K.sb(ph, [128, CW], BF16) for _ in range(NS)]
        engs = ['v', 'a', 'g', 'v']
        cnt = [0]

        def conv(src, dst, Rr, Cc):
            for r0 in range(0, Rr, 128):
                for c0 in range(0, Cc, CW):
                    cw = min(CW, Cc - c0)
                    i = cnt[0]
                    s = i % NS
                    cnt[0] += 1
                    T.dma('s', fin[s][:, :cw], src[r0:r0 + 128, c0:c0 + cw], 'cvi%d' % s, W=[('cvi', s)])
                    e = engs[s]
                    if e == 'a':
                        T.op('a', lambda: nc.scalar.copy(out=fout[s][:, :cw], in_=fin[s][:, :cw]),
                             R=[('cvi', s)], W=[('cvo', s)])
                    elif e == 'v':
                        T.op('v', lambda: nc.vector.tensor_copy(out=fout[s][:, :cw], in_=fin[s][:, :cw]),
                             R=[('cvi', s)], W=[('cvo', s)])
                    else:
                        T.op('g', lambda: nc.gpsimd.tensor_copy(out=fout[s][:, :cw], in_=fin[s][:, :cw]),
                             R=[('cvi', s)], W=[('cvo', s)])
                    T.dma('g', dst[r0:r0 + 128, c0:c0 + cw], fout[s][:, :cw], 'cvo%d' % s, R=[('cvo', s)])

        for l in range(DEPTH):
            conv(I['w_in'][l], S['Wb'][l], D, INC)
            conv(I['w_br_a'][l], S['Wa'][l], 1024, D)
            conv(I['w_br_b'][l], S['Wbb'][l], 1024, D)
            conv(I['w_br_c'][l], S['Wc'][l], 1024, D)
            conv(I['w_out'][l], S['Wo'][l], D, D)
        T.barrier()


def phase_ada(K, l):
    nc, T, I = K.nc, K.T, K.I
    with ExitStack() as ph:
        cvt = K.sb(ph, [128, 32])
        CL = K.sb(ph, [128, 32, 128])
        ones1 = K.sb(ph, [1, 128])
        wa = [K.sb(ph, [128, KT, 512]) for _ in range(2)]
        bb = [K.sb(ph, [1, 512]) for _ in range(2)]
        nwb = K.sb(ph, [128, D])
        ADA = [K.sb(ph, [128, 3 * D]) for _ in range(2)]
        ps = [K.pb(ph) for _ in range(2)]
        T.dma('s', cvt[:], I['cvecT'][:, :], 'c0', W=['cvt'])
        T.dma('s', nwb[:], I['norm_w'][l:l + 1, :].to_broadcast([128, D]), 'c0', W=['nwb'])
        T.barrier()
        T.op('a', lambda: nc.scalar.activation(out=cvt[:], in_=cvt[:], func=AF.Silu), R=['cvt'], W=['cvt'])
        T.op('v', lambda: nc.vector.tensor_copy(out=CL[:], in_=cvt[:, :, None].to_broadcast([128, 32, 128])),
             R=['cvt'], W=['CL'])
        T.op('v', lambda: nc.vector.memset(ones1[:], 1.0), W=['ones1'])
        wv = I['w_ada'][l].rearrange("(kt p) c -> p kt c", p=128)
        n = 0
        for cb in range(12):
            s = cb % 2
            T.dma('s', bb[s][:], I['b_ada'][l:l + 1, cb * 512:(cb + 1) * 512], 'wa%d' % s, W=[('bb', s)])
            T.dma('s', wa[s][:], wv[:, :, cb * 512:(cb + 1) * 512], 'wa%d' % s, W=[('wa', s), ('bb', s)])
            for g in range(2):
                p_ = ps[n % 2]
                for kt in range(KT):
                    T.op('p', lambda: nc.tensor.matmul(p_[:, :], lhsT=CL[:, g * 16 + kt, :], rhs=wa[s][:, kt, :],
                                                       start=(kt == 0), stop=False),
                         R=['CL', ('wa', s)], W=[('ps', n % 2)])
                T.op('p', lambda: nc.tensor.matmul(p_[:, :], lhsT=ones1[0:1, :], rhs=bb[s][0:1, :],
                                                   start=False, stop=True),
                     R=['ones1', ('bb', s)], W=[('ps', n % 2)])
                dst = ADA[g][:, cb * 512:(cb + 1) * 512]
                if n % 2 == 0:
                    T.op('v', lambda: nc.vector.tensor_copy(out=dst, in_=p_[:, :]), R=[('ps', n % 2)], W=[('ADA', g, cb)])
                else:
                    T.op('a', lambda: nc.scalar.copy(out=dst, in_=p_[:, :]), R=[('ps', n % 2)], W=[('ADA', g, cb)])
                n += 1
        T.barrier()
        for g in range(2):
            T.op('v', lambda: nc.vector.scalar_tensor_tensor(out=ADA[g][:, D:2 * D], in0=ADA[g][:, D:2 * D], scalar=1.0,
                                                             in1=nwb[:], op0=ALU.add, op1=ALU.mult))
        T.barrier()
        for g in range(2):
            for j in range(3):
                T.dma('s', K.S['ADAd'][g][:, j * D:(j + 1) * D], ADA[g][:, j * D:(j + 1) * D], 'c0')
        T.barrier()
        if K.cfg.get('debug') and l == 0:
            dbg = nc.dram_tensor('dbg_ada', [2, 128, 3 * D], F32, kind="ExternalOutput").ap()
            for g in range(2):
                for j in range(3):
                    T.dma('s', dbg[g][:, j * D:(j + 1) * D], ADA[g][:, j * D:(j + 1) * D], 'c0')
            dbg2 = nc.dram_tensor('dbg_cl', [128, 32 * 128], F32, kind="ExternalOutput").ap()
            for j in range(2):
                T.dma('s', dbg2[:, j * 2048:(j + 1) * 2048], CL[:, j * 16:(j + 1) * 16, :].rearrange("p a b -> p (a b)"), 'c0')
            dbg3 = nc.dram_tensor('dbg_cvt', [128, 32], F32, kind="ExternalOutput").ap()
            T.dma('s', dbg3, cvt[:], 'c0')
            T.barrier()


def a_blocks():
    blks = []
    for i in range(8):
        blks.append(('TM', i * 512, 512, i * 512))
    for i in range(4):
        blks.append(('FM', 4096 + i * 512, 512, i * 512))
    for i in range(10):
        blks.append(('TM', 6144 + i * 512, 512, 4096 + i * 512))
    blks.append(('FM', 11264, 256, 2048))
    for i in range(12):
        blks.append(('FM', 11520 + i * 512, 512, 2304 + i * 512))
    return blks


def phase_A(K, l):
    nc, T, I, S = K.nc, K.T, K.I, K.S
    NT, Ls = K.NT, K.cfg['Ls']
    with ExitStack() as ph:
        xin = [K.sb(ph, [128, D]) for _ in range(2)]
        hh = [K.sb(ph, [128, D]) for _ in range(2)]
        sq = K.sb(ph, [128, D])
        st = [K.sb(ph, [128, 4]) for _ in range(2)]
        hT = K.sb(ph, [128, KT, 512], BF16)
        wblk = [K.sb(ph, [128, KT, 512], BF16) for _ in range(2)]
        stg = [K.sb(ph, [128, 512]) for _ in range(4)]
        pst = [K.pb(ph) for _ in range(2)]
        psm = [K.pb(ph) for _ in range(4)]
        wv = S['Wb'][l].rearrange("(kt p) c -> p kt c", p=128)
        ADA = [K.sb(ph, [128, 2 * D]) for _ in range(2)]
        for g in range(2):
            for j in range(2):
                T.dma('s', ADA[g][:, j * D:(j + 1) * D], S['ADAd'][g][:, j * D:(j + 1) * D], 'c0')
        T.barrier()
        blks = a_blocks()
        nt = 0
        nm = 0
        nw = 0
        for tb in range(NT // 512):
            g = 0 if tb * 512 < Ls else 1
            A_g = ADA[g][:, D:2 * D]
            sh_g = ADA[g][:, 0:D]
            for sub in range(4):
                r0 = tb * 512 + sub * 128
                s = sub % 2
                T.dma('s', xin[s][:], S['X'][r0:r0 + 128, :], 'xin%d' % s, W=[('xin', s)])
                T.op('v', lambda: nc.vector.tensor_tensor(out=sq[:], in0=xin[s][:], in1=xin[s][:], op=ALU.mult),
                     R=[('xin', s)], W=['sq'])
                T.op('v', lambda: nc.vector.tensor_reduce(out=st[s][:, 0:1], in_=sq[:], axis=AX.X, op=ALU.add),
                     R=['sq'], W=[('st', s)])
                T.op('a', lambda: nc.scalar.activation(out=st[s][:, 1:2], in_=st[s][:, 0:1], func=AF.Sqrt,
                                                       bias=EPS, scale=1.0 / D), R=[('st', s)], W=[('st', s)])
                T.op('v', lambda: nc.vector.reciprocal(out=st[s][:, 2:3], in_=st[s][:, 1:2]), R=[('st', s)], W=[('st', s)])
                T.op('v', lambda: nc.vector.scalar_tensor_tensor(out=hh[s][:], in0=xin[s][:], scalar=st[s][:, 2:3],
                                                                 in1=A_g, op0=ALU.mult, op1=ALU.mult),
                     R=[('xin', s), ('st', s)], W=[('hh', s)])
                T.op('v', lambda: nc.vector.tensor_tensor(out=hh[s][:], in0=hh[s][:], in1=sh_g, op=ALU.add),
                     R=[('hh', s)], W=[('hh', s)])
                if K.cfg.get('debug'):
                    T.dma('g', S['Hdbg'][r0:r0 + 128, :], hh[s][:], 'dbgh', R=[('hh', s)])
                    T.barrier()
                for q4 in range(4):
                    p_ = pst[nt % 2]
                    for j in range(4):
                        kt = q4 * 4 + j
                        T.op('p', lambda: nc.tensor.transpose(out=p_[:, j * 128:(j + 1) * 128],
                                                              in_=hh[s][:, kt * 128:(kt + 1) * 128], identity=K.ident[:]),
                             R=[('hh', s)], W=[('pst', nt % 2)])
                    dst = hT[:, q4 * 4:q4 * 4 + 4, sub * 128:(sub + 1) * 128]
                    src = p_[:, :].rearrange("p (j t) -> p j t", j=4)
                    if nt % 2 == 0:
                        T.op('a', lambda: nc.scalar.copy(out=dst, in_=src), R=[('pst', nt % 2)], W=[('hT', sub)])
                    else:
                        T.op('v', lambda: nc.vector.tensor_copy(out=dst, in_=src), R=[('pst', nt % 2)], W=[('hT', sub)])
                    nt += 1
            if K.cfg.get('debug'):
                T.dma('g', S['HTdbg'][tb], hT[:, :, :].rearrange('p a b -> p (a b)'), 'dbgh', R=[('hT', 0), ('hT', 1), ('hT', 2), ('hT', 3)])
                T.barrier()
            for (mode, c0, ncol, d0) in blks:
                ws = nw % 2
                nw += 1
                T.dma('s', wblk[ws][:, :, :ncol], wv[:, :, c0:c0 + ncol], 'wblk%d' % ws, W=[('wblk', ws)])
                if mode == 'TM':
                    for sub in range(4):
                        r0 = tb * 512 + sub * 128
                        pi = nm % 4
                        p_ = psm[pi]
                        for kt in range(KT):
                            T.op('p', lambda: nc.tensor.matmul(p_[:, :ncol], lhsT=hT[:, kt, sub * 128:(sub + 1) * 128],
                                                               rhs=wblk[ws][:, kt, :ncol], start=(kt == 0), stop=(kt == KT - 1)),
                                 R=[('hT', sub), ('wblk', ws)], W=[('psm', pi)])
                        if nm % 2 == 0:
                            T.op('a', lambda: nc.scalar.copy(out=stg[pi][:, :ncol], in_=p_[:, :ncol]), R=[('psm', pi)], W=[('stg', pi)])
                        else:
                            T.op('v', lambda: nc.vector.tensor_copy(out=stg[pi][:, :ncol], in_=p_[:, :ncol]), R=[('psm', pi)], W=[('stg', pi)])
                        T.dma('g', S['P'][r0:r0 + 128, d0:d0 + ncol], stg[pi][:, :ncol], 'stg%d' % pi, R=[('stg', pi)])
                        nm += 1
                else:
                    for cs in range(ncol // 128):
                        pi = nm % 4
                        p_ = psm[pi]
                        for kt in range(KT):
                            T.op('p', lambda: nc.tensor.matmul(p_[:, :], lhsT=wblk[ws][:, kt, cs * 128:(cs + 1) * 128],
                                                               rhs=hT[:, kt, :], start=(kt == 0), stop=(kt == KT - 1)),
                                 R=[('hT', 0), ('hT', 1), ('hT', 2), ('hT', 3), ('wblk', ws)], W=[('psm', pi)])
                        if nm % 2 == 0:
                            T.op('a', lambda: nc.scalar.copy(out=stg[pi][:], in_=p_[:, :]), R=[('psm', pi)], W=[('stg', pi)])
                        else:
                            T.op('v', lambda: nc.vector.tensor_copy(out=stg[pi][:], in_=p_[:, :]), R=[('psm', pi)], W=[('stg', pi)])
                        T.dma('g', S['PT'][d0 + cs * 128:d0 + (cs + 1) * 128, tb * 512:(tb + 1) * 512], stg[pi][:],
                              'stg%d' % pi, R=[('stg', pi)])
                        nm += 1
        T.barrier()
        for sq_ in K.seqs:
            if sq_['pi'] is None:
                continue
            r0, L, pi = sq_['r0'], sq_['L'], sq_['pi']
            T.dma('s', K.O['out_k'][pi, l, :, :], S['P'][r0:r0 + L, 1024:2048], 'c0')
            T.dma('s', K.O['out_v'][pi, l, :, :], S['P'][r0:r0 + L, 2048:3072], 'c0')
        T.barrier()


def phase_B(K, l):
    nc, T, I, S = K.nc, K.T, K.I, K.S
    cfg = K.cfg
    Ls, PAST = cfg['Ls'], cfg['PAST']
    lam_init = 0.8 - 0.6 * math.exp(-0.3 * l)
    NKmax = PAST + Ls
    with ExitStack() as ph:
        KTt = K.sb(ph, [128, NKmax], BF16)
        QTp = K.sb(ph, [128, Ls // 256, 2, 256], BF16)
        V1 = K.sb(ph, [128, NKmax // 128, 130], BF16)
        raw = [K.sb(ph, [128, 128]) for _ in range(4)]
        vraw = [K.sb(ph, [128, 128]) for _ in range(2)]
        rot = [K.sb(ph, [128, 128]) for _ in range(2)]
        tmp = [K.sb(ph, [128, 64]) for _ in range(2)]
        tab = [K.sb(ph, [128, 128]) for _ in range(2)]
        E = [K.sb(ph, [128, 512], BF16) for _ in range(4)]
        rr = [K.sb(ph, [128, 8]) for _ in range(2)]
        o_ = [K.sb(ph, [128, 128]) for _ in range(2)]
        o2 = K.sb(ph, [128, 128])
        ga = [K.sb(ph, [128, 128]) for _ in range(2)]
        subw = K.sb(ph, [128, 128])
        lq = K.sb(ph, [128, 256])
        lt = K.sb(ph, [128, 128])
        lam = K.sb(ph, [128, 4])
        ps_s = [K.pb(ph) for _ in range(4)]
        acc = [K.pb(ph) for _ in range(4)]
        ps_t = ps_s
        T.dma('s', subw[:], I['subln_w'][l:l + 1, :].to_broadcast([128, 128]), 'c0')
        T.dma('s', lq[:], I['lam4'][l:l + 1, :].to_broadcast([128, 256]), 'c0')
        T.op('v', lambda: nc.vector.memset(V1[:], 1.0))
        T.op('v', lambda: nc.vector.memset(QTp[:], 0.0))
        T.barrier()
        T.op('v', lambda: nc.vector.tensor_scalar(out=subw[:], in0=subw[:], scalar1=(1.0 - lam_init), scalar2=None, op0=ALU.mult))
        T.op('v', lambda: nc.vector.tensor_tensor(out=lt[:, 0:64], in0=lq[:, 0:64], in1=lq[:, 64:128], op=ALU.mult))
        T.op('v', lambda: nc.vector.tensor_tensor(out=lt[:, 64:128], in0=lq[:, 128:192], in1=lq[:, 192:256], op=ALU.mult))
        T.barrier()
        T.op('v', lambda: nc.vector.tensor_reduce(out=lam[:, 0:2], in_=lt[:, :].rearrange("p (a b) -> p a b", a=2), axis=AX.X, op=ALU.add))
        T.barrier()
        T.op('a', lambda: nc.scalar.activation(out=lam[:, 0:2], in_=lam[:, 0:2], func=AF.Exp))
        T.barrier()
        T.op('v', lambda: nc.vector.scalar_tensor_tensor(out=lam[:, 2:3], in0=lam[:, 1:2], scalar=-lam_init, in1=lam[:, 0:1],
                                                         op0=ALU.add, op1=ALU.subtract))
        T.barrier()
        nlam = lam[:, 2:3]
        cnt = dict(t=0, r=0, v=0, s=0, e=0, o=0)

        def rope(src, tb_idx):
            ti = cnt['r'] % 2
            cnt['r'] += 1
            T.dma('s', tab[ti][:], I['ropetab'][tb_idx * 128:(tb_idx + 1) * 128, :], 'tab%d' % ti, W=[('tab', ti)])
            x = raw[src][:, :].rearrange("p (a b c) -> p a b c", a=4, b=2)
            x1, x2 = x[:, :, 0, :], x[:, :, 1, :]
            cosv = tab[ti][:, 0:64].rearrange("p (a c) -> p a c", a=4)
            sinv = tab[ti][:, 64:128].rearrange("p (a c) -> p a c", a=4)
            y = rot[ti][:, :].rearrange("p (a b c) -> p a b c", a=4, b=2)
            y1, y2 = y[:, :, 0, :], y[:, :, 1, :]
            t1 = tmp[0][:, :].rearrange("p (a c) -> p a c", a=4)
            t2 = tmp[1][:, :].rearrange("p (a c) -> p a c", a=4)
            Rk = [('raw', src), ('tab', ti)]
            T.op('v', lambda: nc.vector.tensor_tensor(out=y1, in0=x1, in1=cosv, op=ALU.mult), R=Rk, W=[('rot', ti)])
            T.op('v', lambda: nc.vector.tensor_tensor(out=t1, in0=x2, in1=sinv, op=ALU.mult), R=Rk, W=[('tmp', 0)])
            T.op('v', lambda: nc.vector.tensor_tensor(out=y2, in0=x2, in1=cosv, op=ALU.mult), R=Rk, W=[('rot', ti)])
            T.op('v', lambda: nc.vector.tensor_tensor(out=t2, in0=x1, in1=sinv, op=ALU.mult), R=Rk, W=[('tmp', 1)])
            T.op('v', lambda: nc.vector.tensor_tensor(out=y1, in0=y1, in1=t1, op=ALU.subtract), R=[('rot', ti), ('tmp', 0)], W=[('rot', ti)])
            T.op('v', lambda: nc.vector.tensor_tensor(out=y2, in0=y2, in1=t2, op=ALU.add), R=[('rot', ti), ('tmp', 1)], W=[('rot', ti)])
            return rot[ti], ('rot', ti)

        def transpose_to(src_tile, src_key, dst_ap, dst_key, split=None):
            ti = cnt['t'] % 2
            cnt['t'] += 1
            T.op('p', lambda: nc.tensor.transpose(out=ps_t[ti][:, 0:128], in_=src_tile[:, :], identity=K.ident[:]),
                 R=[src_key], W=[('pss', ti)])
            if split is None:
                parts = [(dst_ap, ps_t[ti][:, 0:128])]
            else:
                parts = [(split[0], ps_t[ti][0:64, 0:128]), (split[1], ps_t[ti][64:128, 0:128])]
            for (dap, sap) in parts:
                if ti == 0:
                    T.op('a', lambda: nc.scalar.copy(out=dap, in_=sap), W=[('pss', ti), dst_key])
                else:
                    T.op('v', lambda: nc.vector.tensor_copy(out=dap, in_=sap), W=[('pss', ti), dst_key])

        for sq_ in K.seqs:
            r0, L, ctx, do_rope = sq_['r0'], sq_['L'], sq_['ctx'], sq_['rope']
            nctx = ctx // 128
            nkt = nctx + L // 128
            for h in range(8):
                for j in range(nkt):
                    ri = cnt['v'] % 4
                    vi = cnt['v'] % 2
                    cnt['v'] += 1
                    if j < nctx:
                        srck = I['ck'][l, j * 128:(j + 1) * 128, h * 128:(h + 1) * 128]
                        srcv = I['cv'][l, j * 128:(j + 1) * 128, h * 128:(h + 1) * 128]
                    else:
                        rows = r0 + (j - nctx) * 128
                        srck = S['P'][rows:rows + 128, 1024 + h * 128:1024 + (h + 1) * 128]
                        srcv = S['P'][rows:rows + 128, 2048 + h * 128:2048 + (h + 1) * 128]
                    T.dma('s', raw[ri][:], srck, 'raw%d' % ri, W=[('raw', ri)])
                    T.dma('s', vraw[vi][:], srcv, 'vraw%d' % vi, W=[('vraw', vi)])
                    if do_rope and j >= nctx:
                        tl, tk = rope(ri, j - nctx)
                    else:
                        tl, tk = raw[ri], ('raw', ri)
                    transpose_to(tl, tk, KTt[:, j * 128:(j + 1) * 128], ('KT', j))
                    T.op('a', lambda: nc.scalar.copy(out=V1[:, j, 0:128], in_=vraw[vi][:]), R=[('vraw', vi)], W=[('V1', j)])
                for qt in range(L // 128):
                    ri = cnt['v'] % 4
                    cnt['v'] += 1
                    rows = r0 + qt * 128
                    T.dma('s', raw[ri][:], S['P'][rows:rows + 128, h * 128:(h + 1) * 128], 'raw%d' % ri, W=[('raw', ri)])
                    if do_rope:
                        tl, tk = rope(ri, qt)
                    else:
                        tl, tk = raw[ri], ('raw', ri)
                    qo = (qt % 2) * 128
                    transpose_to(tl, tk, None, ('QT', qt),
                                 split=(QTp[0:64, qt // 2, 0, qo:qo + 128], QTp[64:128, qt // 2, 1, qo:qo + 128]))
                for qb in range(L // 256):
                    for j in range(nkt):
                        si = cnt['s'] % 4
                        cnt['s'] += 1
                        Rq = [('KT', j), ('QT', 2 * qb), ('QT', 2 * qb + 1)]
                        T.op('p', lambda: nc.tensor.matmul(ps_s[si][:, 0:512], lhsT=KTt[:, j * 128:(j + 1) * 128],
                                                           rhs=QTp[:, qb, :, :].rearrange("p m q -> p (m q)"),
                                                           start=True, stop=True), R=Rq, W=[('pss', si)])
                        T.op('a', lambda: nc.scalar.activation(out=E[si][:, :], in_=ps_s[si][:, 0:512], func=AF.Exp, scale=0.125),
                             W=[('pss', si), ('E', si)])
                        for m in range(2):
                            for sub in range(2):
                                a_i = m * 2 + sub
                                T.op('p', lambda: nc.tensor.matmul(acc[a_i][:, 0:129],
                                                                   lhsT=E[si][:, m * 256 + sub * 128:m * 256 + (sub + 1) * 128],
                                                                   rhs=V1[:, j, 0:129], start=(j == 0), stop=(j == nkt - 1)),
                                     R=[('E', si), ('V1', j)], W=[('acc', a_i)])
                    for sub in range(2):
                        oi = cnt['o'] % 2
                        cnt['o'] += 1
                        rows = r0 + qb * 256 + sub * 128
                        O1, O2 = acc[sub], acc[2 + sub]
                        r_ = rr[oi]
                        o = o_[oi]
                        T.dma('s', ga[oi][:], S['P'][rows:rows + 128, 3072 + h * 128:3072 + (h + 1) * 128], 'ga%d' % oi, W=[('ga', oi)])
                        T.op('a', lambda: nc.scalar.activation(out=ga[oi][:], in_=ga[oi][:], func=AF.Silu), R=[('ga', oi)], W=[('ga', oi)])
                        T.op('v', lambda: nc.vector.reciprocal(out=r_[:, 0:1], in_=O1[:, 128:129]), W=[('acc', sub), ('rr', oi)])
                        T.op('v', lambda: nc.vector.reciprocal(out=r_[:, 1:2], in_=O2[:, 128:129]), W=[('acc', 2 + sub), ('rr', oi)])
                        T.op('v', lambda: nc.vector.tensor_tensor(out=r_[:, 2:3], in0=r_[:, 1:2], in1=nlam, op=ALU.mult), R=[('rr', oi)], W=[('rr', oi)])
                        T.op('v', lambda: nc.vector.tensor_scalar(out=o[:], in0=O1[:, 0:128], scalar1=r_[:, 0:1], scalar2=None, op0=ALU.mult),
                             R=[('rr', oi)], W=[('acc', sub), ('o', oi)])
                        T.op('v', lambda: nc.vector.scalar_tensor_tensor(out=o[:], in0=O2[:, 0:128], scalar=r_[:, 2:3], in1=o[:],
                                                                         op0=ALU.mult, op1=ALU.add),
                             R=[('rr', oi), ('o', oi)], W=[('acc', 2 + sub), ('o', oi)])
                        T.op('v', lambda: nc.vector.tensor_tensor(out=o2[:], in0=o[:], in1=o[:], op=ALU.mult), R=[('o', oi)], W=['o2'])
                        T.op('v', lambda: nc.vector.tensor_reduce(out=r_[:, 3:4], in_=o2[:], axis=AX.X, op=ALU.add), R=['o2'], W=[('rr', oi)])
                        T.op('a', lambda: nc.scalar.activation(out=r_[:, 4:5], in_=r_[:, 3:4], func=AF.Sqrt, bias=EPS, scale=1.0 / 128),
                             R=[('rr', oi)], W=[('rr', oi)])
                        T.op('v', lambda: nc.vector.reciprocal(out=r_[:, 5:6], in_=r_[:, 4:5]), R=[('rr', oi)], W=[('rr', oi)])
                        T.op('v', lambda: nc.vector.scalar_tensor_tensor(out=o[:], in0=o[:], scalar=r_[:, 5:6], in1=subw[:],
                                                                         op0=ALU.mult, op1=ALU.mult), R=[('o', oi), ('rr', oi)], W=[('o', oi)])
                        T.op('v', lambda: nc.vector.tensor_tensor(out=o[:], in0=o[:], in1=ga[oi][:], op=ALU.mult),
                             R=[('o', oi), ('ga', oi)], W=[('o', oi)])
                        T.dma('g', S['ACTA'][rows:rows + 128, h * 128:(h + 1) * 128], o[:], 'oa%d' % oi, R=[('o', oi)])
        T.barrier()


def phase_C(K, l):
    nc, T, I, S = K.nc, K.T, K.I, K.S
    Ls = K.cfg['Ls']
    NT = K.NT
    with ExitStack() as ph:
        a_t = K.sb(ph, [128, Ls])
        g_t = K.sb(ph, [128, Ls])
        zp = K.sb(ph, [128, Ls + 30])
        ac = K.sb(ph, [128, Ls])
        cw = K.sb(ph, [128, 8, 31])
        cb_ = K.sb(ph, [128, 8])
        stg = [K.sb(ph, [128, 4, 128]) for _ in range(2)]
        pst = [K.pb(ph) for _ in range(2)]
        T.dma('s', cw[:], I['conv_wT'][l].rearrange("(cb p) j -> p cb j", p=128), 'c0')
        T.dma('s', cb_[:], I['conv_bT'][l], 'c0')
        T.barrier()
        nt = 0
        for sq_ in K.seqs:
            r0, L = sq_['r0'], sq_['L']
            for cb in range(8):
                for c0_ in range(0, L, 2048):
                    c1_ = min(L, c0_ + 2048)
                    T.dma('s', a_t[:, c0_:c1_], S['PT'][cb * 128:(cb + 1) * 128, r0 + c0_:r0 + c1_], 'ca', W=['a_t'])
                    T.dma('s', g_t[:, c0_:c1_], S['PT'][1024 + cb * 128:1024 + (cb + 1) * 128, r0 + c0_:r0 + c1_], 'cg', W=['g_t'])
                T.op('a', lambda: nc.scalar.activation(out=g_t[:, :L], in_=g_t[:, :L], func=AF.Sigmoid), R=['g_t'], W=['g_t'])
                T.op('g', lambda: nc.gpsimd.memset(zp[:, 0:15], 0.0), W=['zp'])
                T.op('g', lambda: nc.gpsimd.memset(zp[:, 15 + L:30 + L], 0.0), W=['zp'])
                T.op('v', lambda: nc.vector.tensor_tensor(out=zp[:, 15:15 + L], in0=a_t[:, :L], in1=g_t[:, :L], op=ALU.mult),
                     R=['a_t', 'g_t'], W=['zp'])
                T.op('v', lambda: nc.vector.tensor_scalar(out=ac[:, :L], in0=zp[:, 0:L], scalar1=cw[:, cb, 0:1], scalar2=cb_[:, cb:cb + 1],
                                                          op0=ALU.mult, op1=ALU.add), R=['zp'], W=['ac'])
                for j in range(1, 31):
                    T.op('v', lambda: nc.vector.scalar_tensor_tensor(out=ac[:, :L], in0=zp[:, j:j + L], scalar=cw[:, cb, j:j + 1],
                                                                     in1=ac[:, :L], op0=ALU.mult, op1=ALU.add), R=['zp', 'ac'], W=['ac'])
                for t4 in range(L // 512):
                    pi = nt % 2
                    nt += 1
                    for j in range(4):
                        tt = t4 * 4 + j
                        T.op('p', lambda: nc.tensor.transpose(out=pst[pi][:, j * 128:(j + 1) * 128], in_=ac[:, tt * 128:(tt + 1) * 128],
                                                              identity=K.ident[:]), R=['ac'], W=[('pst', pi)])
                    T.op('a', lambda: nc.scalar.copy(out=stg[pi][:, :, :], in_=pst[pi][:, :].rearrange("p (j c) -> p j c", j=4)),
                         R=[('pst', pi)], W=[('stg', pi)])
                    rows = r0 + t4 * 512
                    T.dma('g', S['ZC'][rows:rows + 512, cb * 128:(cb + 1) * 128].rearrange("(j p) c -> p j c", p=128), stg[pi][:, :, :],
                          'cs%d' % pi, R=[('stg', pi)])
                if L % 512 != 0:
                    t0 = (L // 512) * 512
                    pi = nt % 2
                    nt += 1
                    nrem = (L - t0) // 128
                    for j in range(nrem):
                        tt = t0 // 128 + j
                        T.op('p', lambda: nc.tensor.transpose(out=pst[pi][:, j * 128:(j + 1) * 128], in_=ac[:, tt * 128:(tt + 1) * 128],
                                                              identity=K.ident[:]), R=['ac'], W=[('pst', pi)])
                    T.op('a', lambda: nc.scalar.copy(out=stg[pi][:, 0:nrem, :], in_=pst[pi][:, 0:nrem * 128].rearrange("p (j c) -> p j c", j=nrem)),
                         R=[('pst', pi)], W=[('stg', pi)])
                    rows = r0 + t0
                    T.dma('g', S['ZC'][rows:rows + nrem * 128, cb * 128:(cb + 1) * 128].rearrange("(j p) c -> p j c", p=128),
                          stg[pi][:, 0:nrem, :], 'cs%d' % pi, R=[('stg', pi)])
        T.barrier()
    with ExitStack() as ph:
        z = [K.sb(ph, [128, 1024]) for _ in range(2)]
        gb = [K.sb(ph, [128, 1024]) for _ in range(2)]
        lw = K.sb(ph, [128, 1024])
        lb = K.sb(ph, [128, 1024])
        bs = [K.sb(ph, [128, 12]) for _ in range(2)]
        mv = [K.sb(ph, [128, 4]) for _ in range(2)]
        T.dma('s', lw[:], I['conv_ln_w'][l:l + 1, :].to_broadcast([128, 1024]), 'c0')
        T.dma('s', lb[:], I['conv_ln_b'][l:l + 1, :].to_broadcast([128, 1024]), 'c0')
        T.barrier()
        for tt in range(NT // 128):
            s = tt % 2
            rows = tt * 128
            T.dma('s', z[s][:], S['ZC'][rows:rows + 128, :], 'z%d' % s, W=[('z', s)])
            T.dma('s', gb[s][:], S['P'][rows:rows + 128, 4096:5120], 'gb%d' % s, W=[('gb', s)])
            T.op('a', lambda: nc.scalar.activation(out=gb[s][:], in_=gb[s][:], func=AF.Silu), R=[('gb', s)], W=[('gb', s)])
            T.op('v', lambda: nc.vector.bn_stats(out=bs[s][:, 0:6], in_=z[s][:, 0:512]), R=[('z', s)], W=[('bs', s)])
            T.op('v', lambda: nc.vector.bn_stats(out=bs[s][:, 6:12], in_=z[s][:, 512:1024]), R=[('z', s)], W=[('bs', s)])
            T.op('v', lambda: nc.vector.bn_aggr(out=mv[s][:, 0:2], in_=bs[s][:, :]), R=[('bs', s)], W=[('mv', s)])
            T.op('a', lambda: nc.scalar.activation(out=mv[s][:, 2:3], in_=mv[s][:, 1:2], func=AF.Sqrt, bias=LN_EPS, scale=1.0),
                 R=[('mv', s)], W=[('mv', s)])
            T.op('v', lambda: nc.vector.reciprocal(out=mv[s][:, 3:4], in_=mv[s][:, 2:3]), R=[('mv', s)], W=[('mv', s)])
            T.op('v', lambda: nc.vector.tensor_scalar(out=z[s][:], in0=z[s][:], scalar1=mv[s][:, 0:1], scalar2=mv[s][:, 3:4],
                                                      op0=ALU.subtract, op1=ALU.mult), R=[('z', s), ('mv', s)], W=[('z', s)])
            T.op('v', lambda: nc.vector.tensor_tensor(out=z[s][:], in0=z[s][:], in1=lw[:], op=ALU.mult), R=[('z', s)], W=[('z', s)])
            T.op('v', lambda: nc.vector.tensor_tensor(out=z[s][:], in0=z[s][:], in1=lb[:], op=ALU.add), R=[('z', s)], W=[('z', s)])
            T.op('a', lambda: nc.scalar.activation(out=z[s][:], in_=z[s][:], func=AF.Silu), R=[('z', s)], W=[('z', s)])
            T.op('v', lambda: nc.vector.tensor_tensor(out=z[s][:], in0=z[s][:], in1=gb[s][:], op=ALU.mult), R=[('z', s), ('gb', s)], W=[('z', s)])
            T.dma('g', S['ACTB'][rows:rows + 128, :], z[s][:], 'zo%d' % s, R=[('z', s)])
        T.barrier()


def phase_D(K, l):
    nc, T, I, S = K.nc, K.T, K.I, K.S
    NT = K.NT
    with ExitStack() as ph:
        sb = lambda shape, dt=F32: K.sb(ph, shape, dt)
        kkb = sb([128, 1024]); kab = sb([128, 1024]); rkb = sb([128, 1024])
        w0b = sb([128, 1024]); a0b = sb([128, 1024])
        wup = sb([64, 1024]); aup = sb([64, 1024])
        mX = sb([128, 384]); mY = sb([128, 256]); tri = sb([128, 128]); nc0 = sb([128, 1])
        rkv = sb([128, 3072])
        xwT = sb([64, 128]); xaT = sb([64, 128])
        wsig = sb([128, 1024]); a_ = sb([128, 1024])
        kk = sb([128, 1024]); kd = sb([128, 1024]); b_ = sb([128, 1024])
        epos = sb([128, 1024]); eneg = sb([128, 1024]); eprv = sb([128, 1024])
        t1 = eprv; t2 = epos
        s16 = sb([128, 64])
        gC = sb([128, 8])
        bon = sb([128, 16])
        ZALL = sb([128, 16, 128], BF16 if K.cfg.get('scan_bf16', False) else F32)
        FT = [sb([128, 4, 128]) for _ in range(8)]
        SDT = BF16 if K.cfg.get('scan_bf16', False) else F32
        MXt = [sb([128, 384]) for _ in range(8)]
        MYt = [sb([128, 256]) for _ in range(8)]
        LV = [[sb([128, 384], SDT) for _ in range(2)] for _ in range(8)]
        AN0 = [sb([128, 256], SDT) for _ in range(8)]
        XW = [sb([128, 128]) for _ in range(8)]
        U0 = sb([128, 16, 64])
        WT = [sb([128, 128]) for _ in range(8)]
        UN = sb([128, 16, 64])
        STX = [sb([128, 128]) for _ in range(8)]
        ysb = sb([128, 1024])
        psb = [K.pb(ph) for _ in range(8)]
        r_ = rkv[:, 0:1024]
        k_ = rkv[:, 1024:2048]
        v_ = rkv[:, 2048:3072]

        def bc(src_row):
            return src_row.to_broadcast([128, 1024])

        T.dma('s', kkb[:], bc(I['rwkv_k_k'][l:l + 1, :]), 'c0')
        T.dma('s', kab[:], bc(I['rwkv_k_a'][l:l + 1, :]), 'c0')
        T.dma('s', rkb[:], bc(I['rwkv_r_k'][l:l + 1, :]), 'c0')
        T.op('v', lambda: nc.vector.memset(nc0[:], -C0))
        T.barrier()
        vv = lambda ap: ap.rearrange("p (h j) -> p h j", j=64)

        for d in range(2):
            T.dma('s', w0b[:], bc(I['rwkv_w0'][l, d:d + 1, :]), 'c0')
            T.dma('s', a0b[:], bc(I['rwkv_a0'][l, d:d + 1, :]), 'c0')
            T.dma('s', wup[:], I['rwkv_w_up'][l, d], 'c0')
            T.dma('s', aup[:], I['rwkv_a_up'][l, d], 'c0')
            T.dma('s', mX[:], I['maskX'][d], 'c0')
            T.dma('s', mY[:], I['maskY'][d], 'c0')
            T.dma('s', tri[:], I['tri'][d], 'c0')
            T.barrier()
            YC = S['YC0'] if d == 0 else S['YC1']
            for sq_ in K.seqs:
                r0, L, pi = sq_['r0'], sq_['L'], sq_['pi']
                nch = L // 128
                for p in range(8):
                    T.op('v', lambda: nc.vector.memset(STX[p][:], 0.0), W=[('STX', p)])
                if pi is None:
                    for p in range(8):
                        T.dma('s', STX[p][0:64, 0:64], I['st0T'][l, d, 2 * p], 'st0', W=[('STX', p)])
                        T.dma('s', STX[p][64:128, 64:128], I['st0T'][l, d, 2 * p + 1], 'st0', W=[('STX', p)])
                    T.barrier()
                order = range(nch) if d == 0 else range(nch - 1, -1, -1)
                for c in order:
                    rows = r0 + c * 128
                    for j3 in range(3):
                        T.dma('s', rkv[:, j3 * 1024:(j3 + 1) * 1024], S['P'][rows:rows + 128, 5120 + j3 * 1024:5120 + (j3 + 1) * 1024], 'rkv', W=['rkv'])
                    T.dma('s', xwT[:], S['PT'][2048 + d * 64:2048 + (d + 1) * 64, rows:rows + 128], 'xw', W=['xwT'])
                    T.dma('s', xaT[:], S['PT'][2176 + d * 64:2176 + (d + 1) * 64, rows:rows + 128], 'xa', W=['xaT'])
                    T.op('a', lambda: nc.scalar.activation(out=xwT[:], in_=xwT[:], func=AF.Tanh), R=['xwT'], W=['xwT'])
                    for hf in range(2):
                        T.op('p', lambda: nc.tensor.matmul(psb[0 + hf][:, :], lhsT=xwT[:, :], rhs=wup[:, hf * 512:(hf + 1) * 512],
                                                           start=True, stop=True), R=['xwT'], W=[('ps', hf)])
                        T.op('p', lambda: nc.tensor.matmul(psb[2 + hf][:, :], lhsT=xaT[:, :], rhs=aup[:, hf * 512:(hf + 1) * 512],
                                                           start=True, stop=True), R=['xaT'], W=[('ps', 2 + hf)])
                    for hf in range(2):
                        cs_ = slice(hf * 512, (hf + 1) * 512)
                        T.op('v', lambda: nc.vector.tensor_tensor(out=wsig[:, cs_], in0=psb[hf][:, :], in1=w0b[:, cs_], op=ALU.add),
                             W=[('ps', hf), 'wsig'])
                        T.op('v', lambda: nc.vector.tensor_tensor(out=a_[:, cs_], in0=psb[2 + hf][:, :], in1=a0b[:, cs_], op=ALU.add),
                             W=[('ps', 2 + hf), 'a'])
                    T.op('a', lambda: nc.scalar.activation(out=wsig[:], in_=wsig[:], func=AF.Sigmoid), R=['wsig'], W=['wsig'])
                    T.op('a', lambda: nc.scalar.activation(out=a_[:], in_=a_[:], func=AF.Sigmoid), R=['a'], W=['a'])
                    T.op('v', lambda: nc.vector.tensor_tensor(out=kk[:], in0=k_, in1=kkb[:], op=ALU.mult), R=['rkv'], W=['kk'])
                    T.op('g', lambda: nc.gpsimd.tensor_tensor(out=t1[:], in0=kk[:], in1=kk[:], op=ALU.mult), R=['kk'], W=['eprv'])
                    T.op('v', lambda: nc.vector.tensor_reduce(out=s16[:, 0:16], in_=vv(t1[:, :]), axis=AX.X, op=ALU.add), R=['eprv'], W=['s16'])
                    T.op('a', lambda: nc.scalar.activation(out=s16[:, 16:32], in_=s16[:, 0:16], func=AF.Sqrt), R=['s16'], W=['s16'])
                    T.op('v', lambda: nc.vector.tensor_scalar(out=s16[:, 16:32], in0=s16[:, 16:32], scalar1=1e-12, scalar2=None, op0=ALU.max),
                         R=['s16'], W=['s16'])
                    T.op('v', lambda: nc.vector.reciprocal(out=s16[:, 32:48], in_=s16[:, 16:32]), R=['s16'], W=['s16'])
                    T.op('v', lambda: nc.vector.tensor_tensor(out=vv(kk[:, :]), in0=vv(kk[:, :]),
                                                              in1=s16[:, 32:48, None].to_broadcast([128, 16, 64]), op=ALU.mult),
                         R=['kk', 's16'], W=['kk'])
                    T.op('v', lambda: nc.vector.scalar_tensor_tensor(out=t1[:], in0=a_[:], scalar=-1.0, in1=kab[:], op0=ALU.add, op1=ALU.mult),
                         R=['a', 'eprv'], W=['eprv'])
                    T.op('v', lambda: nc.vector.scalar_tensor_tensor(out=kd[:], in0=t1[:], scalar=1.0, in1=k_, op0=ALU.add, op1=ALU.mult),
                         R=['eprv', 'rkv'], W=['kd'])
                    T.op('g', lambda: nc.gpsimd.tensor_tensor(out=b_[:], in0=kk[:], in1=a_[:], op=ALU.mult), R=['kk', 'a'], W=['b'])
                    T.op('g', lambda: nc.gpsimd.tensor_tensor(out=t2[:], in0=r_, in1=rkb[:], op=ALU.mult), R=['rkv'], W=['epos'])
                    T.op('g', lambda: nc.gpsimd.tensor_tensor(out=t2[:], in0=t2[:], in1=kd[:], op=ALU.mult), R=['epos', 'kd'], W=['epos'])
                    T.op('v', lambda: nc.vector.tensor_reduce(out=bon[:], in_=vv(t2[:, :]), axis=AX.X, op=ALU.add), R=['epos'], W=['bon'])
                    T.dma('g', S['BON'][d, rows:rows + 128, :], bon[:], 'bon', R=['bon'])
                    for hf in range(2):
                        T.op('p', lambda: nc.tensor.matmul(psb[4 + hf][:, :], lhsT=tri[:, :], rhs=wsig[:, hf * 512:(hf + 1) * 512],
                                                           start=True, stop=True), R=['wsig'], W=[('ps', 4 + hf)])
                    for p in range(8):
                        T.op('p', lambda: nc.tensor.matmul(psb[6][:, p:p + 1], lhsT=wsig[:, p * 128:(p + 1) * 128], rhs=nc0[:, 0:1],
                                                           start=True, stop=True), R=['wsig'], W=[('ps', 6)])
                    T.op('a', lambda: nc.scalar.activation(out=gC[:], in_=psb[6][:, 0:8], func=AF.Exp), W=[('ps', 6), 'gC'])
                    for hf in range(2):
                        cs_ = slice(hf * 512, (hf + 1) * 512)
                        T.op('a', lambda: nc.scalar.activation(out=epos[:, cs_], in_=psb[4 + hf][:, :], func=AF.Exp), W=[('ps', 4 + hf), 'epos'])
                        T.op('a', lambda: nc.scalar.activation(out=eneg[:, cs_], in_=psb[4 + hf][:, :], func=AF.Exp, scale=-1.0),
                             W=[('ps', 4 + hf), 'eneg'])
                        T.op('v', lambda: nc.vector.scalar_tensor_tensor(out=eprv[:, cs_], in0=wsig[:, cs_], scalar=C0, in1=psb[4 + hf][:, :],
                                                                         op0=ALU.mult, op1=ALU.add), R=['wsig'], W=[('ps', 4 + hf), 'eprv'])
                    T.op('a', lambda: nc.scalar.activation(out=eprv[:], in_=eprv[:], func=AF.Exp), R=['eprv'], W=['eprv'])
                    T.op('v', lambda: nc.vector.tensor_tensor(out=epos[:], in0=epos[:], in1=r_, op=ALU.mult), R=['epos', 'rkv'], W=['epos'])
                    T.op('g', lambda: nc.gpsimd.tensor_tensor(out=eprv[:], in0=eprv[:], in1=kk[:], op=ALU.mult), R=['eprv', 'kk'], W=['eprv'])
                    T.op('v', lambda: nc.vector.tensor_tensor(out=b_[:], in0=b_[:], in1=eneg[:], op=ALU.mult), R=['b', 'eneg'], W=['b'])
                    T.op('g', lambda: nc.gpsimd.tensor_tensor(out=kd[:], in0=kd[:], in1=eneg[:], op=ALU.mult), R=['kd', 'eneg'], W=['kd'])
                    T.op('a', lambda: nc.scalar.copy(out=ZALL[:, :, 0:64], in_=vv(eprv[:, :])), R=['eprv'], W=['ZALLk'] + [('ZALL', h_) for h_ in range(16)])
                    for p in range(8):
                        pp = psb[p % 4]
                        cs_ = slice(p * 128, (p + 1) * 128)
                        for q, (src, key) in enumerate(((epos, 'epos'), (eprv, 'eprv'), (b_, 'b'), (kd, 'kd'))):
                            T.op('p', lambda: nc.tensor.transpose(out=pp[:, q * 128:(q + 1) * 128], in_=src[:, cs_], identity=K.ident[:]),
                                 R=[key], W=[('ps', p % 4)])
                        if p % 2 == 0:
                            T.op('v', lambda: nc.vector.tensor_copy(out=FT[p][:, :, :], in_=pp[:, :].rearrange("p (q t) -> p q t", q=4)),
                                 W=[('ps', p % 4), ('FT', p)])
                        else:
                            T.op('a', lambda: nc.scalar.copy(out=FT[p][:, :, :], in_=pp[:, :].rearrange("p (q t) -> p q t", q=4)),
                                 W=[('ps', p % 4), ('FT', p)])
                    for q2 in range(2):
                        grp = [(2 * q2, 0), (2 * q2 + 1, 4)]
                        for hg, bo in grp:
                            hs = [4 * hg + i for i in range(4)]
                            for i, h in enumerate(hs):
                                p = h // 2
                                sl = slice((h % 2) * 64, (h % 2) * 64 + 64)
                                rT_kpT = FT[p][sl, 0:2, :].rearrange("p a t -> p (a t)")
                                T.op('p', lambda: nc.tensor.matmul(psb[bo + i][:, 0:128], lhsT=FT[p][sl, 1, :], rhs=FT[p][sl, 2, :], start=True, stop=True),
                                     R=[('FT', p)], W=[('ps', bo + i)])
                                T.op('p', lambda: nc.tensor.matmul(psb[bo + i][:, 128:384], lhsT=FT[p][sl, 2, :], rhs=rT_kpT, start=True, stop=True),
                                     R=[('FT', p)], W=[('ps', bo + i)])
                            for i, h in enumerate(hs):
                                T.op('v', lambda: nc.vector.tensor_tensor(out=MXt[bo + i][:], in0=psb[bo + i][:, 0:384], in1=mX[:], op=ALU.mult),
                                     W=[('ps', bo + i), ('MX', bo + i)])
                                if SDT != F32:
                                    T.op('g', lambda: nc.gpsimd.tensor_copy(out=AN0[bo + i][:, :].rearrange("p (a c) -> p a c", a=2),
                                                                            in_=MXt[bo + i][:, :].rearrange("p (a c) -> p a c", a=3)[:, 0:3:2, :]),
                                         R=[('MX', bo + i)], W=[('AN0', bo + i)])
                            for i, h in enumerate(hs):
                                p = h // 2
                                sl = slice((h % 2) * 64, (h % 2) * 64 + 64)
                                rT_kpT = FT[p][sl, 0:2, :].rearrange("p a t -> p (a t)")
                                T.op('p', lambda: nc.tensor.matmul(psb[bo + i][:, 0:256], lhsT=FT[p][sl, 3, :], rhs=rT_kpT, start=True, stop=True),
                                     R=[('FT', p)], W=[('ps', bo + i)])
                            for i, h in enumerate(hs):
                                T.op('v', lambda: nc.vector.tensor_tensor(out=MYt[bo + i][:], in0=psb[bo + i][:, 0:256], in1=mY[:], op=ALU.mult),
                                     W=[('ps', bo + i), ('MY', bo + i)])
                            for i, h in enumerate(hs):
                                T.op('p', lambda: nc.tensor.matmul(psb[bo + i][:, 0:64], lhsT=MYt[bo + i][:, 128:256], rhs=v_[:, h * 64:(h + 1) * 64],
                                                                   start=True, stop=True), R=[('MY', bo + i), 'rkv'], W=[('ps', bo + i)])
                            for i, h in enumerate(hs):
                                T.op('a', lambda: nc.scalar.copy(out=ZALL[:, h, 64:128], in_=psb[bo + i][:, 0:64]), W=[('ps', bo + i), ('ZALL', h)])
                        v3 = lambda ap: ap.rearrange("p (a c) -> p a c", a=3)
                        for lv in range(7):
                            for hg, bo in grp:
                                hs = [4 * hg + i for i in range(4)]
                                for i, h in enumerate(hs):
                                    pb_ = psb[bo + i]
                                    if lv == 0:
                                        if SDT != F32:
                                            A_ap, N_ap, Akey = AN0[bo + i][:, 0:128], AN0[bo + i][:, 128:256], ('AN0', bo + i)
                                        else:
                                            A_ap, N_ap, Akey = MXt[bo + i][:, 0:128], MXt[bo + i][:, 256:384], ('MX', bo + i)
                                        T.op('p', lambda: nc.tensor.matmul(pb_[:, 128:256], lhsT=N_ap, rhs=ZALL[:, h, :], start=True, stop=True),
                                             R=[Akey, ('ZALL', h), 'ZALLk'], W=[('ps', bo + i)])
                                        T.op('p', lambda: nc.tensor.matmul(pb_[:, 0:128], lhsT=N_ap, rhs=A_ap, start=True, stop=True),
                                             R=[Akey], W=[('ps', bo + i)])
                                        T.op('p', lambda: nc.tensor.matmul(pb_[:, 256:384], lhsT=A_ap, rhs=N_ap, start=True, stop=True),
                                             R=[Akey], W=[('ps', bo + i)])
                                    else:
                                        lvt = LV[bo + i][(lv - 1) % 2]
                                        Lkey = ('LV', bo + i, (lv - 1) % 2)
                                        if lv < 6:
                                            T.op('p', lambda: nc.tensor.matmul(pb_[:, 0:256], lhsT=lvt[:, 256:384], rhs=lvt[:, 0:256], start=True, stop=True),
                                                 R=[Lkey], W=[('ps', bo + i)])
                                            T.op('p', lambda: nc.tensor.matmul(pb_[:, 256:384], lhsT=lvt[:, 0:128], rhs=lvt[:, 256:384], start=True, stop=True),
                                                 R=[Lkey], W=[('ps', bo + i)])
                                        else:
                                            T.op('p', lambda: nc.tensor.matmul(pb_[:, 128:256], lhsT=lvt[:, 256:384], rhs=lvt[:, 128:256], start=True, stop=True),
                                                 R=[Lkey], W=[('ps', bo + i)])
                            for hg, bo in grp:
                                hs = [4 * hg + i for i in range(4)]
                                for i, h in enumerate(hs):
                                    p = h // 2
                                    pb_ = psb[bo + i]
                                    if lv == 0:
                                        Xin, Xkey = ZALL[:, h, :], ('ZALL', h)
                                    else:
                                        Xin, Xkey = LV[bo + i][(lv - 1) % 2][:, 128:256], ('LV', bo + i, (lv - 1) % 2)
                                    opx = ALU.subtract if lv == 0 else ALU.add
                                    if lv < 6:
                                        lvo = LV[bo + i][lv % 2]
                                        T.op('v', lambda: nc.vector.tensor_tensor(out=lvo[:, 128:256], in0=Xin, in1=pb_[:, 128:256], op=opx),
                                             R=[Xkey, 'ZALLk'], W=[('ps', bo + i), ('LV', bo + i, lv % 2)])
                                        T.op('v', lambda: nc.vector.tensor_copy(out=v3(lvo[:, :])[:, 0:3:2, :], in_=v3(pb_[:, 0:384])[:, 0:3:2, :]),
                                             W=[('ps', bo + i), ('LV', bo + i, lv % 2)])
                                    else:
                                        T.op('v', lambda: nc.vector.tensor_tensor(out=XW[p][:, (h % 2) * 64:(h % 2) * 64 + 64], in0=Xin[:, 0:64],
                                                                                  in1=pb_[:, 128:192], op=opx),
                                             R=[Xkey], W=[('ps', bo + i), ('XW', p)])
                                        T.op('v', lambda: nc.vector.tensor_tensor(out=U0[:, h, :], in0=Xin[:, 64:128], in1=pb_[:, 192:256], op=opx),
                                             R=[Xkey], W=[('ps', bo + i), ('U0', h)])
                        for hg, bo in grp:
                            hs = [4 * hg + i for i in range(4)]
                            for pi2 in range(2):
                                p = hg * 2 + pi2
                                yb = 4 + (p % 2)
                                T.op('p', lambda: nc.tensor.transpose(out=psb[6][:, 0:128], in_=XW[p][:, :], identity=K.ident[:]),
                                     R=[('XW', p)], W=[('ps', 6)])
                                T.op('a', lambda: nc.scalar.copy(out=WT[p][:, :], in_=psb[6][:, 0:128]), W=[('ps', 6), ('WT', p)])
                                for j2 in range(2):
                                    h = 2 * p + j2
                                    i = pi2 * 2 + j2
                                    sl = slice(j2 * 64, j2 * 64 + 64)
                                    T.op('p', lambda: nc.tensor.matmul(psb[bo + i][:, 0:64], lhsT=WT[p][sl, :], rhs=STX[p][sl, sl], start=True, stop=True),
                                         R=[('WT', p), ('STX', p)], W=[('ps', bo + i)])
                                for j2 in range(2):
                                    h = 2 * p + j2
                                    i = pi2 * 2 + j2
                                    T.op('v', lambda: nc.vector.scalar_tensor_tensor(out=UN[:, h, :], in0=psb[bo + i][:, 0:64], scalar=-1.0, in1=U0[:, h, :],
                                                                                     op0=ALU.mult, op1=ALU.subtract),
                                         R=[('U0', h)], W=[('ps', bo + i), ('UN', h)])
                                for j2 in range(2):
                                    h = 2 * p + j2
                                    i = pi2 * 2 + j2
                                    sl = slice(j2 * 64, j2 * 64 + 64)
                                    yo = psb[yb][:, j2 * 64:(j2 + 1) * 64]
                                    T.op('p', lambda: nc.tensor.matmul(yo, lhsT=FT[p][sl, 0, :], rhs=STX[p][sl, sl], start=True, stop=False),
                                         R=[('FT', p), ('STX', p)], W=[('ps', yb)])
                                    T.op('p', lambda: nc.tensor.matmul(yo, lhsT=MYt[bo + i][:, 0:128], rhs=v_[:, h * 64:(h + 1) * 64], start=False, stop=False),
                                         R=[('MY', bo + i), 'rkv'], W=[('ps', yb)])
                                    T.op('p', lambda: nc.tensor.matmul(yo, lhsT=MXt[bo + i][:, 128:256], rhs=UN[:, h, :], start=False, stop=True),
                                         R=[('MX', bo + i), ('UN', h)], W=[('ps', yb)])
                                T.op('a', lambda: nc.scalar.copy(out=ysb[:, p * 128:(p + 1) * 128], in_=psb[yb][:, 0:128]),
                                     W=[('ps', yb), ('ysb', p)])
                                cs_ = slice(p * 128, (p + 1) * 128)
                                pS = psb[7]
                                T.op('p', lambda: nc.tensor.matmul(pS[:, 0:128], lhsT=kd[:, cs_], rhs=v_[:, cs_], start=True, stop=False),
                                     R=['kd', 'rkv'], W=[('ps', 7)])
                                T.op('p', lambda: nc.tensor.matmul(pS[:, 0:128], lhsT=b_[:, cs_], rhs=UN[:, 2 * p:2 * p + 2, :].rearrange("p a j -> p (a j)"),
                                                                   start=False, stop=False), R=['b', ('UN', 2 * p), ('UN', 2 * p + 1)], W=[('ps', 7)])
                                T.op('p', lambda: nc.tensor.matmul(pS[:, 0:128], lhsT=K.ident[:, :], rhs=STX[p][:, :], start=False, stop=True),
                                     R=[('STX', p)], W=[('ps', 7)])
                                T.op('v', lambda: nc.vector.tensor_scalar(out=STX[p][0:64, 0:64], in0=pS[0:64, 0:64], scalar1=gC[0:64, p:p + 1],
                                                                          scalar2=None, op0=ALU.mult), R=['gC'], W=[('ps', 7), ('STX', p)])
                                T.op('v', lambda: nc.vector.tensor_scalar(out=STX[p][64:128, 64:128], in0=pS[64:128, 64:128], scalar1=gC[64:128, p:p + 1],
                                                                          scalar2=None, op0=ALU.mult), R=['gC'], W=[('ps', 7), ('STX', p)])
                    T.dma('g', YC[rows:rows + 128, :], ysb[:], 'ysb', R=[('ysb', pp_) for pp_ in range(8)])
                if pi is not None:
                    for p in range(8):
                        T.dma('g', K.O['out_st'][pi, l, d, 2 * p], STX[p][0:64, 0:64], 'sto', R=[('STX', p)])
                        T.dma('g', K.O['out_st'][pi, l, d, 2 * p + 1], STX[p][64:128, 64:128], 'sto', R=[('STX', p)])
                    T.barrier()
            T.barrier()
    with ExitStack() as ph:
        y0 = [K.sb(ph, [128, 1024]) for _ in range(2)]
        y1 = [K.sb(ph, [128, 1024]) for _ in range(2)]
        vg = [K.sb(ph, [128, 2048]) for _ in range(2)]
        bo = [K.sb(ph, [128, 32]) for _ in range(2)]
        ysq = K.sb(ph, [128, 1024])
        gw = K.sb(ph, [128, 1024]); gbb = K.sb(ph, [128, 1024])
        sst = [K.sb(ph, [128, 96]) for _ in range(2)]
        T.dma('s', gw[:], I['rwkv_gn_w'][l:l + 1, :].to_broadcast([128, 1024]), 'c0')
        T.dma('s', gbb[:], I['rwkv_gn_b'][l:l + 1, :].to_broadcast([128, 1024]), 'c0')
        T.barrier()
        vv = lambda ap: ap.rearrange("p (h j) -> p h j", j=64)
        bc16 = lambda ap: ap[:, :, None].to_broadcast([128, 16, 64])
        for tt in range(NT // 128):
            s = tt % 2
            rows = tt * 128
            T.dma('s', y0[s][:], S['YC0'][rows:rows + 128, :], 'y0%d' % s, W=[('y0', s)])
            T.dma('s', y1[s][:], S['YC1'][rows:rows + 128, :], 'y1%d' % s, W=[('y1', s)])
            T.dma('s', vg[s][:], S['P'][rows:rows + 128, 7168:9216], 'vg%d' % s, W=[('vg', s)])
            T.dma('s', bo[s][:, 0:16], S['BON'][0, rows:rows + 128, :], 'bo%d' % s, W=[('bo', s)])
            T.dma('s', bo[s][:, 16:32], S['BON'][1, rows:rows + 128, :], 'bo%d' % s, W=[('bo', s)])
            y = y0[s]
            ss = sst[s]
            T.op('v', lambda: nc.vector.tensor_tensor(out=y[:], in0=y[:], in1=y1[s][:], op=ALU.add), R=[('y0', s), ('y1', s)], W=[('y0', s)])
            T.op('g', lambda: nc.gpsimd.tensor_tensor(out=ysq[:], in0=y[:], in1=y[:], op=ALU.mult), R=[('y0', s)], W=['ysq'])
            T.op('v', lambda: nc.vector.tensor_reduce(out=ss[:, 0:16], in_=vv(y[:, :]), axis=AX.X, op=ALU.add), R=[('y0', s)], W=[('ss', s)])
            T.op('v', lambda: nc.vector.tensor_reduce(out=ss[:, 16:32], in_=vv(ysq[:, :]), axis=AX.X, op=ALU.add), R=['ysq'], W=[('ss', s)])
            T.op('v', lambda: nc.vector.tensor_scalar(out=ss[:, 32:48], in0=ss[:, 0:16], scalar1=1.0 / 64, scalar2=None, op0=ALU.mult), R=[('ss', s)], W=[('ss', s)])
            T.op('v', lambda: nc.vector.tensor_tensor(out=ss[:, 48:64], in0=ss[:, 32:48], in1=ss[:, 32:48], op=ALU.mult), R=[('ss', s)], W=[('ss', s)])
            T.op('v', lambda: nc.vector.scalar_tensor_tensor(out=ss[:, 64:80], in0=ss[:, 16:32], scalar=1.0 / 64, in1=ss[:, 48:64],
                                                             op0=ALU.mult, op1=ALU.subtract), R=[('ss', s)], W=[('ss', s)])
            T.op('a', lambda: nc.scalar.activation(out=ss[:, 64:80], in_=ss[:, 64:80], func=AF.Sqrt, bias=GN_EPS, scale=1.0), R=[('ss', s)], W=[('ss', s)])
            T.op('v', lambda: nc.vector.reciprocal(out=ss[:, 80:96], in_=ss[:, 64:80]), R=[('ss', s)], W=[('ss', s)])
            T.op('v', lambda: nc.vector.tensor_tensor(out=vv(y[:, :]), in0=vv(y[:, :]), in1=bc16(ss[:, 32:48]), op=ALU.subtract),
                 R=[('y0', s), ('ss', s)], W=[('y0', s)])
            T.op('v', lambda: nc.vector.tensor_tensor(out=vv(y[:, :]), in0=vv(y[:, :]), in1=bc16(ss[:, 80:96]), op=ALU.mult),
                 R=[('y0', s), ('ss', s)], W=[('y0', s)])
            T.op('g', lambda: nc.gpsimd.tensor_tensor(out=y[:], in0=y[:], in1=gw[:], op=ALU.mult), R=[('y0', s)], W=[('y0', s)])
            T.op('g', lambda: nc.gpsimd.tensor_tensor(out=y[:], in0=y[:], in1=gbb[:], op=ALU.add), R=[('y0', s)], W=[('y0', s)])
            T.op('v', lambda: nc.vector.tensor_tensor(out=bo[s][:, 0:16], in0=bo[s][:, 0:16], in1=bo[s][:, 16:32], op=ALU.add), R=[('bo', s)], W=[('bo', s)])
            T.op('v', lambda: nc.vector.tensor_tensor(out=vv(ysq[:, :]), in0=vv(vg[s][:, 0:1024]), in1=bc16(bo[s][:, 0:16]), op=ALU.mult),
                 R=[('vg', s), ('bo', s)], W=['ysq'])
            T.op('v', lambda: nc.vector.tensor_tensor(out=y[:], in0=y[:], in1=ysq[:], op=ALU.add), R=[('y0', s), 'ysq'], W=[('y0', s)])
            T.op('a', lambda: nc.scalar.activation(out=vg[s][:, 1024:2048], in_=vg[s][:, 1024:2048], func=AF.Silu), R=[('vg', s)], W=[('vg', s)])
            T.op('v', lambda: nc.vector.tensor_tensor(out=y[:], in0=y[:], in1=vg[s][:, 1024:2048], op=ALU.mult), R=[('y0', s), ('vg', s)], W=[('y0', s)])
            T.dma('g', S['ACTC'][rows:rows + 128, :], y[:], 'yo%d' % s, R=[('y0', s)])
        T.barrier()


def phase_E(K, l):
    nc, T, I, S = K.nc, K.T, K.I, K.S
    NT, Ls = K.NT, K.cfg['Ls']
    TB = 256
    last = (l == DEPTH - 1)
    with ExitStack() as ph:
        ain = [K.sb(ph, [128, 1024]) for _ in range(2)]
        aT = [K.sb(ph, [128, 8, TB], BF16) for _ in range(3)]
        mT = K.sb(ph, [128, KT, TB], BF16)
        wbr = [K.sb(ph, [128, 8, 512], BF16) for _ in range(3)]
        wo = [K.sb(ph, [128, KT, 512], BF16) for _ in range(2)]
        gt = [K.sb(ph, [128, TB]) for _ in range(3)]
        mm = K.sb(ph, [128, TB])
        mt = K.sb(ph, [128, TB])
        xt = [K.sb(ph, [128, D]) for _ in range(2)]
        tmpo = K.sb(ph, [128, 512])
        sq = K.sb(ph, [128, D])
        st = K.sb(ph, [128, 4])
        fnw = K.sb(ph, [128, D])
        pst = [K.pb(ph) for _ in range(2)]
        psy = [K.pb(ph) for _ in range(3)]
        pso = [K.pb(ph) for _ in range(2)]
        T.dma('s', fnw[:], I['final_norm_w'][0:1, :].to_broadcast([128, D]), 'c0')
        GATE = [K.sb(ph, [128, D]) for _ in range(2)]
        for g in range(2):
            T.dma('s', GATE[g][:], S['ADAd'][g][:, 2 * D:3 * D], 'c0')
        T.barrier()
        acts = [S['ACTA'], S['ACTB'], S['ACTC']]
        wsrc = [S['Wa'][l].rearrange("(kt p) c -> p kt c", p=128), S['Wbb'][l].rearrange("(kt p) c -> p kt c", p=128),
                S['Wc'][l].rearrange("(kt p) c -> p kt c", p=128)]
        wov = S['Wo'][l].rearrange("(kt p) c -> p kt c", p=128)
        na = 0
        nt = 0
        no = 0
        for tb in range(NT // TB):
            g = 0 if tb * TB < Ls else 1
            gate_g = GATE[g][:, :]
            for br in range(3):
                for sub in range(TB // 128):
                    rows = tb * TB + sub * 128
                    s = na % 2
                    na += 1
                    T.dma('s', ain[s][:], acts[br][rows:rows + 128, :], 'ain%d' % s, W=[('ain', s)])
                    for q4 in range(2):
                        pi = nt % 2
                        nt += 1
                        for j in range(4):
                            kt = q4 * 4 + j
                            T.op('p', lambda: nc.tensor.transpose(out=pst[pi][:, j * 128:(j + 1) * 128], in_=ain[s][:, kt * 128:(kt + 1) * 128],
                                                                  identity=K.ident[:]), R=[('ain', s)], W=[('pst', pi)])
                        dst = aT[br][:, q4 * 4:q4 * 4 + 4, sub * 128:(sub + 1) * 128]
                        src = pst[pi][:, :].rearrange("p (j t) -> p j t", j=4)
                        if pi == 0:
                            T.op('a', lambda: nc.scalar.copy(out=dst, in_=src), R=[('pst', pi)], W=[('aT', br)])
                        else:
                            T.op('v', lambda: nc.vector.tensor_copy(out=dst, in_=src), R=[('pst', pi)], W=[('aT', br)])
            for cg in range(4):
                for br in range(3):
                    T.dma('s', wbr[br][:], wsrc[br][:, :, cg * 512:(cg + 1) * 512], 'wbr%d' % br, W=[('wbr', br)])
                for c4 in range(4):
                    ct = cg * 4 + c4
                    for br in range(3):
                        T.dma('s', gt[br][:], S['PT'][2304 + br * D + ct * 128:2304 + br * D + (ct + 1) * 128, tb * TB:(tb + 1) * TB],
                              'gt%d' % br, W=[('gt', br)])
                        T.op('a', lambda: nc.scalar.activation(out=gt[br][:], in_=gt[br][:], func=AF.Sigmoid), R=[('gt', br)], W=[('gt', br)])
                        for kt in range(8):
                            T.op('p', lambda: nc.tensor.matmul(psy[br][:, 0:TB], lhsT=wbr[br][:, kt, c4 * 128:(c4 + 1) * 128], rhs=aT[br][:, kt, :],
                                                               start=(kt == 0), stop=(kt == 7)), R=[('wbr', br), ('aT', br)], W=[('psy', br)])
                    T.op('v', lambda: nc.vector.tensor_tensor(out=mm[:], in0=psy[0][:, 0:TB], in1=gt[0][:], op=ALU.mult), R=[('psy', 0), ('gt', 0)], W=['mm'])
                    T.op('v', lambda: nc.vector.tensor_tensor(out=mt[:], in0=psy[1][:, 0:TB], in1=gt[1][:], op=ALU.mult), R=[('psy', 1), ('gt', 1)], W=['mt'])
                    T.op('g', lambda: nc.gpsimd.tensor_tensor(out=mm[:], in0=mm[:], in1=mt[:], op=ALU.add), R=['mm', 'mt'], W=['mm'])
                    T.op('v', lambda: nc.vector.tensor_tensor(out=mt[:], in0=psy[2][:, 0:TB], in1=gt[2][:], op=ALU.mult), R=[('psy', 2), ('gt', 2)], W=['mt'])
                    T.op('v', lambda: nc.vector.tensor_tensor(out=mT[:, ct, :], in0=mm[:], in1=mt[:], op=ALU.add), R=['mm', 'mt'], W=['mT'])
            for sub in range(TB // 128):
                rows = tb * TB + sub * 128
                T.dma('s', xt[sub][:], S['X'][rows:rows + 128, :], 'xt%d' % sub, W=[('xt', sub)])
            for cb in range(4):
                ws = no % 2
                no += 1
                T.dma('s', wo[ws][:], wov[:, :, cb * 512:(cb + 1) * 512], 'wo%d' % ws, W=[('wo', ws)])
                for sub in range(TB // 128):
                    pi = (cb * 2 + sub) % 2
                    for kt in range(KT):
                        T.op('p', lambda: nc.tensor.matmul(pso[pi][:, :], lhsT=mT[:, kt, sub * 128:(sub + 1) * 128], rhs=wo[ws][:, kt, :],
                                                           start=(kt == 0), stop=(kt == KT - 1)), R=['mT', ('wo', ws)], W=[('pso', pi)])
                    cs_ = slice(cb * 512, (cb + 1) * 512)
                    T.op('v', lambda: nc.vector.tensor_tensor(out=tmpo[:], in0=pso[pi][:, :], in1=gate_g[:, cs_], op=ALU.mult),
                         R=[('pso', pi)], W=['tmpo'])
                    T.op('v', lambda: nc.vector.tensor_tensor(out=xt[sub][:, cs_], in0=xt[sub][:, cs_], in1=tmpo[:], op=ALU.add),
                         R=['tmpo', ('xt', sub)], W=[('xt', sub)])
            for sub in range(TB // 128):
                rows = tb * TB + sub * 128
                if not last:
                    T.dma('g', S['X'][rows:rows + 128, :], xt[sub][:], 'xo%d' % sub, R=[('xt', sub)])
                else:
                    T.op('v', lambda: nc.vector.tensor_tensor(out=sq[:], in0=xt[sub][:], in1=xt[sub][:], op=ALU.mult), R=[('xt', sub)], W=['sq'])
                    T.op('v', lambda: nc.vector.tensor_reduce(out=st[:, 0:1], in_=sq[:], axis=AX.X, op=ALU.add), R=['sq'], W=['st'])
                    T.op('a', lambda: nc.scalar.activation(out=st[:, 1:2], in_=st[:, 0:1], func=AF.Sqrt, bias=EPS, scale=1.0 / D), R=['st'], W=['st'])
                    T.op('v', lambda: nc.vector.reciprocal(out=st[:, 2:3], in_=st[:, 1:2]), R=['st'], W=['st'])
                    T.op('v', lambda: nc.vector.scalar_tensor_tensor(out=xt[sub][:], in0=xt[sub][:], scalar=st[:, 2:3], in1=fnw[:],
                                                                     op0=ALU.mult, op1=ALU.mult), R=[('xt', sub), 'st'], W=[('xt', sub)])
                    T.dma('g', K.O['y_all'][rows:rows + 128, :], xt[sub][:], 'xo%d' % sub, R=[('xt', sub)])
        T.barrier()


def host_consts(cfg):
    Ls = cfg['Ls']
    GRID_W = 64
    t = np.arange(Ls)
    row = (t // GRID_W).astype(np.float32)
    col = (t % GRID_W).astype(np.float32)
    inv = (10000.0 ** (-np.arange(16, dtype=np.float32) / 16)).astype(np.float32)
    ang_r = row[:, None] * inv[None, :]
    ang_c = col[:, None] * inv[None, :]
    cos4 = np.stack([np.cos(ang_r), np.cos(ang_c), np.cos(ang_r), np.cos(ang_c)], axis=1)
    sin4 = np.stack([np.sin(ang_r), np.sin(ang_c), np.sin(ang_r), np.sin(ang_c)], axis=1)
    ropetab = np.concatenate([cos4.reshape(Ls, 64), sin4.reshape(Ls, 64)], axis=1).astype(np.float32)
    i = np.arange(128)
    P_, F_ = i[:, None], i[None, :]
    SL = (F_ < P_).astype(np.float32)
    SU = (F_ > P_).astype(np.float32)
    LI = (F_ <= P_).astype(np.float32)
    UI = (F_ >= P_).astype(np.float32)
    maskX = np.stack([np.concatenate([SL, UI, SU], 1), np.concatenate([SU, LI, SL], 1)])
    maskY = np.stack([np.concatenate([UI, SU], 1), np.concatenate([LI, SL], 1)])
    tri = np.stack([-C0 * UI, -C0 * LI]).astype(np.float32)
    return dict(ident=np.eye(128, dtype=np.float32), ropetab=ropetab, maskX=maskX.astype(np.float32),
                maskY=maskY.astype(np.float32), tri=tri)


_CACHE = {}


def run(cfg, inp):
    Ls, Lp, NP, PAST = cfg['Ls'], cfg['Lp'], cfg['NP'], cfg['PAST']
    key = (Ls, Lp, NP, PAST, cfg.get('scan_bf16', False), cfg.get('stop', 'Z'), cfg.get('nl', DEPTH), tuple(cfg.get('taps', ())), cfg.get('debug', False))
    if key not in _CACHE:
        _CACHE[key] = build(cfg)
    nc = _CACHE[key]
    f = lambda a: np.ascontiguousarray(np.asarray(a, dtype=np.float32))
    hc = host_consts(cfg)
    shared = dict(
        w_ada=f(inp['w_ada']), b_ada=f(inp['b_ada']), norm_w=f(inp['norm_w']), w_in=f(inp['w_in']),
        lam4=f(np.concatenate([inp['lambda_q1'], inp['lambda_k1'], inp['lambda_q2'], inp['lambda_k2']], axis=1)),
        subln_w=f(inp['subln_w']),
        conv_wT=f(np.transpose(inp['conv_w'], (0, 2, 1))),
        conv_bT=f(np.transpose(np.asarray(inp['conv_b']).reshape(DEPTH, 8, 128), (0, 2, 1))),
        conv_ln_w=f(inp['conv_ln_w']), conv_ln_b=f(inp['conv_ln_b']),
        rwkv_w0=f(inp['rwkv_w0']), rwkv_w_up=f(inp['rwkv_w_up']), rwkv_a0=f(inp['rwkv_a0']), rwkv_a_up=f(inp['rwkv_a_up']),
        rwkv_k_k=f(inp['rwkv_k_k']), rwkv_k_a=f(inp['rwkv_k_a']), rwkv_r_k=f(np.asarray(inp['rwkv_r_k']).reshape(DEPTH, 1024)),
        rwkv_gn_w=f(inp['rwkv_gn_w']), rwkv_gn_b=f(inp['rwkv_gn_b']),
        w_br_a=f(inp['w_br_a']), w_br_b=f(inp['w_br_b']), w_br_c=f(inp['w_br_c']), w_out=f(inp['w_out']),
        final_norm_w=f(np.asarray(inp['final_norm_w']).reshape(1, D)),
        **hc)
    xs, xp = np.asarray(inp['x_sample']), np.asarray(inp['x_prompt'])
    in_maps = []
    ncores = cfg.get('ncores', 8)
    for c in range(ncores):
        b = c // 4
        m = dict(shared)
        m['x_all'] = f(np.concatenate([xs[b], xp[c * NP:(c + 1) * NP].reshape(NP * Lp, D)], axis=0))
        cv2 = np.stack([np.asarray(inp['c'])[b], np.asarray(inp['c_ctx'])], axis=0)
        m['cvecT'] = f(np.transpose(cv2.reshape(2, KT, 128), (2, 0, 1)).reshape(128, 32))
        m['ck'] = f(np.asarray(inp['cache_k'])[b].reshape(DEPTH, PAST, 1024))
        m['cv'] = f(np.asarray(inp['cache_v'])[b].reshape(DEPTH, PAST, 1024))
        m['st0T'] = f(np.transpose(np.asarray(inp['state_rwkv'])[b], (0, 1, 2, 4, 3)))
        in_maps.append(m)
    res = run_bass_kernel_spmd(nc, in_maps, core_ids=list(range(ncores)))
    R = res.results
    if cfg.get('raw'):
        return R
    nb = xs.shape[0]
    y_sample = np.stack([R[4 * b]['y_all'][:Ls] for b in range(nb)], axis=0).astype(np.float32)
    y_prompt = np.concatenate([R[c]['y_all'][Ls:].reshape(NP, Lp, D) for c in range(8)], axis=0).astype(np.float32)
    nk = np.concatenate([R[c]['out_k'] for c in range(8)], axis=0).reshape(8 * NP, DEPTH, Lp, 8, 2, 64).astype(np.float32)
    nv = np.concatenate([R[c]['out_v'] for c in range(8)], axis=0).reshape(8 * NP, DEPTH, Lp, 8, 128).astype(np.float32)
    ns = np.concatenate([R[c]['out_st'] for c in range(8)], axis=0)
    ns = np.ascontiguousarray(np.transpose(ns, (0, 1, 2, 3, 5, 4))).astype(np.float32)
    return (y_prompt, y_sample, nk, nv, ns)


def kernel(**inputs):
    cfg = dict(Ls=4096, Lp=256, NP=4, PAST=512)
    return run(cfg, inputs)
```

```python
import math
from contextlib import ExitStack
import numpy as np
import concourse.bass as bass
import concourse.mybir as mybir
from concourse.bass_utils import run_bass_kernel_spmd

F32 = mybir.dt.float32
BF16 = mybir.dt.bfloat16
AF = mybir.ActivationFunctionType
ALU = mybir.AluOpType
AX = mybir.AxisListType

D = 2048
KT = 16
INC = 17664
DEPTH = 2
C0 = math.exp(-0.5)
EPS = 1e-6
LN_EPS = 1e-5
GN_EPS = 64e-5
NPC = 9216
NPT = 8448


class Trk:
    def __init__(s, nc, es):
        s.nc = nc
        s.es = es
        s.eng = {'p': nc.tensor, 'v': nc.vector, 'a': nc.scalar, 'g': nc.gpsimd, 's': nc.sync}
        s.sem = {}
        s.cnt = {}
        s.waited = {}
        s.lastw = {}
        s.readers = {}
        s.nins = 0

    def _sem(s, key):
        if key not in s.sem:
            s.sem[key] = s.es.enter_context(s.nc.semaphore("sm%d" % len(s.sem)))
            s.cnt[key] = 0
        return s.sem[key]

    def _deps(s, R, W):
        d = {}
        for b in R:
            t = s.lastw.get(b)
            if t is not None and d.get(t[0], 0) < t[1]:
                d[t[0]] = t[1]
        for b in W:
            t = s.lastw.get(b)
            if t is not None and d.get(t[0], 0) < t[1]:
                d[t[0]] = t[1]
            rd = s.readers.get(b)
            if rd:
                for k, v in rd.items():
                    if d.get(k, 0) < v:
                        d[k] = v
        return d

    def _wait(s, e, d, skip_self=False):
        for k, v in d.items():
            if skip_self and k == e:
                continue
            if s.waited.get((e, k), 0) < v:
                s.eng[e].wait_ge(s._sem(k), v)
                s.waited[(e, k)] = v
                s.nins += 1

    def _record(s, tok, R, W):
        for b in W:
            s.lastw[b] = tok
            s.readers[b] = {}
        for b in R:
            rd = s.readers.setdefault(b, {})
            if rd.get(tok[0], 0) < tok[1]:
                rd[tok[0]] = tok[1]

    def op(s, e, fn, R=(), W=()):
        s._wait(e, s._deps(R, W), skip_self=(e == 'p'))
        ins = fn()
        sem = s._sem(e)
        s.cnt[e] += 1
        ins.then_inc(sem, 1)
        s._record((e, s.cnt[e]), R, W)
        s.nins += 1

    def dma(s, q, out, in_, stream, R=(), W=()):
        s._wait(q, s._deps(R, W))
        sem = s._sem(stream)
        s.cnt[stream] += 16
        s.eng[q].dma_start(out=out, in_=in_).then_inc(sem, 16)
        s._record((stream, s.cnt[stream]), R, W)
        s.nins += 1

    def barrier(s):
        keys = list(s.cnt.keys())
        for e in ('p', 'v', 'a', 'g', 's'):
            for k in keys:
                v = s.cnt[k]
                if v > 0 and s.waited.get((e, k), 0) < v:
                    s.eng[e].wait_ge(s._sem(k), v)
                    s.waited[(e, k)] = v
        s.lastw = {}
        s.readers = {}


class Ctx:
    pass


def build(cfg):
    Ls, Lp, NP, PAST = cfg['Ls'], cfg['Lp'], cfg['NP'], cfg['PAST']
    NT = Ls + NP * Lp
    assert Ls % 512 == 0 and (NP * Lp) % 512 == 0 and Lp % 256 == 0 and PAST % 128 == 0
    nc = bass.Bass("TRN2", target_bir_lowering=False)
    K = Ctx()
    K.nc = nc
    K.cfg = cfg
    K.NT = NT

    def din(name, shape, dt=F32):
        return nc.dram_tensor(name, list(shape), dt, kind="ExternalInput").ap()

    def dout(name, shape):
        return nc.dram_tensor(name, list(shape), F32, kind="ExternalOutput").ap()

    def dscr(name, shape, dt=F32):
        return nc.dram_tensor(name, list(shape), dt, kind="Internal").ap()

    I = {}
    I['x_all'] = din('x_all', [NT, D])
    I['cvecT'] = din('cvecT', [128, 32])
    I['ck'] = din('ck', [DEPTH, PAST, 1024])
    I['cv'] = din('cv', [DEPTH, PAST, 1024])
    I['st0T'] = din('st0T', [DEPTH, 2, 16, 64, 64])
    I['w_ada'] = din('w_ada', [DEPTH, D, 3 * D])
    I['b_ada'] = din('b_ada', [DEPTH, 3 * D])
    I['norm_w'] = din('norm_w', [DEPTH, D])
    I['w_in'] = din('w_in', [DEPTH, D, INC])
    I['lam4'] = din('lam4', [DEPTH, 256])
    I['subln_w'] = din('subln_w', [DEPTH, 128])
    I['conv_wT'] = din('conv_wT', [DEPTH, 1024, 31])
    I['conv_bT'] = din('conv_bT', [DEPTH, 128, 8])
    I['conv_ln_w'] = din('conv_ln_w', [DEPTH, 1024])
    I['conv_ln_b'] = din('conv_ln_b', [DEPTH, 1024])
    I['rwkv_w0'] = din('rwkv_w0', [DEPTH, 2, 1024])
    I['rwkv_w_up'] = din('rwkv_w_up', [DEPTH, 2, 64, 1024])
    I['rwkv_a0'] = din('rwkv_a0', [DEPTH, 2, 1024])
    I['rwkv_a_up'] = din('rwkv_a_up', [DEPTH, 2, 64, 1024])
    I['rwkv_k_k'] = din('rwkv_k_k', [DEPTH, 1024])
    I['rwkv_k_a'] = din('rwkv_k_a', [DEPTH, 1024])
    I['rwkv_r_k'] = din('rwkv_r_k', [DEPTH, 1024])
    I['rwkv_gn_w'] = din('rwkv_gn_w', [DEPTH, 1024])
    I['rwkv_gn_b'] = din('rwkv_gn_b', [DEPTH, 1024])
    I['w_br_a'] = din('w_br_a', [DEPTH, 1024, D])
    I['w_br_b'] = din('w_br_b', [DEPTH, 1024, D])
    I['w_br_c'] = din('w_br_c', [DEPTH, 1024, D])
    I['w_out'] = din('w_out', [DEPTH, D, D])
    I['final_norm_w'] = din('final_norm_w', [1, D])
    I['ident'] = din('ident', [128, 128])
    I['ropetab'] = din('ropetab', [Ls, 128])
    I['maskX'] = din('maskX', [2, 128, 384])
    I['maskY'] = din('maskY', [2, 128, 256])
    I['tri'] = din('tri', [2, 128, 128])
    K.I = I
    O = {}
    O['y_all'] = dout('y_all', [NT, D])
    O['out_k'] = dout('out_k', [NP, DEPTH, Lp, 1024])
    O['out_v'] = dout('out_v', [NP, DEPTH, Lp, 1024])
    O['out_st'] = dout('out_st', [NP, DEPTH, 2, 16, 64, 64])
    K.O = O
    S = {}
    S['X'] = dscr('X', [NT, D])
    S['P'] = dscr('P', [NT, NPC])
    S['PT'] = dscr('PT', [NPT, NT])
    S['Wb'] = dscr('Wb', [DEPTH, D, INC], BF16)
    S['Wa'] = dscr('Wba', [DEPTH, 1024, D], BF16)
    S['Wbb'] = dscr('Wbb', [DEPTH, 1024, D], BF16)
    S['Wc'] = dscr('Wbc', [DEPTH, 1024, D], BF16)
    S['Wo'] = dscr('Wbo', [DEPTH, D, D], BF16)
    S['ACTA'] = dscr('ACTA', [NT, 1024])
    S['ACTB'] = dscr('ACTB', [NT, 1024])
    S['ACTC'] = dscr('ACTC', [NT, 1024])
    S['ZC'] = dscr('ZC', [NT, 1024])
    S['YC0'] = dscr('YC0', [NT, 1024])
    S['YC1'] = dscr('YC1', [NT, 1024])
    S['BON'] = dscr('BON', [2, NT, 16])
    S['ADAd'] = dscr('ADAd', [2, 128, 3 * D])
    if cfg.get('debug'):
        S['Hdbg'] = dscr('Hdbg', [NT, D])
        S['HTdbg'] = dscr('HTdbg', [NT // 512, 128, KT * 512], BF16)
    K.S = S

    seqs = [dict(r0=0, L=Ls, ctx=PAST, rope=True, g=0, pi=None)]
    for i in range(NP):
        seqs.append(dict(r0=Ls + i * Lp, L=Lp, ctx=0, rope=False, g=1, pi=i))
    K.seqs = seqs

    with ExitStack() as es:
        T = Trk(nc, es)
        K.T = T
        K.uid = 0

        def sb(stack, shape, dt=F32, nm="t"):
            K.uid += 1
            return stack.enter_context(nc.sbuf_tensor("%s%d" % (nm, K.uid), list(shape), dt))

        def pb(stack, shape=(128, 512), dt=F32, nm="ps"):
            K.uid += 1
            return stack.enter_context(nc.psum_tensor("%s%d" % (nm, K.uid), list(shape), dt))

        K.sb = sb
        K.pb = pb
        K.ident = sb(es, [128, 128])
        T.dma('s', K.ident[:], I['ident'][:, :], 'c0')
        T.barrier()

        ORD = '0aABCDEZ'
        stop = ORD.index(cfg.get('stop', 'Z'))
        nl = cfg.get('nl', DEPTH)
        phase_convert(K)
        T.dma('s', S['X'][:, :], I['x_all'][:, :], 'c0')
        T.barrier()
        for l in range(nl):
            if stop >= ORD.index('a'):
                phase_ada(K, l)
            if stop >= ORD.index('A'):
                phase_A(K, l)
            if stop >= ORD.index('B'):
                phase_B(K, l)
            if stop >= ORD.index('C'):
                phase_C(K, l)
            if stop >= ORD.index('D'):
                phase_D(K, l)
            if stop >= ORD.index('E'):
                phase_E(K, l)
        T.barrier()
        for nm in cfg.get('taps', ()):
            src = S[nm]
            shp = list(src.shape)
            dst = nc.dram_tensor('tap_' + nm, shp, src.dtype, kind="ExternalOutput").ap()
            T.dma('s', dst, src, 'c0')
        T.barrier()
        print("instructions:", T.nins, "sems:", len(T.sem))
    return nc


def phase_convert(K):
    nc, T, I, S = K.nc, K.T, K.I, K.S
    with ExitStack() as ph:
        NS = 4
        CW = 2048
        fin = [K.sb(ph, [128, CW]) for _ in range(NS)]
        fout = [K.sb(ph, [128, CW], BF16) for _ in range(NS)]
        engs = ['v', 'a', 'g', 'v']
        cnt = [0]

        def conv(src, dst, Rr, Cc):
            for r0 in range(0, Rr, 128):
                for c0 in range(0, Cc, CW):
                    cw = min(CW, Cc - c0)
                    i = cnt[0]
                    s = i % NS
                    cnt[0] += 1
                    T.dma('s', fin[s][:, :cw], src[r0:r0 + 128, c0:c0 + cw], 'cvi%d' % s, W=[('cvi', s)])
                    e = engs[s]
                    if e == 'a':
                        T.op('a', lambda: nc.scalar.copy(out=fout[s][:, :cw], in_=fin[s][:, :cw]),
                             R=[('cvi', s)], W=[('cvo', s)])
                    elif e == 'v':
                        T.op('v', lambda: nc.vector.tensor_copy(out=fout[s][:, :cw], in_=fin[s][:, :cw]),
                             R=[('cvi', s)], W=[('cvo', s)])
                    else:
                        T.op('g', lambda: nc.gpsimd.tensor_copy(out=fout[s][:, :cw], in_=fin[s][:, :cw]),
                             R=[('cvi', s)], W=[('cvo', s)])
                    T.dma('g', dst[r0:r0 + 128, c0:c0 + cw], fout[s][:, :cw], 'cvo%d' % s, R=[('cvo', s)])

        for l in range(DEPTH):
            conv(I['w_in'][l], S['Wb'][l], D, INC)
            conv(I['w_br_a'][l], S['Wa'][l], 1024, D)
            conv(I['w_br_b'][l], S['Wbb'][l], 1024, D)
            conv(I['w_br_c'][l], S['Wc'][l], 1024, D)
            conv(I['w_out'][l], S['Wo'][l], D, D)
        T.barrier()


def phase_ada(K, l):
    nc, T, I = K.nc, K.T, K.I
    with ExitStack() as ph:
        cvt = K.sb(ph, [128, 32])
        CL = K.sb(ph, [128, 32, 128])
        ones1 = K.sb(ph, [1, 128])
        wa = [K.sb(ph, [128, KT, 512]) for _ in range(2)]
        bb = [K.sb(ph, [1, 512]) for _ in range(2)]
        nwb = K.sb(ph, [128, D])
        ADA = [K.sb(ph, [128, 3 * D]) for _ in range(2)]
        ps = [K.pb(ph) for _ in range(2)]
        T.dma('s', cvt[:], I['cvecT'][:, :], 'c0', W=['cvt'])
        T.dma('s', nwb[:], I['norm_w'][l:l + 1, :].to_broadcast([128, D]), 'c0', W=['nwb'])
        T.barrier()
        T.op('a', lambda: nc.scalar.activation(out=cvt[:], in_=cvt[:], func=AF.Silu), R=['cvt'], W=['cvt'])
        T.op('v', lambda: nc.vector.tensor_copy(out=CL[:], in_=cvt[:, :, None].to_broadcast([128, 32, 128])),
             R=['cvt'], W=['CL'])
        T.op('v', lambda: nc.vector.memset(ones1[:], 1.0), W=['ones1'])
        wv = I['w_ada'][l].rearrange("(kt p) c -> p kt c", p=128)
        n = 0
        for cb in range(12):
            s = cb % 2
            T.dma('s', bb[s][:], I['b_ada'][l:l + 1, cb * 512:(cb + 1) * 512], 'wa%d' % s, W=[('bb', s)])
            T.dma('s', wa[s][:], wv[:, :, cb * 512:(cb + 1) * 512], 'wa%d' % s, W=[('wa', s), ('bb', s)])
            for g in range(2):
                p_ = ps[n % 2]
                for kt in range(KT):
                    T.op('p', lambda: nc.tensor.matmul(p_[:, :], lhsT=CL[:, g * 16 + kt, :], rhs=wa[s][:, kt, :],
                                                       start=(kt == 0), stop=False),
                         R=['CL', ('wa', s)], W=[('ps', n % 2)])
                T.op('p', lambda: nc.tensor.matmul(p_[:, :], lhsT=ones1[0:1, :], rhs=bb[s][0:1, :],
                                                   start=False, stop=True),
                     R=['ones1', ('bb', s)], W=[('ps', n % 2)])
                dst = ADA[g][:, cb * 512:(cb + 1) * 512]
                if n % 2 == 0:
                    T.op('v', lambda: nc.vector.tensor_copy(out=dst, in_=p_[:, :]), R=[('ps', n % 2)], W=[('ADA', g, cb)])
                else:
                    T.op('a', lambda: nc.scalar.copy(out=dst, in_=p_[:, :]), R=[('ps', n % 2)], W=[('ADA', g, cb)])
                n += 1
        T.barrier()
        for g in range(2):
            T.op('v', lambda: nc.vector.scalar_tensor_tensor(out=ADA[g][:, D:2 * D], in0=ADA[g][:, D:2 * D], scalar=1.0,
                                                             in1=nwb[:], op0=ALU.add, op1=ALU.mult))
        T.barrier()
        for g in range(2):
            for j in range(3):
                T.dma('s', K.S['ADAd'][g][:, j * D:(j + 1) * D], ADA[g][:, j * D:(j + 1) * D], 'c0')
        T.barrier()
        if K.cfg.get('debug') and l == 0:
            dbg = nc.dram_tensor('dbg_ada', [2, 128, 3 * D], F32, kind="ExternalOutput").ap()
            for g in range(2):
                for j in range(3):
                    T.dma('s', dbg[g][:, j * D:(j + 1) * D], ADA[g][:, j * D:(j + 1) * D], 'c0')
            dbg2 = nc.dram_tensor('dbg_cl', [128, 32 * 128], F32, kind="ExternalOutput").ap()
            for j in range(2):
                T.dma('s', dbg2[:, j * 2048:(j + 1) * 2048], CL[:, j * 16:(j + 1) * 16, :].rearrange("p a b -> p (a b)"), 'c0')
            dbg3 = nc.dram_tensor('dbg_cvt', [128, 32], F32, kind="ExternalOutput").ap()
            T.dma('s', dbg3, cvt[:], 'c0')
            T.barrier()


def a_blocks():
    blks = []
    for i in range(8):
        blks.append(('TM', i * 512, 512, i * 512))
    for i in range(4):
        blks.append(('FM', 4096 + i * 512, 512, i * 512))
    for i in range(10):
        blks.append(('TM', 6144 + i * 512, 512, 4096 + i * 512))
    blks.append(('FM', 11264, 256, 2048))
    for i in range(12):
        blks.append(('FM', 11520 + i * 512, 512, 2304 + i * 512))
    return blks


def phase_A(K, l):
    nc, T, I, S = K.nc, K.T, K.I, K.S
    NT, Ls = K.NT, K.cfg['Ls']
    with ExitStack() as ph:
        xin = [K.sb(ph, [128, D]) for _ in range(2)]
        hh = [K.sb(ph, [128, D]) for _ in range(2)]
        sq = K.sb(ph, [128, D])
        st = [K.sb(ph, [128, 4]) for _ in range(2)]
        hT = K.sb(ph, [128, KT, 512], BF16)
        wblk = [K.sb(ph, [128, KT, 512], BF16) for _ in range(2)]
        stg = [K.sb(ph, [128, 512]) for _ in range(4)]
        pst = [K.pb(ph) for _ in range(2)]
        psm = [K.pb(ph) for _ in range(4)]
        wv = S['Wb'][l].rearrange("(kt p) c -> p kt c", p=128)
        ADA = [K.sb(ph, [128, 2 * D]) for _ in range(2)]
        for g in range(2):
            for j in range(2):
                T.dma('s', ADA[g][:, j * D:(j + 1) * D], S['ADAd'][g][:, j * D:(j + 1) * D], 'c0')
        T.barrier()
        blks = a_blocks()
        nt = 0
        nm = 0
        nw = 0
        for tb in range(NT // 512):
            g = 0 if tb * 512 < Ls else 1
            A_g = ADA[g][:, D:2 * D]
            sh_g = ADA[g][:, 0:D]
            for sub in range(4):
                r0 = tb * 512 + sub * 128
                s = sub % 2
                T.dma('s', xin[s][:], S['X'][r0:r0 + 128, :], 'xin%d' % s, W=[('xin', s)])
                T.op('v', lambda: nc.vector.tensor_tensor(out=sq[:], in0=xin[s][:], in1=xin[s][:], op=ALU.mult),
                     R=[('xin', s)], W=['sq'])
                T.op('v', lambda: nc.vector.tensor_reduce(out=st[s][:, 0:1], in_=sq[:], axis=AX.X, op=ALU.add),
                     R=['sq'], W=[('st', s)])
                T.op('a', lambda: nc.scalar.activation(out=st[s][:, 1:2], in_=st[s][:, 0:1], func=AF.Sqrt,
                                                       bias=EPS, scale=1.0 / D), R=[('st', s)], W=[('st', s)])
                T.op('v', lambda: nc.vector.reciprocal(out=st[s][:, 2:3], in_=st[s][:, 1:2]), R=[('st', s)], W=[('st', s)])
                T.op('v', lambda: nc.vector.scalar_tensor_tensor(out=hh[s][:], in0=xin[s][:], scalar=st[s][:, 2:3],
                                                                 in1=A_g, op0=ALU.mult, op1=ALU.mult),
                     R=[('xin', s), ('st', s)], W=[('hh', s)])
                T.op('v', lambda: nc.vector.tensor_tensor(out=hh[s][:], in0=hh[s][:], in1=sh_g, op=ALU.add),
                     R=[('hh', s)], W=[('hh', s)])
                if K.cfg.get('debug'):
                    T.dma('g', S['Hdbg'][r0:r0 + 128, :], hh[s][:], 'dbgh', R=[('hh', s)])
                    T.barrier()
                for q4 in range(4):
                    p_ = pst[nt % 2]
                    for j in range(4):
                        kt = q4 * 4 + j
                        T.op('p', lambda: nc.tensor.transpose(out=p_[:, j * 128:(j + 1) * 128],
                                                              in_=hh[s][:, kt * 128:(kt + 1) * 128], identity=K.ident[:]),
                             R=[('hh', s)], W=[('pst', nt % 2)])
                    dst = hT[:, q4 * 4:q4 * 4 + 4, sub * 128:(sub + 1) * 128]
                    src = p_[:, :].rearrange("p (j t) -> p j t", j=4)
                    if nt % 2 == 0:
                        T.op('a', lambda: nc.scalar.copy(out=dst, in_=src), R=[('pst', nt % 2)], W=[('hT', sub)])
                    else:
                        T.op('v', lambda: nc.vector.tensor_copy(out=dst, in_=src), R=[('pst', nt % 2)], W=[('hT', sub)])
                    nt += 1
            if K.cfg.get('debug'):
                T.dma('g', S['HTdbg'][tb], hT[:, :, :].rearrange('p a b -> p (a b)'), 'dbgh', R=[('hT', 0), ('hT', 1), ('hT', 2), ('hT', 3)])
                T.barrier()
            for (mode, c0, ncol, d0) in blks:
                ws = nw % 2
                nw += 1
                T.dma('s', wblk[ws][:, :, :ncol], wv[:, :, c0:c0 + ncol], 'wblk%d' % ws, W=[('wblk', ws)])
                if mode == 'TM':
                    for sub in range(4):
                        r0 = tb * 512 + sub * 128
                        pi = nm % 4
                        p_ = psm[pi]
                        for kt in range(KT):
                            T.op('p', lambda: nc.tensor.matmul(p_[:, :ncol], lhsT=hT[:, kt, sub * 128:(sub + 1) * 128],
                                                               rhs=wblk[ws][:, kt, :ncol], start=(kt == 0), stop=(kt == KT - 1)),
                                 R=[('hT', sub), ('wblk', ws)], W=[('psm', pi)])
                        if nm % 2 == 0:
                            T.op('a', lambda: nc.scalar.copy(out=stg[pi][:, :ncol], in_=p_[:, :ncol]), R=[('psm', pi)], W=[('stg', pi)])
                        else:
                            T.op('v', lambda: nc.vector.tensor_copy(out=stg[pi][:, :ncol], in_=p_[:, :ncol]), R=[('psm', pi)], W=[('stg', pi)])
                        T.dma('g', S['P'][r0:r0 + 128, d0:d0 + ncol], stg[pi][:, :ncol], 'stg%d' % pi, R=[('stg', pi)])
                        nm += 1
                else:
                    for cs in range(ncol // 128):
                        pi = nm % 4
                        p_ = psm[pi]
                        for kt in range(KT):
                            T.op('p', lambda: nc.tensor.matmul(p_[:, :], lhsT=wblk[ws][:, kt, cs * 128:(cs + 1) * 128],
                                                               rhs=hT[:, kt, :], start=(kt == 0), stop=(kt == KT - 1)),
                                 R=[('hT', 0), ('hT', 1), ('hT', 2), ('hT', 3), ('wblk', ws)], W=[('psm', pi)])
                        if nm % 2 == 0:
                            T.op('a', lambda: nc.scalar.copy(out=stg[pi][:], in_=p_[:, :]), R=[('psm', pi)], W=[('stg', pi)])
                        else:
                            T.op('v', lambda: nc.vector.tensor_copy(out=stg[pi][:], in_=p_[:, :]), R=[('psm', pi)], W=[('stg', pi)])
                        T.dma('g', S['PT'][d0 + cs * 128:d0 + (cs + 1) * 128, tb * 512:(tb + 1) * 512], stg[pi][:],
                              'stg%d' % pi, R=[('stg', pi)])
                        nm += 1
        T.barrier()
        for sq_ in K.seqs:
            if sq_['pi'] is None:
                continue
            r0, L, pi = sq_['r0'], sq_['L'], sq_['pi']
            T.dma('s', K.O['out_k'][pi, l, :, :], S['P'][r0:r0 + L, 1024:2048], 'c0')
            T.dma('s', K.O['out_v'][pi, l, :, :], S['P'][r0:r0 + L, 2048:3072], 'c0')
        T.barrier()


def phase_B(K, l):
    nc, T, I, S = K.nc, K.T, K.I, K.S
    cfg = K.cfg
    Ls, PAST = cfg['Ls'], cfg['PAST']
    lam_init = 0.8 - 0.6 * math.exp(-0.3 * l)
    NKmax = PAST + Ls
    with ExitStack() as ph:
        KTt = K.sb(ph, [128, NKmax], BF16)
        QTp = K.sb(ph, [128, Ls // 256, 2, 256], BF16)
        V1 = K.sb(ph, [128, NKmax // 128, 130], BF16)
        raw = [K.sb(ph, [128, 128]) for _ in range(4)]
        vraw = [K.sb(ph, [128, 128]) for _ in range(2)]
        rot = [K.sb(ph, [128, 128]) for _ in range(2)]
        tmp = [K.sb(ph, [128, 64]) for _ in range(2)]
        tab = [K.sb(ph, [128, 128]) for _ in range(2)]
        E = [K.sb(ph, [128, 512], BF16) for _ in range(4)]
        rr = [K.sb(ph, [128, 8]) for _ in range(2)]
        o_ = [K.sb(ph, [128, 128]) for _ in range(2)]
        o2 = K.sb(ph, [128, 128])
        ga = [K.sb(ph, [128, 128]) for _ in range(2)]
        subw = K.sb(ph, [128, 128])
        lq = K.sb(ph, [128, 256])
        lt = K.sb(ph, [128, 128])
        lam = K.sb(ph, [128, 4])
        ps_s = [K.pb(ph) for _ in range(4)]
        acc = [K.pb(ph) for _ in range(4)]
        ps_t = ps_s
        T.dma('s', subw[:], I['subln_w'][l:l + 1, :].to_broadcast([128, 128]), 'c0')
        T.dma('s', lq[:], I['lam4'][l:l + 1, :].to_broadcast([128, 256]), 'c0')
        T.op('v', lambda: nc.vector.memset(V1[:], 1.0))
        T.op('v', lambda: nc.vector.memset(QTp[:], 0.0))
        T.barrier()
        T.op('v', lambda: nc.vector.tensor_scalar(out=subw[:], in0=subw[:], scalar1=(1.0 - lam_init), scalar2=None, op0=ALU.mult))
        T.op('v', lambda: nc.vector.tensor_tensor(out=lt[:, 0:64], in0=lq[:, 0:64], in1=lq[:, 64:128], op=ALU.mult))
        T.op('v', lambda: nc.vector.tensor_tensor(out=lt[:, 64:128], in0=lq[:, 128:192], in1=lq[:, 192:256], op=ALU.mult))
        T.barrier()
        T.op('v', lambda: nc.vector.tensor_reduce(out=lam[:, 0:2], in_=lt[:, :].rearrange("p (a b) -> p a b", a=2), axis=AX.X, op=ALU.add))
        T.barrier()
        T.op('a', lambda: nc.scalar.activation(out=lam[:, 0:2], in_=lam[:, 0:2], func=AF.Exp))
        T.barrier()
        T.op('v', lambda: nc.vector.scalar_tensor_tensor(out=lam[:, 2:3], in0=lam[:, 1:2], scalar=-lam_init, in1=lam[:, 0:1],
                                                         op0=ALU.add, op1=ALU.subtract))
        T.barrier()
        nlam = lam[:, 2:3]
        cnt = dict(t=0, r=0, v=0, s=0, e=0, o=0)

        def rope(src, tb_idx):
            ti = cnt['r'] % 2
            cnt['r'] += 1
            T.dma('s', tab[ti][:], I['ropetab'][tb_idx * 128:(tb_idx + 1) * 128, :], 'tab%d' % ti, W=[('tab', ti)])
            x = raw[src][:, :].rearrange("p (a b c) -> p a b c", a=4, b=2)
            x1, x2 = x[:, :, 0, :], x[:, :, 1, :]
            cosv = tab[ti][:, 0:64].rearrange("p (a c) -> p a c", a=4)
            sinv = tab[ti][:, 64:128].rearrange("p (a c) -> p a c", a=4)
            y = rot[ti][:, :].rearrange("p (a b c) -> p a b c", a=4, b=2)
            y1, y2 = y[:, :, 0, :], y[:, :, 1, :]
            t1 = tmp[0][:, :].rearrange("p (a c) -> p a c", a=4)
            t2 = tmp[1][:, :].rearrange("p (a c) -> p a c", a=4)
            Rk = [('raw', src), ('tab', ti)]
            T.op('v', lambda: nc.vector.tensor_tensor(out=y1, in0=x1, in1=cosv, op=ALU.mult), R=Rk, W=[('rot', ti)])
            T.op('v', lambda: nc.vector.tensor_tensor(out=t1, in0=x2, in1=sinv, op=ALU.mult), R=Rk, W=[('tmp', 0)])
            T.op('v', lambda: nc.vector.tensor_tensor(out=y2, in0=x2, in1=cosv, op=ALU.mult), R=Rk, W=[('rot', ti)])
            T.op('v', lambda: nc.vector.tensor_tensor(out=t2, in0=x1, in1=sinv, op=ALU.mult), R=Rk, W=[('tmp', 1)])
            T.op('v', lambda: nc.vector.tensor_tensor(out=y1, in0=y1, in1=t1, op=ALU.subtract), R=[('rot', ti), ('tmp', 0)], W=[('rot', ti)])
            T.op('v', lambda: nc.vector.tensor_tensor(out=y2, in0=y2, in1=t2, op=ALU.add), R=[('rot', ti), ('tmp', 1)], W=[('rot', ti)])
            return rot[ti], ('rot', ti)

        def transpose_to(src_tile, src_key, dst_ap, dst_key, split=None):
            ti = cnt['t'] % 2
            cnt['t'] += 1
            T.op('p', lambda: nc.tensor.transpose(out=ps_t[ti][:, 0:128], in_=src_tile[:, :], identity=K.ident[:]),
                 R=[src_key], W=[('pss', ti)])
            if split is None:
                parts = [(dst_ap, ps_t[ti][:, 0:128])]
            else:
                parts = [(split[0], ps_t[ti][0:64, 0:128]), (split[1], ps_t[ti][64:128, 0:128])]
            for (dap, sap) in parts:
                if ti == 0:
                    T.op('a', lambda: nc.scalar.copy(out=dap, in_=sap), W=[('pss', ti), dst_key])
                else:
                    T.op('v', lambda: nc.vector.tensor_copy(out=dap, in_=sap), W=[('pss', ti), dst_key])

        for sq_ in K.seqs:
            r0, L, ctx, do_rope = sq_['r0'], sq_['L'], sq_['ctx'], sq_['rope']
            nctx = ctx // 128
            nkt = nctx + L // 128
            for h in range(8):
                for j in range(nkt):
                    ri = cnt['v'] % 4
                    vi = cnt['v'] % 2
                    cnt['v'] += 1
                    if j < nctx:
                        srck = I['ck'][l, j * 128:(j + 1) * 128, h * 128:(h + 1) * 128]
                        srcv = I['cv'][l, j * 128:(j + 1) * 128, h * 128:(h + 1) * 128]
                    else:
                        rows = r0 + (j - nctx) * 128
                        srck = S['P'][rows:rows + 128, 1024 + h * 128:1024 + (h + 1) * 128]
                        srcv = S['P'][rows:rows + 128, 2048 + h * 128:2048 + (h + 1) * 128]
                    T.dma('s', raw[ri][:], srck, 'raw%d' % ri, W=[('raw', ri)])
                    T.dma('s', vraw[vi][:], srcv, 'vraw%d' % vi, W=[('vraw', vi)])
                    if do_rope and j >= nctx:
                        tl, tk = rope(ri, j - nctx)
                    else:
                        tl, tk = raw[ri], ('raw', ri)
                    transpose_to(tl, tk, KTt[:, j * 128:(j + 1) * 128], ('KT', j))
                    T.op('a', lambda: nc.scalar.copy(out=V1[:, j, 0:128], in_=vraw[vi][:]), R=[('vraw', vi)], W=[('V1', j)])
                for qt in range(L // 128):
                    ri = cnt['v'] % 4
                    cnt['v'] += 1
                    rows = r0 + qt * 128
                    T.dma('s', raw[ri][:], S['P'][rows:rows + 128, h * 128:(h + 1) * 128], 'raw%d' % ri, W=[('raw', ri)])
                    if do_rope:
                        tl, tk = rope(ri, qt)
                    else:
                        tl, tk = raw[ri], ('raw', ri)
                    qo = (qt % 2) * 128
                    transpose_to(tl, tk, None, ('QT', qt),
                                 split=(QTp[0:64, qt // 2, 0, qo:qo + 128], QTp[64:128, qt // 2, 1, qo:qo + 128]))
                its = [(qb, j) for qb in range(L // 256) for j in range(nkt)]
                DPIPE = 3

                def emit_S(n):
                    qb, j = its[n]
                    si = n % 4
                    Rq = [('KT', j), ('QT', 2 * qb), ('QT', 2 * qb + 1)]
                    T.op('p', lambda: nc.tensor.matmul(ps_s[si][:, 0:512], lhsT=KTt[:, j * 128:(j + 1) * 128],
                                                       rhs=QTp[:, qb, :, :].rearrange("p m q -> p (m q)"),
                                                       start=True, stop=True), R=Rq, W=[('pss', si)])
                    T.op('a', lambda: nc.scalar.activation(out=E[si][:, :], in_=ps_s[si][:, 0:512], func=AF.Exp, scale=0.125),
                         W=[('pss', si), ('E', si)])

                def emit_PV(n):
                    qb, j = its[n]
                    si = n % 4
                    for m in range(2):
                        for sub in range(2):
                            a_i = m * 2 + sub
                            T.op('p', lambda: nc.tensor.matmul(acc[a_i][:, 0:129],
                                                               lhsT=E[si][:, m * 256 + sub * 128:m * 256 + (sub + 1) * 128],
                                                               rhs=V1[:, j, 0:129], start=(j == 0), stop=(j == nkt - 1)),
                                 R=[('E', si), ('V1', j)], W=[('acc', a_i)])

                def epilogue(qb):
                    for sub in range(2):
                        oi = cnt['o'] % 2
                        cnt['o'] += 1
                        rows = r0 + qb * 256 + sub * 128
                        O1, O2 = acc[sub], acc[2 + sub]
                        r_ = rr[oi]
                        o = o_[oi]
                        T.dma('s', ga[oi][:], S['P'][rows:rows + 128, 3072 + h * 128:3072 + (h + 1) * 128], 'ga%d' % oi, W=[('ga', oi)])
                        T.op('a', lambda: nc.scalar.activation(out=ga[oi][:], in_=ga[oi][:], func=AF.Silu), R=[('ga', oi)], W=[('ga', oi)])
                        T.op('v', lambda: nc.vector.reciprocal(out=r_[:, 0:1], in_=O1[:, 128:129]), W=[('acc', sub), ('rr', oi)])
                        T.op('v', lambda: nc.vector.reciprocal(out=r_[:, 1:2], in_=O2[:, 128:129]), W=[('acc', 2 + sub), ('rr', oi)])
                        T.op('v', lambda: nc.vector.tensor_tensor(out=r_[:, 2:3], in0=r_[:, 1:2], in1=nlam, op=ALU.mult), R=[('rr', oi)], W=[('rr', oi)])
                        T.op('v', lambda: nc.vector.tensor_scalar(out=o[:], in0=O1[:, 0:128], scalar1=r_[:, 0:1], scalar2=None, op0=ALU.mult),
                             R=[('rr', oi)], W=[('acc', sub), ('o', oi)])
                        T.op('v', lambda: nc.vector.scalar_tensor_tensor(out=o[:], in0=O2[:, 0:128], scalar=r_[:, 2:3], in1=o[:],
                                                                         op0=ALU.mult, op1=ALU.add),
                             R=[('rr', oi), ('o', oi)], W=[('acc', 2 + sub), ('o', oi)])
                        T.op('v', lambda: nc.vector.tensor_tensor(out=o2[:], in0=o[:], in1=o[:], op=ALU.mult), R=[('o', oi)], W=['o2'])
                        T.op('v', lambda: nc.vector.tensor_reduce(out=r_[:, 3:4], in_=o2[:], axis=AX.X, op=ALU.add), R=['o2'], W=[('rr', oi)])
                        T.op('a', lambda: nc.scalar.activation(out=r_[:, 4:5], in_=r_[:, 3:4], func=AF.Sqrt, bias=EPS, scale=1.0 / 128),
                             R=[('rr', oi)], W=[('rr', oi)])
                        T.op('v', lambda: nc.vector.reciprocal(out=r_[:, 5:6], in_=r_[:, 4:5]), R=[('rr', oi)], W=[('rr', oi)])
                        T.op('v', lambda: nc.vector.scalar_tensor_tensor(out=o[:], in0=o[:], scalar=r_[:, 5:6], in1=subw[:],
                                                                         op0=ALU.mult, op1=ALU.mult), R=[('o', oi), ('rr', oi)], W=[('o', oi)])
                        T.op('v', lambda: nc.vector.tensor_tensor(out=o[:], in0=o[:], in1=ga[oi][:], op=ALU.mult),
                             R=[('o', oi), ('ga', oi)], W=[('o', oi)])
                        T.dma('g', S['ACTA'][rows:rows + 128, h * 128:(h + 1) * 128], o[:], 'oa%d' % oi, R=[('o', oi)])

                for n in range(min(DPIPE, len(its))):
                    emit_S(n)
                for n in range(len(its)):
                    if n + DPIPE < len(its):
                        emit_S(n + DPIPE)
                    emit_PV(n)
                    if its[n][1] == nkt - 1:
                        epilogue(its[n][0])
        T.barrier()


def phase_C(K, l):
    nc, T, I, S = K.nc, K.T, K.I, K.S
    Ls = K.cfg['Ls']
    NT = K.NT
    with ExitStack() as ph:
        a_t = K.sb(ph, [128, Ls])
        g_t = K.sb(ph, [128, Ls])
        zp = K.sb(ph, [128, Ls + 30])
        ac = K.sb(ph, [128, Ls])
        cw = K.sb(ph, [128, 8, 31])
        cb_ = K.sb(ph, [128, 8])
        stg = [K.sb(ph, [128, 4, 128]) for _ in range(2)]
        pst = [K.pb(ph) for _ in range(2)]
        T.dma('s', cw[:], I['conv_wT'][l].rearrange("(cb p) j -> p cb j", p=128), 'c0')
        T.dma('s', cb_[:], I['conv_bT'][l], 'c0')
        T.barrier()
        nt = 0
        for sq_ in K.seqs:
            r0, L = sq_['r0'], sq_['L']
            for cb in range(8):
                for c0_ in range(0, L, 2048):
                    c1_ = min(L, c0_ + 2048)
                    T.dma('s', a_t[:, c0_:c1_], S['PT'][cb * 128:(cb + 1) * 128, r0 + c0_:r0 + c1_], 'ca', W=['a_t'])
                    T.dma('s', g_t[:, c0_:c1_], S['PT'][1024 + cb * 128:1024 + (cb + 1) * 128, r0 + c0_:r0 + c1_], 'cg', W=['g_t'])
                T.op('a', lambda: nc.scalar.activation(out=g_t[:, :L], in_=g_t[:, :L], func=AF.Sigmoid), R=['g_t'], W=['g_t'])
                T.op('g', lambda: nc.gpsimd.memset(zp[:, 0:15], 0.0), W=['zp'])
                T.op('g', lambda: nc.gpsimd.memset(zp[:, 15 + L:30 + L], 0.0), W=['zp'])
                T.op('v', lambda: nc.vector.tensor_tensor(out=zp[:, 15:15 + L], in0=a_t[:, :L], in1=g_t[:, :L], op=ALU.mult),
                     R=['a_t', 'g_t'], W=['zp'])
                T.op('v', lambda: nc.vector.tensor_scalar(out=ac[:, :L], in0=zp[:, 0:L], scalar1=cw[:, cb, 0:1], scalar2=cb_[:, cb:cb + 1],
                                                          op0=ALU.mult, op1=ALU.add), R=['zp'], W=['ac'])
                for j in range(1, 31):
                    T.op('v', lambda: nc.vector.scalar_tensor_tensor(out=ac[:, :L], in0=zp[:, j:j + L], scalar=cw[:, cb, j:j + 1],
                                                                     in1=ac[:, :L], op0=ALU.mult, op1=ALU.add), R=['zp', 'ac'], W=['ac'])
                for t4 in range(L // 512):
                    pi = nt % 2
                    nt += 1
                    for j in range(4):
                        tt = t4 * 4 + j
                        T.op('p', lambda: nc.tensor.transpose(out=pst[pi][:, j * 128:(j + 1) * 128], in_=ac[:, tt * 128:(tt + 1) * 128],
                                                              identity=K.ident[:]), R=['ac'], W=[('pst', pi)])
                    T.op('a', lambda: nc.scalar.copy(out=stg[pi][:, :, :], in_=pst[pi][:, :].rearrange("p (j c) -> p j c", j=4)),
                         R=[('pst', pi)], W=[('stg', pi)])
                    rows = r0 + t4 * 512
                    T.dma('g', S['ZC'][rows:rows + 512, cb * 128:(cb + 1) * 128].rearrange("(j p) c -> p j c", p=128), stg[pi][:, :, :],
                          'cs%d' % pi, R=[('stg', pi)])
                if L % 512 != 0:
                    t0 = (L // 512) * 512
                    pi = nt % 2
                    nt += 1
                    nrem = (L - t0) // 128
                    for j in range(nrem):
                        tt = t0 // 128 + j
                        T.op('p', lambda: nc.tensor.transpose(out=pst[pi][:, j * 128:(j + 1) * 128], in_=ac[:, tt * 128:(tt + 1) * 128],
                                                              identity=K.ident[:]), R=['ac'], W=[('pst', pi)])
                    T.op('a', lambda: nc.scalar.copy(out=stg[pi][:, 0:nrem, :], in_=pst[pi][:, 0:nrem * 128].rearrange("p (j c) -> p j c", j=nrem)),
                         R=[('pst', pi)], W=[('stg', pi)])
                    rows = r0 + t0
                    T.dma('g', S['ZC'][rows:rows + nrem * 128, cb * 128:(cb + 1) * 128].rearrange("(j p) c -> p j c", p=128),
                          stg[pi][:, 0:nrem, :], 'cs%d' % pi, R=[('stg', pi)])
        T.barrier()
    with ExitStack() as ph:
        z = [K.sb(ph, [128, 1024]) for _ in range(2)]
        gb = [K.sb(ph, [128, 1024]) for _ in range(2)]
        lw = K.sb(ph, [128, 1024])
        lb = K.sb(ph, [128, 1024])
        bs = [K.sb(ph, [128, 12]) for _ in range(2)]
        mv = [K.sb(ph, [128, 4]) for _ in range(2)]
        T.dma('s', lw[:], I['conv_ln_w'][l:l + 1, :].to_broadcast([128, 1024]), 'c0')
        T.dma('s', lb[:], I['conv_ln_b'][l:l + 1, :].to_broadcast([128, 1024]), 'c0')
        T.barrier()
        for tt in range(NT // 128):
            s = tt % 2
            rows = tt * 128
            T.dma('s', z[s][:], S['ZC'][rows:rows + 128, :], 'z%d' % s, W=[('z', s)])
            T.dma('s', gb[s][:], S['P'][rows:rows + 128, 4096:5120], 'gb%d' % s, W=[('gb', s)])
            T.op('a', lambda: nc.scalar.activation(out=gb[s][:], in_=gb[s][:], func=AF.Silu), R=[('gb', s)], W=[('gb', s)])
            T.op('v', lambda: nc.vector.bn_stats(out=bs[s][:, 0:6], in_=z[s][:, 0:512]), R=[('z', s)], W=[('bs', s)])
            T.op('v', lambda: nc.vector.bn_stats(out=bs[s][:, 6:12], in_=z[s][:, 512:1024]), R=[('z', s)], W=[('bs', s)])
            T.op('v', lambda: nc.vector.bn_aggr(out=mv[s][:, 0:2], in_=bs[s][:, :]), R=[('bs', s)], W=[('mv', s)])
            T.op('a', lambda: nc.scalar.activation(out=mv[s][:, 2:3], in_=mv[s][:, 1:2], func=AF.Sqrt, bias=LN_EPS, scale=1.0),
                 R=[('mv', s)], W=[('mv', s)])
            T.op('v', lambda: nc.vector.reciprocal(out=mv[s][:, 3:4], in_=mv[s][:, 2:3]), R=[('mv', s)], W=[('mv', s)])
            T.op('v', lambda: nc.vector.tensor_scalar(out=z[s][:], in0=z[s][:], scalar1=mv[s][:, 0:1], scalar2=mv[s][:, 3:4],
                                                      op0=ALU.subtract, op1=ALU.mult), R=[('z', s), ('mv', s)], W=[('z', s)])
            T.op('v', lambda: nc.vector.tensor_tensor(out=z[s][:], in0=z[s][:], in1=lw[:], op=ALU.mult), R=[('z', s)], W=[('z', s)])
            T.op('v', lambda: nc.vector.tensor_tensor(out=z[s][:], in0=z[s][:], in1=lb[:], op=ALU.add), R=[('z', s)], W=[('z', s)])
            T.op('a', lambda: nc.scalar.activation(out=z[s][:], in_=z[s][:], func=AF.Silu), R=[('z', s)], W=[('z', s)])
            T.op('v', lambda: nc.vector.tensor_tensor(out=z[s][:], in0=z[s][:], in1=gb[s][:], op=ALU.mult), R=[('z', s), ('gb', s)], W=[('z', s)])
            T.dma('g', S['ACTB'][rows:rows + 128, :], z[s][:], 'zo%d' % s, R=[('z', s)])
        T.barrier()


def phase_D(K, l):
    nc, T, I, S = K.nc, K.T, K.I, K.S
    NT = K.NT
    with ExitStack() as ph:
        sb = lambda shape, dt=F32: K.sb(ph, shape, dt)
        kkb = sb([128, 1024]); kab = sb([128, 1024]); rkb = sb([128, 1024])
        w0b = sb([128, 1024]); a0b = sb([128, 1024])
        wup = sb([64, 1024]); aup = sb([64, 1024])
        mX = sb([128, 384]); mY = sb([128, 256]); tri = sb([128, 128]); nc0 = sb([128, 1])
        rkv = sb([128, 3072])
        xwT = sb([64, 128]); xaT = sb([64, 128])
        wsig = sb([128, 1024]); a_ = sb([128, 1024])
        kk = sb([128, 1024]); kd = sb([128, 1024]); b_ = sb([128, 1024])
        epos = sb([128, 1024]); eneg = sb([128, 1024]); eprv = sb([128, 1024])
        t1 = eprv; t2 = epos
        s16 = sb([128, 64])
        gC = sb([128, 8])
        bon = sb([128, 16])
        ZALL = sb([128, 16, 128], BF16 if K.cfg.get('scan_bf16', False) else F32)
        FT = [sb([128, 4, 128]) for _ in range(8)]
        SDT = BF16 if K.cfg.get('scan_bf16', False) else F32
        MXt = [sb([128, 384]) for _ in range(8)]
        MYt = [sb([128, 256]) for _ in range(8)]
        LV = [[sb([128, 384], SDT) for _ in range(2)] for _ in range(8)]
        AN0 = [sb([128, 256], SDT) for _ in range(8)]
        XW = [sb([128, 128]) for _ in range(8)]
        U0 = sb([128, 16, 64])
        WT = [sb([128, 128]) for _ in range(8)]
        UN = sb([128, 16, 64])
        STX = [sb([128, 128]) for _ in range(8)]
        ysb = sb([128, 1024])
        psb = [K.pb(ph) for _ in range(8)]
        r_ = rkv[:, 0:1024]
        k_ = rkv[:, 1024:2048]
        v_ = rkv[:, 2048:3072]

        def bc(src_row):
            return src_row.to_broadcast([128, 1024])

        T.dma('s', kkb[:], bc(I['rwkv_k_k'][l:l + 1, :]), 'c0')
        T.dma('s', kab[:], bc(I['rwkv_k_a'][l:l + 1, :]), 'c0')
        T.dma('s', rkb[:], bc(I['rwkv_r_k'][l:l + 1, :]), 'c0')
        T.op('v', lambda: nc.vector.memset(nc0[:], -C0))
        T.barrier()
        vv = lambda ap: ap.rearrange("p (h j) -> p h j", j=64)

        for d in range(2):
            T.dma('s', w0b[:], bc(I['rwkv_w0'][l, d:d + 1, :]), 'c0')
            T.dma('s', a0b[:], bc(I['rwkv_a0'][l, d:d + 1, :]), 'c0')
            T.dma('s', wup[:], I['rwkv_w_up'][l, d], 'c0')
            T.dma('s', aup[:], I['rwkv_a_up'][l, d], 'c0')
            T.dma('s', mX[:], I['maskX'][d], 'c0')
            T.dma('s', mY[:], I['maskY'][d], 'c0')
            T.dma('s', tri[:], I['tri'][d], 'c0')
            T.barrier()
            YC = S['YC0'] if d == 0 else S['YC1']
            for sq_ in K.seqs:
                r0, L, pi = sq_['r0'], sq_['L'], sq_['pi']
                nch = L // 128
                for p in range(8):
                    T.op('v', lambda: nc.vector.memset(STX[p][:], 0.0), W=[('STX', p)])
                if pi is None:
                    for p in range(8):
                        T.dma('s', STX[p][0:64, 0:64], I['st0T'][l, d, 2 * p], 'st0', W=[('STX', p)])
                        T.dma('s', STX[p][64:128, 64:128], I['st0T'][l, d, 2 * p + 1], 'st0', W=[('STX', p)])
                    T.barrier()
                order = range(nch) if d == 0 else range(nch - 1, -1, -1)
                for c in order:
                    rows = r0 + c * 128
                    for j3 in range(3):
                        T.dma('s', rkv[:, j3 * 1024:(j3 + 1) * 1024], S['P'][rows:rows + 128, 5120 + j3 * 1024:5120 + (j3 + 1) * 1024], 'rkv', W=['rkv'])
                    T.dma('s', xwT[:], S['PT'][2048 + d * 64:2048 + (d + 1) * 64, rows:rows + 128], 'xw', W=['xwT'])
                    T.dma('s', xaT[:], S['PT'][2176 + d * 64:2176 + (d + 1) * 64, rows:rows + 128], 'xa', W=['xaT'])
                    T.op('a', lambda: nc.scalar.activation(out=xwT[:], in_=xwT[:], func=AF.Tanh), R=['xwT'], W=['xwT'])
                    for hf in range(2):
                        T.op('p', lambda: nc.tensor.matmul(psb[0 + hf][:, :], lhsT=xwT[:, :], rhs=wup[:, hf * 512:(hf + 1) * 512],
                                                           start=True, stop=True), R=['xwT'], W=[('ps', hf)])
                        T.op('p', lambda: nc.tensor.matmul(psb[2 + hf][:, :], lhsT=xaT[:, :], rhs=aup[:, hf * 512:(hf + 1) * 512],
                                                           start=True, stop=True), R=['xaT'], W=[('ps', 2 + hf)])
                    for hf in range(2):
                        cs_ = slice(hf * 512, (hf + 1) * 512)
                        T.op('v', lambda: nc.vector.tensor_tensor(out=wsig[:, cs_], in0=psb[hf][:, :], in1=w0b[:, cs_], op=ALU.add),
                             W=[('ps', hf), 'wsig'])
                        T.op('v', lambda: nc.vector.tensor_tensor(out=a_[:, cs_], in0=psb[2 + hf][:, :], in1=a0b[:, cs_], op=ALU.add),
                             W=[('ps', 2 + hf), 'a'])
                    T.op('a', lambda: nc.scalar.activation(out=wsig[:], in_=wsig[:], func=AF.Sigmoid), R=['wsig'], W=['wsig'])
                    T.op('a', lambda: nc.scalar.activation(out=a_[:], in_=a_[:], func=AF.Sigmoid), R=['a'], W=['a'])
                    T.op('v', lambda: nc.vector.tensor_tensor(out=kk[:], in0=k_, in1=kkb[:], op=ALU.mult), R=['rkv'], W=['kk'])
                    T.op('g', lambda: nc.gpsimd.tensor_tensor(out=t1[:], in0=kk[:], in1=kk[:], op=ALU.mult), R=['kk'], W=['eprv'])
                    T.op('v', lambda: nc.vector.tensor_reduce(out=s16[:, 0:16], in_=vv(t1[:, :]), axis=AX.X, op=ALU.add), R=['eprv'], W=['s16'])
                    T.op('a', lambda: nc.scalar.activation(out=s16[:, 16:32], in_=s16[:, 0:16], func=AF.Sqrt), R=['s16'], W=['s16'])
                    T.op('v', lambda: nc.vector.tensor_scalar(out=s16[:, 16:32], in0=s16[:, 16:32], scalar1=1e-12, scalar2=None, op0=ALU.max),
                         R=['s16'], W=['s16'])
                    T.op('v', lambda: nc.vector.reciprocal(out=s16[:, 32:48], in_=s16[:, 16:32]), R=['s16'], W=['s16'])
                    T.op('v', lambda: nc.vector.tensor_tensor(out=vv(kk[:, :]), in0=vv(kk[:, :]),
                                                              in1=s16[:, 32:48, None].to_broadcast([128, 16, 64]), op=ALU.mult),
                         R=['kk', 's16'], W=['kk'])
                    T.op('v', lambda: nc.vector.scalar_tensor_tensor(out=t1[:], in0=a_[:], scalar=-1.0, in1=kab[:], op0=ALU.add, op1=ALU.mult),
                         R=['a', 'eprv'], W=['eprv'])
                    T.op('v', lambda: nc.vector.scalar_tensor_tensor(out=kd[:], in0=t1[:], scalar=1.0, in1=k_, op0=ALU.add, op1=ALU.mult),
                         R=['eprv', 'rkv'], W=['kd'])
                    T.op('g', lambda: nc.gpsimd.tensor_tensor(out=b_[:], in0=kk[:], in1=a_[:], op=ALU.mult), R=['kk', 'a'], W=['b'])
                    T.op('g', lambda: nc.gpsimd.tensor_tensor(out=t2[:], in0=r_, in1=rkb[:], op=ALU.mult), R=['rkv'], W=['epos'])
                    T.op('g', lambda: nc.gpsimd.tensor_tensor(out=t2[:], in0=t2[:], in1=kd[:], op=ALU.mult), R=['epos', 'kd'], W=['epos'])
                    T.op('v', lambda: nc.vector.tensor_reduce(out=bon[:], in_=vv(t2[:, :]), axis=AX.X, op=ALU.add), R=['epos'], W=['bon'])
                    T.dma('g', S['BON'][d, rows:rows + 128, :], bon[:], 'bon', R=['bon'])
                    for hf in range(2):
                        T.op('p', lambda: nc.tensor.matmul(psb[4 + hf][:, :], lhsT=tri[:, :], rhs=wsig[:, hf * 512:(hf + 1) * 512],
                                                           start=True, stop=True), R=['wsig'], W=[('ps', 4 + hf)])
                    for p in range(8):
                        T.op('p', lambda: nc.tensor.matmul(psb[6][:, p:p + 1], lhsT=wsig[:, p * 128:(p + 1) * 128], rhs=nc0[:, 0:1],
                                                           start=True, stop=True), R=['wsig'], W=[('ps', 6)])
                    T.op('a', lambda: nc.scalar.activation(out=gC[:], in_=psb[6][:, 0:8], func=AF.Exp), W=[('ps', 6), 'gC'])
                    for hf in range(2):
                        cs_ = slice(hf * 512, (hf + 1) * 512)
                        T.op('a', lambda: nc.scalar.activation(out=epos[:, cs_], in_=psb[4 + hf][:, :], func=AF.Exp), W=[('ps', 4 + hf), 'epos'])
                        T.op('a', lambda: nc.scalar.activation(out=eneg[:, cs_], in_=psb[4 + hf][:, :], func=AF.Exp, scale=-1.0),
                             W=[('ps', 4 + hf), 'eneg'])
                        T.op('v', lambda: nc.vector.scalar_tensor_tensor(out=eprv[:, cs_], in0=wsig[:, cs_], scalar=C0, in1=psb[4 + hf][:, :],
                                                                         op0=ALU.mult, op1=ALU.add), R=['wsig'], W=[('ps', 4 + hf), 'eprv'])
                    T.op('a', lambda: nc.scalar.activation(out=eprv[:], in_=eprv[:], func=AF.Exp), R=['eprv'], W=['eprv'])
                    T.op('v', lambda: nc.vector.tensor_tensor(out=epos[:], in0=epos[:], in1=r_, op=ALU.mult), R=['epos', 'rkv'], W=['epos'])
                    T.op('g', lambda: nc.gpsimd.tensor_tensor(out=eprv[:], in0=eprv[:], in1=kk[:], op=ALU.mult), R=['eprv', 'kk'], W=['eprv'])
                    T.op('v', lambda: nc.vector.tensor_tensor(out=b_[:], in0=b_[:], in1=eneg[:], op=ALU.mult), R=['b', 'eneg'], W=['b'])
                    T.op('g', lambda: nc.gpsimd.tensor_tensor(out=kd[:], in0=kd[:], in1=eneg[:], op=ALU.mult), R=['kd', 'eneg'], W=['kd'])
                    T.op('a', lambda: nc.scalar.copy(out=ZALL[:, :, 0:64], in_=vv(eprv[:, :])), R=['eprv'], W=['ZALLk'] + [('ZALL', h_) for h_ in range(16)])
                    for p in range(8):
                        pp = psb[p % 4]
                        cs_ = slice(p * 128, (p + 1) * 128)
                        for q, (src, key) in enumerate(((epos, 'epos'), (eprv, 'eprv'), (b_, 'b'), (kd, 'kd'))):
                            T.op('p', lambda: nc.tensor.transpose(out=pp[:, q * 128:(q + 1) * 128], in_=src[:, cs_], identity=K.ident[:]),
                                 R=[key], W=[('ps', p % 4)])
                        if p % 2 == 0:
                            T.op('v', lambda: nc.vector.tensor_copy(out=FT[p][:, :, :], in_=pp[:, :].rearrange("p (q t) -> p q t", q=4)),
                                 W=[('ps', p % 4), ('FT', p)])
                        else:
                            T.op('a', lambda: nc.scalar.copy(out=FT[p][:, :, :], in_=pp[:, :].rearrange("p (q t) -> p q t", q=4)),
                                 W=[('ps', p % 4), ('FT', p)])
                    for q2 in range(2):
                        grp = [(2 * q2, 0), (2 * q2 + 1, 4)]
                        for hg, bo in grp:
                            hs = [4 * hg + i for i in range(4)]
                            for i, h in enumerate(hs):
                                p = h // 2
                                sl = slice((h % 2) * 64, (h % 2) * 64 + 64)
                                rT_kpT = FT[p][sl, 0:2, :].rearrange("p a t -> p (a t)")
                                T.op('p', lambda: nc.tensor.matmul(psb[bo + i][:, 0:128], lhsT=FT[p][sl, 1, :], rhs=FT[p][sl, 2, :], start=True, stop=True),
                                     R=[('FT', p)], W=[('ps', bo + i)])
                                T.op('p', lambda: nc.tensor.matmul(psb[bo + i][:, 128:384], lhsT=FT[p][sl, 2, :], rhs=rT_kpT, start=True, stop=True),
                                     R=[('FT', p)], W=[('ps', bo + i)])
                            for i, h in enumerate(hs):
                                T.op('v', lambda: nc.vector.tensor_tensor(out=MXt[bo + i][:], in0=psb[bo + i][:, 0:384], in1=mX[:], op=ALU.mult),
                                     W=[('ps', bo + i), ('MX', bo + i)])
                                if SDT != F32:
                                    T.op('g', lambda: nc.gpsimd.tensor_copy(out=AN0[bo + i][:, :].rearrange("p (a c) -> p a c", a=2),
                                                                            in_=MXt[bo + i][:, :].rearrange("p (a c) -> p a c", a=3)[:, 0:3:2, :]),
                                         R=[('MX', bo + i)], W=[('AN0', bo + i)])
                            for i, h in enumerate(hs):
                                p = h // 2
                                sl = slice((h % 2) * 64, (h % 2) * 64 + 64)
                                rT_kpT = FT[p][sl, 0:2, :].rearrange("p a t -> p (a t)")
                                T.op('p', lambda: nc.tensor.matmul(psb[bo + i][:, 0:256], lhsT=FT[p][sl, 3, :], rhs=rT_kpT, start=True, stop=True),
                                     R=[('FT', p)], W=[('ps', bo + i)])
                            for i, h in enumerate(hs):
                                T.op('v', lambda: nc.vector.tensor_tensor(out=MYt[bo + i][:], in0=psb[bo + i][:, 0:256], in1=mY[:], op=ALU.mult),
                                     W=[('ps', bo + i), ('MY', bo + i)])
                            for i, h in enumerate(hs):
                                T.op('p', lambda: nc.tensor.matmul(psb[bo + i][:, 0:64], lhsT=MYt[bo + i][:, 128:256], rhs=v_[:, h * 64:(h + 1) * 64],
                                                                   start=True, stop=True), R=[('MY', bo + i), 'rkv'], W=[('ps', bo + i)])
                            for i, h in enumerate(hs):
                                T.op('a', lambda: nc.scalar.copy(out=ZALL[:, h, 64:128], in_=psb[bo + i][:, 0:64]), W=[('ps', bo + i), ('ZALL', h)])
                        v3 = lambda ap: ap.rearrange("p (a c) -> p a c", a=3)
                        for lv in range(7):
                            for hg, bo in grp:
                                hs = [4 * hg + i for i in range(4)]
                                for i, h in enumerate(hs):
                                    pb_ = psb[bo + i]
                                    if lv == 0:
                                        if SDT != F32:
                                            A_ap, N_ap, Akey = AN0[bo + i][:, 0:128], AN0[bo + i][:, 128:256], ('AN0', bo + i)
                                        else:
                                            A_ap, N_ap, Akey = MXt[bo + i][:, 0:128], MXt[bo + i][:, 256:384], ('MX', bo + i)
                                        T.op('p', lambda: nc.tensor.matmul(pb_[:, 128:256], lhsT=N_ap, rhs=ZALL[:, h, :], start=True, stop=True),
                                             R=[Akey, ('ZALL', h), 'ZALLk'], W=[('ps', bo + i)])
                                        T.op('p', lambda: nc.tensor.matmul(pb_[:, 0:128], lhsT=N_ap, rhs=A_ap, start=True, stop=True),
                                             R=[Akey], W=[('ps', bo + i)])
                                        T.op('p', lambda: nc.tensor.matmul(pb_[:, 256:384], lhsT=A_ap, rhs=N_ap, start=True, stop=True),
                                             R=[Akey], W=[('ps', bo + i)])
                                    else:
                                        lvt = LV[bo + i][(lv - 1) % 2]
                                        Lkey = ('LV', bo + i, (lv - 1) % 2)
                                        if lv < 6:
                                            T.op('p', lambda: nc.tensor.matmul(pb_[:, 0:256], lhsT=lvt[:, 256:384], rhs=lvt[:, 0:256], start=True, stop=True),
                                                 R=[Lkey], W=[('ps', bo + i)])
                                            T.op('p', lambda: nc.tensor.matmul(pb_[:, 256:384], lhsT=lvt[:, 0:128], rhs=lvt[:, 256:384], start=True, stop=True),
                                                 R=[Lkey], W=[('ps', bo + i)])
                                        else:
                                            T.op('p', lambda: nc.tensor.matmul(pb_[:, 128:256], lhsT=lvt[:, 256:384], rhs=lvt[:, 128:256], start=True, stop=True),
                                                 R=[Lkey], W=[('ps', bo + i)])
                            for hg, bo in grp:
                                hs = [4 * hg + i for i in range(4)]
                                for i, h in enumerate(hs):
                                    p = h // 2
                                    pb_ = psb[bo + i]
                                    if lv == 0:
                                        Xin, Xkey = ZALL[:, h, :], ('ZALL', h)
                                    else:
                                        Xin, Xkey = LV[bo + i][(lv - 1) % 2][:, 128:256], ('LV', bo + i, (lv - 1) % 2)
                                    opx = ALU.subtract if lv == 0 else ALU.add
                                    if lv < 6:
                                        lvo = LV[bo + i][lv % 2]
                                        T.op('v', lambda: nc.vector.tensor_tensor(out=lvo[:, 128:256], in0=Xin, in1=pb_[:, 128:256], op=opx),
                                             R=[Xkey, 'ZALLk'], W=[('ps', bo + i), ('LV', bo + i, lv % 2)])
                                        T.op('v', lambda: nc.vector.tensor_copy(out=v3(lvo[:, :])[:, 0:3:2, :], in_=v3(pb_[:, 0:384])[:, 0:3:2, :]),
                                             W=[('ps', bo + i), ('LV', bo + i, lv % 2)])
                                    else:
                                        T.op('v', lambda: nc.vector.tensor_tensor(out=XW[p][:, (h % 2) * 64:(h % 2) * 64 + 64], in0=Xin[:, 0:64],
                                                                                  in1=pb_[:, 128:192], op=opx),
                                             R=[Xkey], W=[('ps', bo + i), ('XW', p)])
                                        T.op('v', lambda: nc.vector.tensor_tensor(out=U0[:, h, :], in0=Xin[:, 64:128], in1=pb_[:, 192:256], op=opx),
                                             R=[Xkey], W=[('ps', bo + i), ('U0', h)])
                        for hg, bo in grp:
                            hs = [4 * hg + i for i in range(4)]
                            for pi2 in range(2):
                                p = hg * 2 + pi2
                                yb = 4 + (p % 2)
                                T.op('p', lambda: nc.tensor.transpose(out=psb[6][:, 0:128], in_=XW[p][:, :], identity=K.ident[:]),
                                     R=[('XW', p)], W=[('ps', 6)])
                                T.op('a', lambda: nc.scalar.copy(out=WT[p][:, :], in_=psb[6][:, 0:128]), W=[('ps', 6), ('WT', p)])
                                for j2 in range(2):
                                    h = 2 * p + j2
                                    i = pi2 * 2 + j2
                                    sl = slice(j2 * 64, j2 * 64 + 64)
                                    T.op('p', lambda: nc.tensor.matmul(psb[bo + i][:, 0:64], lhsT=WT[p][sl, :], rhs=STX[p][sl, sl], start=True, stop=True),
                                         R=[('WT', p), ('STX', p)], W=[('ps', bo + i)])
                                for j2 in range(2):
                                    h = 2 * p + j2
                                    i = pi2 * 2 + j2
                                    T.op('v', lambda: nc.vector.scalar_tensor_tensor(out=UN[:, h, :], in0=psb[bo + i][:, 0:64], scalar=-1.0, in1=U0[:, h, :],
                                                                                     op0=ALU.mult, op1=ALU.subtract),
                                         R=[('U0', h)], W=[('ps', bo + i), ('UN', h)])
                                for j2 in range(2):
                                    h = 2 * p + j2
                                    i = pi2 * 2 + j2
                                    sl = slice(j2 * 64, j2 * 64 + 64)
                                    yo = psb[yb][:, j2 * 64:(j2 + 1) * 64]
                                    T.op('p', lambda: nc.tensor.matmul(yo, lhsT=FT[p][sl, 0, :], rhs=STX[p][sl, sl], start=True, stop=False),
                                         R=[('FT', p), ('STX', p)], W=[('ps', yb)])
                                    T.op('p', lambda: nc.tensor.matmul(yo, lhsT=MYt[bo + i][:, 0:128], rhs=v_[:, h * 64:(h + 1) * 64], start=False, stop=False),
                                         R=[('MY', bo + i), 'rkv'], W=[('ps', yb)])
                                    T.op('p', lambda: nc.tensor.matmul(yo, lhsT=MXt[bo + i][:, 128:256], rhs=UN[:, h, :], start=False, stop=True),
                                         R=[('MX', bo + i), ('UN', h)], W=[('ps', yb)])
                                T.op('a', lambda: nc.scalar.copy(out=ysb[:, p * 128:(p + 1) * 128], in_=psb[yb][:, 0:128]),
                                     W=[('ps', yb), ('ysb', p)])
                                cs_ = slice(p * 128, (p + 1) * 128)
                                pS = psb[7]
                                T.op('p', lambda: nc.tensor.matmul(pS[:, 0:128], lhsT=kd[:, cs_], rhs=v_[:, cs_], start=True, stop=False),
                                     R=['kd', 'rkv'], W=[('ps', 7)])
                                T.op('p', lambda: nc.tensor.matmul(pS[:, 0:128], lhsT=b_[:, cs_], rhs=UN[:, 2 * p:2 * p + 2, :].rearrange("p a j -> p (a j)"),
                                                                   start=False, stop=False), R=['b', ('UN', 2 * p), ('UN', 2 * p + 1)], W=[('ps', 7)])
                                T.op('p', lambda: nc.tensor.matmul(pS[:, 0:128], lhsT=K.ident[:, :], rhs=STX[p][:, :], start=False, stop=True),
                                     R=[('STX', p)], W=[('ps', 7)])
                                T.op('v', lambda: nc.vector.tensor_scalar(out=STX[p][0:64, 0:64], in0=pS[0:64, 0:64], scalar1=gC[0:64, p:p + 1],
                                                                          scalar2=None, op0=ALU.mult), R=['gC'], W=[('ps', 7), ('STX', p)])
                                T.op('v', lambda: nc.vector.tensor_scalar(out=STX[p][64:128, 64:128], in0=pS[64:128, 64:128], scalar1=gC[64:128, p:p + 1],
                                                                          scalar2=None, op0=ALU.mult), R=['gC'], W=[('ps', 7), ('STX', p)])
                    T.dma('g', YC[rows:rows + 128, :], ysb[:], 'ysb', R=[('ysb', pp_) for pp_ in range(8)])
                if pi is not None:
                    for p in range(8):
                        T.dma('g', K.O['out_st'][pi, l, d, 2 * p], STX[p][0:64, 0:64], 'sto', R=[('STX', p)])
                        T.dma('g', K.O['out_st'][pi, l, d, 2 * p + 1], STX[p][64:128, 64:128], 'sto', R=[('STX', p)])
                    T.barrier()
            T.barrier()
    with ExitStack() as ph:
        y0 = [K.sb(ph, [128, 1024]) for _ in range(2)]
        y1 = [K.sb(ph, [128, 1024]) for _ in range(2)]
        vg = [K.sb(ph, [128, 2048]) for _ in range(2)]
        bo = [K.sb(ph, [128, 32]) for _ in range(2)]
        ysq = K.sb(ph, [128, 1024])
        gw = K.sb(ph, [128, 1024]); gbb = K.sb(ph, [128, 1024])
        sst = [K.sb(ph, [128, 96]) for _ in range(2)]
        T.dma('s', gw[:], I['rwkv_gn_w'][l:l + 1, :].to_broadcast([128, 1024]), 'c0')
        T.dma('s', gbb[:], I['rwkv_gn_b'][l:l + 1, :].to_broadcast([128, 1024]), 'c0')
        T.barrier()
        vv = lambda ap: ap.rearrange("p (h j) -> p h j", j=64)
        bc16 = lambda ap: ap[:, :, None].to_broadcast([128, 16, 64])
        for tt in range(NT // 128):
            s = tt % 2
            rows = tt * 128
            T.dma('s', y0[s][:], S['YC0'][rows:rows + 128, :], 'y0%d' % s, W=[('y0', s)])
            T.dma('s', y1[s][:], S['YC1'][rows:rows + 128, :], 'y1%d' % s, W=[('y1', s)])
            T.dma('s', vg[s][:], S['P'][rows:rows + 128, 7168:9216], 'vg%d' % s, W=[('vg', s)])
            T.dma('s', bo[s][:, 0:16], S['BON'][0, rows:rows + 128, :], 'bo%d' % s, W=[('bo', s)])
            T.dma('s', bo[s][:, 16:32], S['BON'][1, rows:rows + 128, :], 'bo%d' % s, W=[('bo', s)])
            y = y0[s]
            ss = sst[s]
            T.op('v', lambda: nc.vector.tensor_tensor(out=y[:], in0=y[:], in1=y1[s][:], op=ALU.add), R=[('y0', s), ('y1', s)], W=[('y0', s)])
            T.op('g', lambda: nc.gpsimd.tensor_tensor(out=ysq[:], in0=y[:], in1=y[:], op=ALU.mult), R=[('y0', s)], W=['ysq'])
            T.op('v', lambda: nc.vector.tensor_reduce(out=ss[:, 0:16], in_=vv(y[:, :]), axis=AX.X, op=ALU.add), R=[('y0', s)], W=[('ss', s)])
            T.op('v', lambda: nc.vector.tensor_reduce(out=ss[:, 16:32], in_=vv(ysq[:, :]), axis=AX.X, op=ALU.add), R=['ysq'], W=[('ss', s)])
            T.op('v', lambda: nc.vector.tensor_scalar(out=ss[:, 32:48], in0=ss[:, 0:16], scalar1=1.0 / 64, scalar2=None, op0=ALU.mult), R=[('ss', s)], W=[('ss', s)])
            T.op('v', lambda: nc.vector.tensor_tensor(out=ss[:, 48:64], in0=ss[:, 32:48], in1=ss[:, 32:48], op=ALU.mult), R=[('ss', s)], W=[('ss', s)])
            T.op('v', lambda: nc.vector.scalar_tensor_tensor(out=ss[:, 64:80], in0=ss[:, 16:32], scalar=1.0 / 64, in1=ss[:, 48:64],
                                                             op0=ALU.mult, op1=ALU.subtract), R=[('ss', s)], W=[('ss', s)])
            T.op('a', lambda: nc.scalar.activation(out=ss[:, 64:80], in_=ss[:, 64:80], func=AF.Sqrt, bias=GN_EPS, scale=1.0), R=[('ss', s)], W=[('ss', s)])
            T.op('v', lambda: nc.vector.reciprocal(out=ss[:, 80:96], in_=ss[:, 64:80]), R=[('ss', s)], W=[('ss', s)])
            T.op('v', lambda: nc.vector.tensor_tensor(out=vv(y[:, :]), in0=vv(y[:, :]), in1=bc16(ss[:, 32:48]), op=ALU.subtract),
                 R=[('y0', s), ('ss', s)], W=[('y0', s)])
            T.op('v', lambda: nc.vector.tensor_tensor(out=vv(y[:, :]), in0=vv(y[:, :]), in1=bc16(ss[:, 80:96]), op=ALU.mult),
                 R=[('y0', s), ('ss', s)], W=[('y0', s)])
            T.op('g', lambda: nc.gpsimd.tensor_tensor(out=y[:], in0=y[:], in1=gw[:], op=ALU.mult), R=[('y0', s)], W=[('y0', s)])
            T.op('g', lambda: nc.gpsimd.tensor_tensor(out=y[:], in0=y[:], in1=gbb[:], op=ALU.add), R=[('y0', s)], W=[('y0', s)])
            T.op('v', lambda: nc.vector.tensor_tensor(out=bo[s][:, 0:16], in0=bo[s][:, 0:16], in1=bo[s][:, 16:32], op=ALU.add), R=[('bo', s)], W=[('bo', s)])
            T.op('v', lambda: nc.vector.tensor_tensor(out=vv(ysq[:, :]), in0=vv(vg[s][:, 0:1024]), in1=bc16(bo[s][:, 0:16]), op=ALU.mult),
                 R=[('vg', s), ('bo', s)], W=['ysq'])
            T.op('v', lambda: nc.vector.tensor_tensor(out=y[:], in0=y[:], in1=ysq[:], op=ALU.add), R=[('y0', s), 'ysq'], W=[('y0', s)])
            T.op('a', lambda: nc.scalar.activation(out=vg[s][:, 1024:2048], in_=vg[s][:, 1024:2048], func=AF.Silu), R=[('vg', s)], W=[('vg', s)])
            T.op('v', lambda: nc.vector.tensor_tensor(out=y[:], in0=y[:], in1=vg[s][:, 1024:2048], op=ALU.mult), R=[('y0', s), ('vg', s)], W=[('y0', s)])
            T.dma('g', S['ACTC'][rows:rows + 128, :], y[:], 'yo%d' % s, R=[('y0', s)])
        T.barrier()


def phase_E(K, l):
    nc, T, I, S = K.nc, K.T, K.I, K.S
    NT, Ls = K.NT, K.cfg['Ls']
    TB = 256
    last = (l == DEPTH - 1)
    with ExitStack() as ph:
        ain = [K.sb(ph, [128, 1024]) for _ in range(2)]
        aT = [K.sb(ph, [128, 8, TB], BF16) for _ in range(3)]
        mT = K.sb(ph, [128, KT, TB], BF16)
        wbr = [K.sb(ph, [128, 8, 512], BF16) for _ in range(3)]
        wo = [K.sb(ph, [128, KT, 512], BF16) for _ in range(2)]
        gt = [K.sb(ph, [128, TB]) for _ in range(3)]
        mm = K.sb(ph, [128, TB])
        mt = K.sb(ph, [128, TB])
        xt = [K.sb(ph, [128, D]) for _ in range(2)]
        tmpo = K.sb(ph, [128, 512])
        sq = K.sb(ph, [128, D])
        st = K.sb(ph, [128, 4])
        fnw = K.sb(ph, [128, D])
        pst = [K.pb(ph) for _ in range(2)]
        psy = [K.pb(ph) for _ in range(3)]
        pso = [K.pb(ph) for _ in range(2)]
        T.dma('s', fnw[:], I['final_norm_w'][0:1, :].to_broadcast([128, D]), 'c0')
        GATE = [K.sb(ph, [128, D]) for _ in range(2)]
        for g in range(2):
            T.dma('s', GATE[g][:], S['ADAd'][g][:, 2 * D:3 * D], 'c0')
        T.barrier()
        acts = [S['ACTA'], S['ACTB'], S['ACTC']]
        wsrc = [S['Wa'][l].rearrange("(kt p) c -> p kt c", p=128), S['Wbb'][l].rearrange("(kt p) c -> p kt c", p=128),
                S['Wc'][l].rearrange("(kt p) c -> p kt c", p=128)]
        wov = S['Wo'][l].rearrange("(kt p) c -> p kt c", p=128)
        na = 0
        nt = 0
        no = 0
        for tb in range(NT // TB):
            g = 0 if tb * TB < Ls else 1
            gate_g = GATE[g][:, :]
            for br in range(3):
                for sub in range(TB // 128):
                    rows = tb * TB + sub * 128
                    s = na % 2
                    na += 1
                    T.dma('s', ain[s][:], acts[br][rows:rows + 128, :], 'ain%d' % s, W=[('ain', s)])
                    for q4 in range(2):
                        pi = nt % 2
                        nt += 1
                        for j in range(4):
                            kt = q4 * 4 + j
                            T.op('p', lambda: nc.tensor.transpose(out=pst[pi][:, j * 128:(j + 1) * 128], in_=ain[s][:, kt * 128:(kt + 1) * 128],
                                                                  identity=K.ident[:]), R=[('ain', s)], W=[('pst', pi)])
                        dst = aT[br][:, q4 * 4:q4 * 4 + 4, sub * 128:(sub + 1) * 128]
                        src = pst[pi][:, :].rearrange("p (j t) -> p j t", j=4)
                        if pi == 0:
                            T.op('a', lambda: nc.scalar.copy(out=dst, in_=src), R=[('pst', pi)], W=[('aT', br)])
                        else:
                            T.op('v', lambda: nc.vector.tensor_copy(out=dst, in_=src), R=[('pst', pi)], W=[('aT', br)])
            for cg in range(4):
                for br in range(3):
                    T.dma('s', wbr[br][:], wsrc[br][:, :, cg * 512:(cg + 1) * 512], 'wbr%d' % br, W=[('wbr', br)])
                for c4 in range(4):
                    ct = cg * 4 + c4
                    for br in range(3):
                        T.dma('s', gt[br][:], S['PT'][2304 + br * D + ct * 128:2304 + br * D + (ct + 1) * 128, tb * TB:(tb + 1) * TB],
                              'gt%d' % br, W=[('gt', br)])
                        T.op('a', lambda: nc.scalar.activation(out=gt[br][:], in_=gt[br][:], func=AF.Sigmoid), R=[('gt', br)], W=[('gt', br)])
                        for kt in range(8):
                            T.op('p', lambda: nc.tensor.matmul(psy[br][:, 0:TB], lhsT=wbr[br][:, kt, c4 * 128:(c4 + 1) * 128], rhs=aT[br][:, kt, :],
                                                               start=(kt == 0), stop=(kt == 7)), R=[('wbr', br), ('aT', br)], W=[('psy', br)])
                    T.op('v', lambda: nc.vector.tensor_tensor(out=mm[:], in0=psy[0][:, 0:TB], in1=gt[0][:], op=ALU.mult), R=[('psy', 0), ('gt', 0)], W=['mm'])
                    T.op('v', lambda: nc.vector.tensor_tensor(out=mt[:], in0=psy[1][:, 0:TB], in1=gt[1][:], op=ALU.mult), R=[('psy', 1), ('gt', 1)], W=['mt'])
                    T.op('g', lambda: nc.gpsimd.tensor_tensor(out=mm[:], in0=mm[:], in1=mt[:], op=ALU.add), R=['mm', 'mt'], W=['mm'])
                    T.op('v', lambda: nc.vector.tensor_tensor(out=mt[:], in0=psy[2][:, 0:TB], in1=gt[2][:], op=ALU.mult), R=[('psy', 2), ('gt', 2)], W=['mt'])
                    T.op('v', lambda: nc.vector.tensor_tensor(out=mT[:, ct, :], in0=mm[:], in1=mt[:], op=ALU.add), R=['mm', 'mt'], W=['mT'])
            for sub in range(TB // 128):
                rows = tb * TB + sub * 128
                T.dma('s', xt[sub][:], S['X'][rows:rows + 128, :], 'xt%d' % sub, W=[('xt', sub)])
            for cb in range(4):
                ws = no % 2
                no += 1
                T.dma('s', wo[ws][:], wov[:, :, cb * 512:(cb + 1) * 512], 'wo%d' % ws, W=[('wo', ws)])
                for sub in range(TB // 128):
                    pi = (cb * 2 + sub) % 2
                    for kt in range(KT):
                        T.op('p', lambda: nc.tensor.matmul(pso[pi][:, :], lhsT=mT[:, kt, sub * 128:(sub + 1) * 128], rhs=wo[ws][:, kt, :],
                                                           start=(kt == 0), stop=(kt == KT - 1)), R=['mT', ('wo', ws)], W=[('pso', pi)])
                    cs_ = slice(cb * 512, (cb + 1) * 512)
                    T.op('v', lambda: nc.vector.tensor_tensor(out=tmpo[:], in0=pso[pi][:, :], in1=gate_g[:, cs_], op=ALU.mult),
                         R=[('pso', pi)], W=['tmpo'])
                    T.op('v', lambda: nc.vector.tensor_tensor(out=xt[sub][:, cs_], in0=xt[sub][:, cs_], in1=tmpo[:], op=ALU.add),
                         R=['tmpo', ('xt', sub)], W=[('xt', sub)])
            for sub in range(TB // 128):
                rows = tb * TB + sub * 128
                if not last:
                    T.dma('g', S['X'][rows:rows + 128, :], xt[sub][:], 'xo%d' % sub, R=[('xt', sub)])
                else:
                    T.op('v', lambda: nc.vector.tensor_tensor(out=sq[:], in0=xt[sub][:], in1=xt[sub][:], op=ALU.mult), R=[('xt', sub)], W=['sq'])
                    T.op('v', lambda: nc.vector.tensor_reduce(out=st[:, 0:1], in_=sq[:], axis=AX.X, op=ALU.add), R=['sq'], W=['st'])
                    T.op('a', lambda: nc.scalar.activation(out=st[:, 1:2], in_=st[:, 0:1], func=AF.Sqrt, bias=EPS, scale=1.0 / D), R=['st'], W=['st'])
                    T.op('v', lambda: nc.vector.reciprocal(out=st[:, 2:3], in_=st[:, 1:2]), R=['st'], W=['st'])
                    T.op('v', lambda: nc.vector.scalar_tensor_tensor(out=xt[sub][:], in0=xt[sub][:], scalar=st[:, 2:3], in1=fnw[:],
                                                                     op0=ALU.mult, op1=ALU.mult), R=[('xt', sub), 'st'], W=[('xt', sub)])
                    T.dma('g', K.O['y_all'][rows:rows + 128, :], xt[sub][:], 'xo%d' % sub, R=[('xt', sub)])
        T.barrier()


def host_consts(cfg):
    Ls = cfg['Ls']
    GRID_W = 64
    t = np.arange(Ls)
    row = (t // GRID_W).astype(np.float32)
    col = (t % GRID_W).astype(np.float32)
    inv = (10000.0 ** (-np.arange(16, dtype=np.float32) / 16)).astype(np.float32)
    ang_r = row[:, None] * inv[None, :]
    ang_c = col[:, None] * inv[None, :]
    cos4 = np.stack([np.cos(ang_r), np.cos(ang_c), np.cos(ang_r), np.cos(ang_c)], axis=1)
    sin4 = np.stack([np.sin(ang_r), np.sin(ang_c), np.sin(ang_r), np.sin(ang_c)], axis=1)
    ropetab = np.concatenate([cos4.reshape(Ls, 64), sin4.reshape(Ls, 64)], axis=1).astype(np.float32)
    i = np.arange(128)
    P_, F_ = i[:, None], i[None, :]
    SL = (F_ < P_).astype(np.float32)
    SU = (F_ > P_).astype(np.float32)
    LI = (F_ <= P_).astype(np.float32)
    UI = (F_ >= P_).astype(np.float32)
    maskX = np.stack([np.concatenate([SL, UI, SU], 1), np.concatenate([SU, LI, SL], 1)])
    maskY = np.stack([np.concatenate([UI, SU], 1), np.concatenate([LI, SL], 1)])
    tri = np.stack([-C0 * UI, -C0 * LI]).astype(np.float32)
    return dict(ident=np.eye(128, dtype=np.float32), ropetab=ropetab, maskX=maskX.astype(np.float32),
                maskY=maskY.astype(np.float32), tri=tri)


_CACHE = {}


def run(cfg, inp):
    Ls, Lp, NP, PAST = cfg['Ls'], cfg['Lp'], cfg['NP'], cfg['PAST']
    key = (Ls, Lp, NP, PAST, cfg.get('scan_bf16', False), cfg.get('stop', 'Z'), cfg.get('nl', DEPTH), tuple(cfg.get('taps', ())), cfg.get('debug', False))
    if key not in _CACHE:
        _CACHE[key] = build(cfg)
    nc = _CACHE[key]
    f = lambda a: np.ascontiguousarray(np.asarray(a, dtype=np.float32))
    hc = host_consts(cfg)
    shared = dict(
        w_ada=f(inp['w_ada']), b_ada=f(inp['b_ada']), norm_w=f(inp['norm_w']), w_in=f(inp['w_in']),
        lam4=f(np.concatenate([inp['lambda_q1'], inp['lambda_k1'], inp['lambda_q2'], inp['lambda_k2']], axis=1)),
        subln_w=f(inp['subln_w']),
        conv_wT=f(np.transpose(inp['conv_w'], (0, 2, 1))),
        conv_bT=f(np.transpose(np.asarray(inp['conv_b']).reshape(DEPTH, 8, 128), (0, 2, 1))),
        conv_ln_w=f(inp['conv_ln_w']), conv_ln_b=f(inp['conv_ln_b']),
        rwkv_w0=f(inp['rwkv_w0']), rwkv_w_up=f(inp['rwkv_w_up']), rwkv_a0=f(inp['rwkv_a0']), rwkv_a_up=f(inp['rwkv_a_up']),
        rwkv_k_k=f(inp['rwkv_k_k']), rwkv_k_a=f(inp['rwkv_k_a']), rwkv_r_k=f(np.asarray(inp['rwkv_r_k']).reshape(DEPTH, 1024)),
        rwkv_gn_w=f(inp['rwkv_gn_w']), rwkv_gn_b=f(inp['rwkv_gn_b']),
        w_br_a=f(inp['w_br_a']), w_br_b=f(inp['w_br_b']), w_br_c=f(inp['w_br_c']), w_out=f(inp['w_out']),
        final_norm_w=f(np.asarray(inp['final_norm_w']).reshape(1, D)),
        **hc)
    xs, xp = np.asarray(inp['x_sample']), np.asarray(inp['x_prompt'])
    in_maps = []
    ncores = cfg.get('ncores', 8)
    for c in range(ncores):
        b = c // 4
        m = dict(shared)
        m['x_all'] = f(np.concatenate([xs[b], xp[c * NP:(c + 1) * NP].reshape(NP * Lp, D)], axis=0))
        cv2 = np.stack([np.asarray(inp['c'])[b], np.asarray(inp['c_ctx'])], axis=0)
        m['cvecT'] = f(np.transpose(cv2.reshape(2, KT, 128), (2, 0, 1)).reshape(128, 32))
        m['ck'] = f(np.asarray(inp['cache_k'])[b].reshape(DEPTH, PAST, 1024))
        m['cv'] = f(np.asarray(inp['cache_v'])[b].reshape(DEPTH, PAST, 1024))
        m['st0T'] = f(np.transpose(np.asarray(inp['state_rwkv'])[b], (0, 1, 2, 4, 3)))
        in_maps.append(m)
    res = run_bass_kernel_spmd(nc, in_maps, core_ids=list(range(ncores)))
    R = res.results
    if cfg.get('raw'):
        return R
    nb = xs.shape[0]
    y_sample = np.stack([R[4 * b]['y_all'][:Ls] for b in range(nb)], axis=0).astype(np.float32)
    y_prompt = np.concatenate([R[c]['y_all'][Ls:].reshape(NP, Lp, D) for c in range(8)], axis=0).astype(np.float32)
    nk = np.concatenate([R[c]['out_k'] for c in range(8)], axis=0).reshape(8 * NP, DEPTH, Lp, 8, 2, 64).astype(np.float32)
    nv = np.concatenate([R[c]['out_v'] for c in range(8)], axis=0).reshape(8 * NP, DEPTH, Lp, 8, 128).astype(np.float32)
    ns = np.concatenate([R[c]['out_st'] for c in range(8)], axis=0)
    ns = np.ascontiguousarray(np.transpose(ns, (0, 1, 2, 3, 5, 4))).astype(np.float32)
    return (y_prompt, y_sample, nk, nv, ns)


def kernel(**inputs):
    cfg = dict(Ls=4096, Lp=256, NP=4, PAST=512)
    return run(cfg, inputs)
```

```python
import math
from contextlib import ExitStack
import numpy as np
import concourse.bass as bass
import concourse.mybir as mybir
from concourse.bass_utils import run_bass_kernel_spmd

F32 = mybir.dt.float32
BF16 = mybir.dt.bfloat16
AF = mybir.ActivationFunctionType
ALU = mybir.AluOpType
AX = mybir.AxisListType

D = 2048
KT = 16
INC = 17664
DEPTH = 2
C0 = math.exp(-0.5)
EPS = 1e-6
LN_EPS = 1e-5
GN_EPS = 64e-5
NPC = 9216
NPT = 8448


class Trk:
    def __init__(s, nc, es):
        s.nc = nc
        s.es = es
        s.eng = {'p': nc.tensor, 'v': nc.vector, 'a': nc.scalar, 'g': nc.gpsimd, 's': nc.sync}
        s.sem = {}
        s.cnt = {}
        s.waited = {}
        s.lastw = {}
        s.readers = {}
        s.nins = 0

    def _sem(s, key):
        if key not in s.sem:
            s.sem[key] = s.es.enter_context(s.nc.semaphore("sm%d" % len(s.sem)))
            s.cnt[key] = 0
        return s.sem[key]

    def _deps(s, R, W):
        d = {}
        for b in R:
            t = s.lastw.get(b)
            if t is not None and d.get(t[0], 0) < t[1]:
                d[t[0]] = t[1]
        for b in W:
            t = s.lastw.get(b)
            if t is not None and d.get(t[0], 0) < t[1]:
                d[t[0]] = t[1]
            rd = s.readers.get(b)
            if rd:
                for k, v in rd.items():
                    if d.get(k, 0) < v:
                        d[k] = v
        return d

    def _wait(s, e, d, skip_self=False):
        for k, v in d.items():
            if skip_self and k == e:
                continue
            if s.waited.get((e, k), 0) < v:
                s.eng[e].wait_ge(s._sem(k), v)
                s.waited[(e, k)] = v
                s.nins += 1

    def _record(s, tok, R, W):
        for b in W:
            s.lastw[b] = tok
            s.readers[b] = {}
        for b in R:
            rd = s.readers.setdefault(b, {})
            if rd.get(tok[0], 0) < tok[1]:
                rd[tok[0]] = tok[1]

    def op(s, e, fn, R=(), W=()):
        s._wait(e, s._deps(R, W), skip_self=(e == 'p'))
        ins = fn()
        sem = s._sem(e)
        s.cnt[e] += 1
        ins.then_inc(sem, 1)
        s._record((e, s.cnt[e]), R, W)
        s.nins += 1

    def dma(s, q, out, in_, stream, R=(), W=()):
        s._wait(q, s._deps(R, W))
        sem = s._sem(stream)
        s.cnt[stream] += 16
        s.eng[q].dma_start(out=out, in_=in_).then_inc(sem, 16)
        s._record((stream, s.cnt[stream]), R, W)
        s.nins += 1

    def barrier(s):
        keys = list(s.cnt.keys())
        for e in ('p', 'v', 'a', 'g', 's'):
            for k in keys:
                v = s.cnt[k]
                if v > 0 and s.waited.get((e, k), 0) < v:
                    s.eng[e].wait_ge(s._sem(k), v)
                    s.waited[(e, k)] = v
        s.lastw = {}
        s.readers = {}


class Ctx:
    pass


def build(cfg):
    Ls, Lp, NP, PAST = cfg['Ls'], cfg['Lp'], cfg['NP'], cfg['PAST']
    NT = Ls + NP * Lp
    assert Ls % 512 == 0 and (NP * Lp) % 512 == 0 and Lp % 256 == 0 and PAST % 128 == 0
    nc = bass.Bass("TRN2", target_bir_lowering=False)
    K = Ctx()
    K.nc = nc
    K.cfg = cfg
    K.NT = NT

    def din(name, shape, dt=F32):
        return nc.dram_tensor(name, list(shape), dt, kind="ExternalInput").ap()

    def dout(name, shape):
        return nc.dram_tensor(name, list(shape), F32, kind="ExternalOutput").ap()

    def dscr(name, shape, dt=F32):
        return nc.dram_tensor(name, list(shape), dt, kind="Internal").ap()

    I = {}
    I['x_all'] = din('x_all', [NT, D])
    I['cvecT'] = din('cvecT', [128, 32])
    I['ck'] = din('ck', [DEPTH, PAST, 1024])
    I['cv'] = din('cv', [DEPTH, PAST, 1024])
    I['st0T'] = din('st0T', [DEPTH, 2, 16, 64, 64])
    I['w_ada'] = din('w_ada', [DEPTH, D, 3 * D])
    I['b_ada'] = din('b_ada', [DEPTH, 3 * D])
    I['norm_w'] = din('norm_w', [DEPTH, D])
    I['w_in'] = din('w_in', [DEPTH, D, INC])
    I['lam4'] = din('lam4', [DEPTH, 256])
    I['subln_w'] = din('subln_w', [DEPTH, 128])
    I['conv_wT'] = din('conv_wT', [DEPTH, 1024, 31])
    I['conv_bT'] = din('conv_bT', [DEPTH, 128, 8])
    I['conv_ln_w'] = din('conv_ln_w', [DEPTH, 1024])
    I['conv_ln_b'] = din('conv_ln_b', [DEPTH, 1024])
    I['rwkv_w0'] = din('rwkv_w0', [DEPTH, 2, 1024])
    I['rwkv_w_up'] = din('rwkv_w_up', [DEPTH, 2, 64, 1024])
    I['rwkv_a0'] = din('rwkv_a0', [DEPTH, 2, 1024])
    I['rwkv_a_up'] = din('rwkv_a_up', [DEPTH, 2, 64, 1024])
    I['rwkv_k_k'] = din('rwkv_k_k', [DEPTH, 1024])
    I['rwkv_k_a'] = din('rwkv_k_a', [DEPTH, 1024])
    I['rwkv_r_k'] = din('rwkv_r_k', [DEPTH, 1024])
    I['rwkv_gn_w'] = din('rwkv_gn_w', [DEPTH, 1024])
    I['rwkv_gn_b'] = din('rwkv_gn_b', [DEPTH, 1024])
    I['w_br_a'] = din('w_br_a', [DEPTH, 1024, D])
    I['w_br_b'] = din('w_br_b', [DEPTH, 1024, D])
    I['w_br_c'] = din('w_br_c', [DEPTH, 1024, D])
    I['w_out'] = din('w_out', [DEPTH, D, D])
    I['final_norm_w'] = din('final_norm_w', [1, D])
    I['ident'] = din('ident', [128, 128])
    I['ropetab'] = din('ropetab', [Ls, 128])
    I['maskX'] = din('maskX', [2, 128, 384])
    I['maskY'] = din('maskY', [2, 128, 256])
    I['tri'] = din('tri', [2, 128, 128])
    K.I = I
    O = {}
    O['y_all'] = dout('y_all', [NT, D])
    O['out_k'] = dout('out_k', [NP, DEPTH, Lp, 1024])
    O['out_v'] = dout('out_v', [NP, DEPTH, Lp, 1024])
    O['out_st'] = dout('out_st', [NP, DEPTH, 2, 16, 64, 64])
    K.O = O
    S = {}
    S['X'] = dscr('X', [NT, D])
    S['P'] = dscr('P', [NT, NPC])
    S['PT'] = dscr('PT', [NPT, NT])
    S['Wb'] = dscr('Wb', [DEPTH, D, INC], BF16)
    S['Wa'] = dscr('Wba', [DEPTH, 1024, D], BF16)
    S['Wbb'] = dscr('Wbb', [DEPTH, 1024, D], BF16)
    S['Wc'] = dscr('Wbc', [DEPTH, 1024, D], BF16)
    S['Wo'] = dscr('Wbo', [DEPTH, D, D], BF16)
    S['ACTA'] = dscr('ACTA', [NT, 1024])
    S['ACTB'] = dscr('ACTB', [NT, 1024])
    S['ACTC'] = dscr('ACTC', [NT, 1024])
    S['ZC'] = dscr('ZC', [NT, 1024])
    S['YC0'] = dscr('YC0', [NT, 1024])
    S['YC1'] = dscr('YC1', [NT, 1024])
    S['BON'] = dscr('BON', [2, NT, 16])
    S['ADAd'] = dscr('ADAd', [2, 128, 3 * D])
    if cfg.get('debug'):
        S['Hdbg'] = dscr('Hdbg', [NT, D])
        S['HTdbg'] = dscr('HTdbg', [NT // 512, 128, KT * 512], BF16)
    K.S = S

    seqs = [dict(r0=0, L=Ls, ctx=PAST, rope=True, g=0, pi=None)]
    for i in range(NP):
        seqs.append(dict(r0=Ls + i * Lp, L=Lp, ctx=0, rope=False, g=1, pi=i))
    K.seqs = seqs

    with ExitStack() as es:
        T = Trk(nc, es)
        K.T = T
        K.uid = 0

        def sb(stack, shape, dt=F32, nm="t"):
            K.uid += 1
            return stack.enter_context(nc.sbuf_tensor("%s%d" % (nm, K.uid), list(shape), dt))

        def pb(stack, shape=(128, 512), dt=F32, nm="ps"):
            K.uid += 1
            return stack.enter_context(nc.psum_tensor("%s%d" % (nm, K.uid), list(shape), dt))

        K.sb = sb
        K.pb = pb
        K.ident = sb(es, [128, 128])
        T.dma('s', K.ident[:], I['ident'][:, :], 'c0')
        T.barrier()

        ORD = '0aABCDEZ'
        stop = ORD.index(cfg.get('stop', 'Z'))
        nl = cfg.get('nl', DEPTH)
        phase_convert(K)
        T.dma('s', S['X'][:, :], I['x_all'][:, :], 'c0')
        T.barrier()
        for l in range(nl):
            if stop >= ORD.index('a'):
                phase_ada(K, l)
            if stop >= ORD.index('A'):
                phase_A(K, l)
            if stop >= ORD.index('B'):
                phase_B(K, l)
            if stop >= ORD.index('C'):
                phase_C(K, l)
            if stop >= ORD.index('D'):
                phase_D(K, l)
            if stop >= ORD.index('E'):
                phase_E(K, l)
        T.barrier()
        for nm in cfg.get('taps', ()):
            src = S[nm]
            shp = list(src.shape)
            dst = nc.dram_tensor('tap_' + nm, shp, src.dtype, kind="ExternalOutput").ap()
            T.dma('s', dst, src, 'c0')
        T.barrier()
        print("instructions:", T.nins, "sems:", len(T.sem))
    return nc


def phase_convert(K):
    nc, T, I, S = K.nc, K.T, K.I, K.S
    with ExitStack() as ph:
        NS = 4
        CW = 2048
        fin = [K.sb(ph, [128, CW]) for _ in range(NS)]
        fout = [K.sb(ph, [128, CW], BF16) for _ in range(NS)]
        engs = ['v', 'a', 'g', 'v']
        cnt = [0]

        def conv(src, dst, Rr, Cc):
            for r0 in range(0, Rr, 128):
                for c0 in range(0, Cc, CW):
                    cw = min(CW, Cc - c0)
                    i = cnt[0]
                    s = i % NS
                    cnt[0] += 1
                    T.dma('s', fin[s][:, :cw], src[r0:r0 + 128, c0:c0 + cw], 'cvi%d' % s, W=[('cvi', s)])
                    e = engs[s]
                    if e == 'a':
                        T.op('a', lambda: nc.scalar.copy(out=fout[s][:, :cw], in_=fin[s][:, :cw]),
                             R=[('cvi', s)], W=[('cvo', s)])
                    elif e == 'v':
                        T.op('v', lambda: nc.vector.tensor_copy(out=fout[s][:, :cw], in_=fin[s][:, :cw]),
                             R=[('cvi', s)], W=[('cvo', s)])
                    else:
                        T.op('g', lambda: nc.gpsimd.tensor_copy(out=fout[s][:, :cw], in_=fin[s][:, :cw]),
                             R=[('cvi', s)], W=[('cvo', s)])
                    T.dma('g', dst[r0:r0 + 128, c0:c0 + cw], fout[s][:, :cw], 'cvo%d' % s, R=[('cvo', s)])

        for l in range(DEPTH):
            conv(I['w_in'][l], S['Wb'][l], D, INC)
            conv(I['w_br_a'][l], S['Wa'][l], 1024, D)
            conv(I['w_br_b'][l], S['Wbb'][l], 1024, D)
            conv(I['w_br_c'][l], S['Wc'][l], 1024, D)
            conv(I['w_out'][l], S['Wo'][l], D, D)
        T.barrier()


def phase_ada(K, l):
    nc, T, I = K.nc, K.T, K.I
    with ExitStack() as ph:
        cvt = K.sb(ph, [128, 32])
        CL = K.sb(ph, [128, 32, 128])
        ones1 = K.sb(ph, [1, 128])
        wa = [K.sb(ph, [128, KT, 512]) for _ in range(2)]
        bb = [K.sb(ph, [1, 512]) for _ in range(2)]
        nwb = K.sb(ph, [128, D])
        ADA = [K.sb(ph, [128, 3 * D]) for _ in range(2)]
        ps = [K.pb(ph) for _ in range(2)]
        T.dma('s', cvt[:], I['cvecT'][:, :], 'c0', W=['cvt'])
        T.dma('s', nwb[:], I['norm_w'][l:l + 1, :].to_broadcast([128, D]), 'c0', W=['nwb'])
        T.barrier()
        T.op('a', lambda: nc.scalar.activation(out=cvt[:], in_=cvt[:], func=AF.Silu), R=['cvt'], W=['cvt'])
        T.op('v', lambda: nc.vector.tensor_copy(out=CL[:], in_=cvt[:, :, None].to_broadcast([128, 32, 128])),
             R=['cvt'], W=['CL'])
        T.op('v', lambda: nc.vector.memset(ones1[:], 1.0), W=['ones1'])
        wv = I['w_ada'][l].rearrange("(kt p) c -> p kt c", p=128)
        n = 0
        for cb in range(12):
            s = cb % 2
            T.dma('s', bb[s][:], I['b_ada'][l:l + 1, cb * 512:(cb + 1) * 512], 'wa%d' % s, W=[('bb', s)])
            T.dma('s', wa[s][:], wv[:, :, cb * 512:(cb + 1) * 512], 'wa%d' % s, W=[('wa', s), ('bb', s)])
            for g in range(2):
                p_ = ps[n % 2]
                for kt in range(KT):
                    T.op('p', lambda: nc.tensor.matmul(p_[:, :], lhsT=CL[:, g * 16 + kt, :], rhs=wa[s][:, kt, :],
                                                       start=(kt == 0), stop=False),
                         R=['CL', ('wa', s)], W=[('ps', n % 2)])
                T.op('p', lambda: nc.tensor.matmul(p_[:, :], lhsT=ones1[0:1, :], rhs=bb[s][0:1, :],
                                                   start=False, stop=True),
                     R=['ones1', ('bb', s)], W=[('ps', n % 2)])
                dst = ADA[g][:, cb * 512:(cb + 1) * 512]
                if n % 2 == 0:
                    T.op('v', lambda: nc.vector.tensor_copy(out=dst, in_=p_[:, :]), R=[('ps', n % 2)], W=[('ADA', g, cb)])
                else:
                    T.op('a', lambda: nc.scalar.copy(out=dst, in_=p_[:, :]), R=[('ps', n % 2)], W=[('ADA', g, cb)])
                n += 1
        T.barrier()
        for g in range(2):
            T.op('v', lambda: nc.vector.scalar_tensor_tensor(out=ADA[g][:, D:2 * D], in0=ADA[g][:, D:2 * D], scalar=1.0,
                                                             in1=nwb[:], op0=ALU.add, op1=ALU.mult))
        T.barrier()
        for g in range(2):
            for j in range(3):
                T.dma('s', K.S['ADAd'][g][:, j * D:(j + 1) * D], ADA[g][:, j * D:(j + 1) * D], 'c0')
        T.barrier()
        if K.cfg.get('debug') and l == 0:
            dbg = nc.dram_tensor('dbg_ada', [2, 128, 3 * D], F32, kind="ExternalOutput").ap()
            for g in range(2):
                for j in range(3):
                    T.dma('s', dbg[g][:, j * D:(j + 1) * D], ADA[g][:, j * D:(j + 1) * D], 'c0')
            dbg2 = nc.dram_tensor('dbg_cl', [128, 32 * 128], F32, kind="ExternalOutput").ap()
            for j in range(2):
                T.dma('s', dbg2[:, j * 2048:(j + 1) * 2048], CL[:, j * 16:(j + 1) * 16, :].rearrange("p a b -> p (a b)"), 'c0')
            dbg3 = nc.dram_tensor('dbg_cvt', [128, 32], F32, kind="ExternalOutput").ap()
            T.dma('s', dbg3, cvt[:], 'c0')
            T.barrier()


def a_blocks():
    blks = []
    for i in range(8):
        blks.append(('TM', i * 512, 512, i * 512))
    for i in range(4):
        blks.append(('FM', 4096 + i * 512, 512, i * 512))
    for i in range(10):
        blks.append(('TM', 6144 + i * 512, 512, 4096 + i * 512))
    blks.append(('FM', 11264, 256, 2048))
    for i in range(12):
        blks.append(('FM', 11520 + i * 512, 512, 2304 + i * 512))
    return blks


def phase_A(K, l):
    nc, T, I, S = K.nc, K.T, K.I, K.S
    NT, Ls = K.NT, K.cfg['Ls']
    with ExitStack() as ph:
        xin = [K.sb(ph, [128, D]) for _ in range(2)]
        hh = [K.sb(ph, [128, D]) for _ in range(2)]
        sq = K.sb(ph, [128, D])
        st = [K.sb(ph, [128, 4]) for _ in range(2)]
        hT = K.sb(ph, [128, KT, 512], BF16)
        wblk = [K.sb(ph, [128, KT, 512], BF16) for _ in range(2)]
        stg = [K.sb(ph, [128, 512]) for _ in range(4)]
        pst = [K.pb(ph) for _ in range(2)]
        psm = [K.pb(ph) for _ in range(4)]
        wv = S['Wb'][l].rearrange("(kt p) c -> p kt c", p=128)
        ADA = [K.sb(ph, [128, 2 * D]) for _ in range(2)]
        for g in range(2):
            for j in range(2):
                T.dma('s', ADA[g][:, j * D:(j + 1) * D], S['ADAd'][g][:, j * D:(j + 1) * D], 'c0')
        T.barrier()
        blks = a_blocks()
        nt = 0
        nm = 0
        nw = 0
        for tb in range(NT // 512):
            g = 0 if tb * 512 < Ls else 1
            A_g = ADA[g][:, D:2 * D]
            sh_g = ADA[g][:, 0:D]
            for sub in range(4):
                r0 = tb * 512 + sub * 128
                s = sub % 2
                T.dma('s', xin[s][:], S['X'][r0:r0 + 128, :], 'xin%d' % s, W=[('xin', s)])
                T.op('v', lambda: nc.vector.tensor_tensor(out=sq[:], in0=xin[s][:], in1=xin[s][:], op=ALU.mult),
                     R=[('xin', s)], W=['sq'])
                T.op('v', lambda: nc.vector.tensor_reduce(out=st[s][:, 0:1], in_=sq[:], axis=AX.X, op=ALU.add),
                     R=['sq'], W=[('st', s)])
                T.op('a', lambda: nc.scalar.activation(out=st[s][:, 1:2], in_=st[s][:, 0:1], func=AF.Sqrt,
                                                       bias=EPS, scale=1.0 / D), R=[('st', s)], W=[('st', s)])
                T.op('v', lambda: nc.vector.reciprocal(out=st[s][:, 2:3], in_=st[s][:, 1:2]), R=[('st', s)], W=[('st', s)])
                T.op('v', lambda: nc.vector.scalar_tensor_tensor(out=hh[s][:], in0=xin[s][:], scalar=st[s][:, 2:3],
                                                                 in1=A_g, op0=ALU.mult, op1=ALU.mult),
                     R=[('xin', s), ('st', s)], W=[('hh', s)])
                T.op('v', lambda: nc.vector.tensor_tensor(out=hh[s][:], in0=hh[s][:], in1=sh_g, op=ALU.add),
                     R=[('hh', s)], W=[('hh', s)])
                if K.cfg.get('debug'):
                    T.dma('g', S['Hdbg'][r0:r0 + 128, :], hh[s][:], 'dbgh', R=[('hh', s)])
                    T.barrier()
                for q4 in range(4):
                    p_ = pst[nt % 2]
                    for j in range(4):
                        kt = q4 * 4 + j
                        T.op('p', lambda: nc.tensor.transpose(out=p_[:, j * 128:(j + 1) * 128],
                                                              in_=hh[s][:, kt * 128:(kt + 1) * 128], identity=K.ident[:]),
                             R=[('hh', s)], W=[('pst', nt % 2)])
                    dst = hT[:, q4 * 4:q4 * 4 + 4, sub * 128:(sub + 1) * 128]
                    src = p_[:, :].rearrange("p (j t) -> p j t", j=4)
                    if nt % 2 == 0:
                        T.op('a', lambda: nc.scalar.copy(out=dst, in_=src), R=[('pst', nt % 2)], W=[('hT', sub)])
                    else:
                        T.op('v', lambda: nc.vector.tensor_copy(out=dst, in_=src), R=[('pst', nt % 2)], W=[('hT', sub)])
                    nt += 1
            if K.cfg.get('debug'):
                T.dma('g', S['HTdbg'][tb], hT[:, :, :].rearrange('p a b -> p (a b)'), 'dbgh', R=[('hT', 0), ('hT', 1), ('hT', 2), ('hT', 3)])
                T.barrier()
            for (mode, c0, ncol, d0) in blks:
                ws = nw % 2
                nw += 1
                T.dma('s', wblk[ws][:, :, :ncol], wv[:, :, c0:c0 + ncol], 'wblk%d' % ws, W=[('wblk', ws)])
                if mode == 'TM':
                    for sub in range(4):
                        r0 = tb * 512 + sub * 128
                        pi = nm % 4
                        p_ = psm[pi]
                        for kt in range(KT):
                            T.op('p', lambda: nc.tensor.matmul(p_[:, :ncol], lhsT=hT[:, kt, sub * 128:(sub + 1) * 128],
                                                               rhs=wblk[ws][:, kt, :ncol], start=(kt == 0), stop=(kt == KT - 1)),
                                 R=[('hT', sub), ('wblk', ws)], W=[('psm', pi)])
                        if nm % 2 == 0:
                            T.op('a', lambda: nc.scalar.copy(out=stg[pi][:, :ncol], in_=p_[:, :ncol]), R=[('psm', pi)], W=[('stg', pi)])
                        else:
                            T.op('v', lambda: nc.vector.tensor_copy(out=stg[pi][:, :ncol], in_=p_[:, :ncol]), R=[('psm', pi)], W=[('stg', pi)])
                        T.dma('g', S['P'][r0:r0 + 128, d0:d0 + ncol], stg[pi][:, :ncol], 'stg%d' % pi, R=[('stg', pi)])
                        nm += 1
                else:
                    for cs in range(ncol // 128):
                        pi = nm % 4
                        p_ = psm[pi]
                        for kt in range(KT):
                            T.op('p', lambda: nc.tensor.matmul(p_[:, :], lhsT=wblk[ws][:, kt, cs * 128:(cs + 1) * 128],
                                                               rhs=hT[:, kt, :], start=(kt == 0), stop=(kt == KT - 1)),
                                 R=[('hT', 0), ('hT', 1), ('hT', 2), ('hT', 3), ('wblk', ws)], W=[('psm', pi)])
                        if nm % 2 == 0:
                            T.op('a', lambda: nc.scalar.copy(out=stg[pi][:], in_=p_[:, :]), R=[('psm', pi)], W=[('stg', pi)])
                        else:
                            T.op('v', lambda: nc.vector.tensor_copy(out=stg[pi][:], in_=p_[:, :]), R=[('psm', pi)], W=[('stg', pi)])
                        T.dma('g', S['PT'][d0 + cs * 128:d0 + (cs + 1) * 128, tb * 512:(tb + 1) * 512], stg[pi][:],
                              'stg%d' % pi, R=[('stg', pi)])
                        nm += 1
        T.barrier()
        for sq_ in K.seqs:
            if sq_['pi'] is None:
                continue
            r0, L, pi = sq_['r0'], sq_['L'], sq_['pi']
            T.dma('s', K.O['out_k'][pi, l, :, :], S['P'][r0:r0 + L, 1024:2048], 'c0')
            T.dma('s', K.O['out_v'][pi, l, :, :], S['P'][r0:r0 + L, 2048:3072], 'c0')
        T.barrier()


def phase_B(K, l):
    nc, T, I, S = K.nc, K.T, K.I, K.S
    cfg = K.cfg
    Ls, PAST = cfg['Ls'], cfg['PAST']
    lam_init = 0.8 - 0.6 * math.exp(-0.3 * l)
    NKmax = PAST + Ls
    with ExitStack() as ph:
        KTt = K.sb(ph, [128, NKmax], BF16)
        QTp = K.sb(ph, [128, Ls // 256, 2, 256], BF16)
        V1 = K.sb(ph, [128, NKmax // 128, 130], BF16)
        raw = [K.sb(ph, [128, 128]) for _ in range(4)]
        vraw = [K.sb(ph, [128, 128]) for _ in range(2)]
        rot = [K.sb(ph, [128, 128]) for _ in range(2)]
        tmp = [K.sb(ph, [128, 64]) for _ in range(2)]
        tab = [K.sb(ph, [128, 128]) for _ in range(2)]
        E = [K.sb(ph, [128, 512], BF16) for _ in range(4)]
        rr = [K.sb(ph, [128, 8]) for _ in range(2)]
        o_ = [K.sb(ph, [128, 128]) for _ in range(2)]
        o2 = K.sb(ph, [128, 128])
        ga = [K.sb(ph, [128, 128]) for _ in range(2)]
        subw = K.sb(ph, [128, 128])
        lq = K.sb(ph, [128, 256])
        lt = K.sb(ph, [128, 128])
        lam = K.sb(ph, [128, 4])
        ps_s = [K.pb(ph) for _ in range(4)]
        acc = [K.pb(ph) for _ in range(4)]
        ps_t = ps_s
        T.dma('s', subw[:], I['subln_w'][l:l + 1, :].to_broadcast([128, 128]), 'c0')
        T.dma('s', lq[:], I['lam4'][l:l + 1, :].to_broadcast([128, 256]), 'c0')
        T.op('v', lambda: nc.vector.memset(V1[:], 1.0))
        T.op('v', lambda: nc.vector.memset(QTp[:], 0.0))
        T.barrier()
        T.op('v', lambda: nc.vector.tensor_scalar(out=subw[:], in0=subw[:], scalar1=(1.0 - lam_init), scalar2=None, op0=ALU.mult))
        T.op('v', lambda: nc.vector.tensor_tensor(out=lt[:, 0:64], in0=lq[:, 0:64], in1=lq[:, 64:128], op=ALU.mult))
        T.op('v', lambda: nc.vector.tensor_tensor(out=lt[:, 64:128], in0=lq[:, 128:192], in1=lq[:, 192:256], op=ALU.mult))
        T.barrier()
        T.op('v', lambda: nc.vector.tensor_reduce(out=lam[:, 0:2], in_=lt[:, :].rearrange("p (a b) -> p a b", a=2), axis=AX.X, op=ALU.add))
        T.barrier()
        T.op('a', lambda: nc.scalar.activation(out=lam[:, 0:2], in_=lam[:, 0:2], func=AF.Exp))
        T.barrier()
        T.op('v', lambda: nc.vector.scalar_tensor_tensor(out=lam[:, 2:3], in0=lam[:, 1:2], scalar=-lam_init, in1=lam[:, 0:1],
                                                         op0=ALU.add, op1=ALU.subtract))
        T.barrier()
        nlam = lam[:, 2:3]
        cnt = dict(t=0, r=0, v=0, s=0, e=0, o=0)

        def rope(src, tb_idx):
            ti = cnt['r'] % 2
            cnt['r'] += 1
            T.dma('s', tab[ti][:], I['ropetab'][tb_idx * 128:(tb_idx + 1) * 128, :], 'tab%d' % ti, W=[('tab', ti)])
            x = raw[src][:, :].rearrange("p (a b c) -> p a b c", a=4, b=2)
            x1, x2 = x[:, :, 0, :], x[:, :, 1, :]
            cosv = tab[ti][:, 0:64].rearrange("p (a c) -> p a c", a=4)
            sinv = tab[ti][:, 64:128].rearrange("p (a c) -> p a c", a=4)
            y = rot[ti][:, :].rearrange("p (a b c) -> p a b c", a=4, b=2)
            y1, y2 = y[:, :, 0, :], y[:, :, 1, :]
            t1 = tmp[0][:, :].rearrange("p (a c) -> p a c", a=4)
            t2 = tmp[1][:, :].rearrange("p (a c) -> p a c", a=4)
            Rk = [('raw', src), ('tab', ti)]
            T.op('v', lambda: nc.vector.tensor_tensor(out=y1, in0=x1, in1=cosv, op=ALU.mult), R=Rk, W=[('rot', ti)])
            T.op('v', lambda: nc.vector.tensor_tensor(out=t1, in0=x2, in1=sinv, op=ALU.mult), R=Rk, W=[('tmp', 0)])
            T.op('v', lambda: nc.vector.tensor_tensor(out=y2, in0=x2, in1=cosv, op=ALU.mult), R=Rk, W=[('rot', ti)])
            T.op('v', lambda: nc.vector.tensor_tensor(out=t2, in0=x1, in1=sinv, op=ALU.mult), R=Rk, W=[('tmp', 1)])
            T.op('v', lambda: nc.vector.tensor_tensor(out=y1, in0=y1, in1=t1, op=ALU.subtract), R=[('rot', ti), ('tmp', 0)], W=[('rot', ti)])
            T.op('v', lambda: nc.vector.tensor_tensor(out=y2, in0=y2, in1=t2, op=ALU.add), R=[('rot', ti), ('tmp', 1)], W=[('rot', ti)])
            return rot[ti], ('rot', ti)

        def transpose_to(src_tile, src_key, dst_ap, dst_key, split=None):
            ti = cnt['t'] % 2
            cnt['t'] += 1
            T.op('p', lambda: nc.tensor.transpose(out=ps_t[ti][:, 0:128], in_=src_tile[:, :], identity=K.ident[:]),
                 R=[src_key], W=[('pss', ti)])
            if split is None:
                parts = [(dst_ap, ps_t[ti][:, 0:128])]
            else:
                parts = [(split[0], ps_t[ti][0:64, 0:128]), (split[1], ps_t[ti][64:128, 0:128])]
            for (dap, sap) in parts:
                if ti == 0:
                    T.op('a', lambda: nc.scalar.copy(out=dap, in_=sap), W=[('pss', ti), dst_key])
                else:
                    T.op('v', lambda: nc.vector.tensor_copy(out=dap, in_=sap), W=[('pss', ti), dst_key])

        for sq_ in K.seqs:
            r0, L, ctx, do_rope = sq_['r0'], sq_['L'], sq_['ctx'], sq_['rope']
            nctx = ctx // 128
            nkt = nctx + L // 128
            for h in range(8):
                for j in range(nkt):
                    ri = cnt['v'] % 4
                    vi = cnt['v'] % 2
                    cnt['v'] += 1
                    if j < nctx:
                        srck = I['ck'][l, j * 128:(j + 1) * 128, h * 128:(h + 1) * 128]
                        srcv = I['cv'][l, j * 128:(j + 1) * 128, h * 128:(h + 1) * 128]
                    else:
                        rows = r0 + (j - nctx) * 128
                        srck = S['P'][rows:rows + 128, 1024 + h * 128:1024 + (h + 1) * 128]
                        srcv = S['P'][rows:rows + 128, 2048 + h * 128:2048 + (h + 1) * 128]
                    T.dma('s', raw[ri][:], srck, 'raw%d' % ri, W=[('raw', ri)])
                    T.dma('s', vraw[vi][:], srcv, 'vraw%d' % vi, W=[('vraw', vi)])
                    if do_rope and j >= nctx:
                        tl, tk = rope(ri, j - nctx)
                    else:
                        tl, tk = raw[ri], ('raw', ri)
                    transpose_to(tl, tk, KTt[:, j * 128:(j + 1) * 128], ('KT', j))
                    T.op('a', lambda: nc.scalar.copy(out=V1[:, j, 0:128], in_=vraw[vi][:]), R=[('vraw', vi)], W=[('V1', j)])
                for qt in range(L // 128):
                    ri = cnt['v'] % 4
                    cnt['v'] += 1
                    rows = r0 + qt * 128
                    T.dma('s', raw[ri][:], S['P'][rows:rows + 128, h * 128:(h + 1) * 128], 'raw%d' % ri, W=[('raw', ri)])
                    if do_rope:
                        tl, tk = rope(ri, qt)
                    else:
                        tl, tk = raw[ri], ('raw', ri)
                    qo = (qt % 2) * 128
                    transpose_to(tl, tk, None, ('QT', qt),
                                 split=(QTp[0:64, qt // 2, 0, qo:qo + 128], QTp[64:128, qt // 2, 1, qo:qo + 128]))
                its = [(qb, j) for qb in range(L // 256) for j in range(nkt)]
                DPIPE = 3

                def emit_S(n):
                    qb, j = its[n]
                    si = n % 4
                    Rq = [('KT', j), ('QT', 2 * qb), ('QT', 2 * qb + 1)]
                    T.op('p', lambda: nc.tensor.matmul(ps_s[si][:, 0:512], lhsT=KTt[:, j * 128:(j + 1) * 128],
                                                       rhs=QTp[:, qb, :, :].rearrange("p m q -> p (m q)"),
                                                       start=True, stop=True), R=Rq, W=[('pss', si)])
                    T.op('a', lambda: nc.scalar.activation(out=E[si][:, :], in_=ps_s[si][:, 0:512], func=AF.Exp, scale=0.125),
                         W=[('pss', si), ('E', si)])

                def emit_PV(n):
                    qb, j = its[n]
                    si = n % 4
                    for m in range(2):
                        for sub in range(2):
                            a_i = m * 2 + sub
                            T.op('p', lambda: nc.tensor.matmul(acc[a_i][:, 0:129],
                                                               lhsT=E[si][:, m * 256 + sub * 128:m * 256 + (sub + 1) * 128],
                                                               rhs=V1[:, j, 0:129], start=(j == 0), stop=(j == nkt - 1)),
                                 R=[('E', si), ('V1', j)], W=[('acc', a_i)])

                def epilogue(qb):
                    for sub in range(2):
                        oi = cnt['o'] % 2
                        cnt['o'] += 1
                        rows = r0 + qb * 256 + sub * 128
                        O1, O2 = acc[sub], acc[2 + sub]
                        r_ = rr[oi]
                        o = o_[oi]
                        T.dma('s', ga[oi][:], S['P'][rows:rows + 128, 3072 + h * 128:3072 + (h + 1) * 128], 'ga%d' % oi, W=[('ga', oi)])
                        T.op('a', lambda: nc.scalar.activation(out=ga[oi][:], in_=ga[oi][:], func=AF.Silu), R=[('ga', oi)], W=[('ga', oi)])
                        T.op('v', lambda: nc.vector.reciprocal(out=r_[:, 0:1], in_=O1[:, 128:129]), W=[('acc', sub), ('rr', oi)])
                        T.op('v', lambda: nc.vector.reciprocal(out=r_[:, 1:2], in_=O2[:, 128:129]), W=[('acc', 2 + sub), ('rr', oi)])
                        T.op('v', lambda: nc.vector.tensor_tensor(out=r_[:, 2:3], in0=r_[:, 1:2], in1=nlam, op=ALU.mult), R=[('rr', oi)], W=[('rr', oi)])
                        T.op('v', lambda: nc.vector.tensor_scalar(out=o[:], in0=O1[:, 0:128], scalar1=r_[:, 0:1], scalar2=None, op0=ALU.mult),
                             R=[('rr', oi)], W=[('acc', sub), ('o', oi)])
                        T.op('v', lambda: nc.vector.scalar_tensor_tensor(out=o[:], in0=O2[:, 0:128], scalar=r_[:, 2:3], in1=o[:],
                                                                         op0=ALU.mult, op1=ALU.add),
                             R=[('rr', oi), ('o', oi)], W=[('acc', 2 + sub), ('o', oi)])
                        T.op('v', lambda: nc.vector.tensor_tensor(out=o2[:], in0=o[:], in1=o[:], op=ALU.mult), R=[('o', oi)], W=['o2'])
                        T.op('v', lambda: nc.vector.tensor_reduce(out=r_[:, 3:4], in_=o2[:], axis=AX.X, op=ALU.add), R=['o2'], W=[('rr', oi)])
                        T.op('a', lambda: nc.scalar.activation(out=r_[:, 4:5], in_=r_[:, 3:4], func=AF.Sqrt, bias=EPS, scale=1.0 / 128),
                             R=[('rr', oi)], W=[('rr', oi)])
                        T.op('v', lambda: nc.vector.reciprocal(out=r_[:, 5:6], in_=r_[:, 4:5]), R=[('rr', oi)], W=[('rr', oi)])
                        T.op('v', lambda: nc.vector.scalar_tensor_tensor(out=o[:], in0=o[:], scalar=r_[:, 5:6], in1=subw[:],
                                                                         op0=ALU.mult, op1=ALU.mult), R=[('o', oi), ('rr', oi)], W=[('o', oi)])
                        T.op('v', lambda: nc.vector.tensor_tensor(out=o[:], in0=o[:], in1=ga[oi][:], op=ALU.mult),
                             R=[('o', oi), ('ga', oi)], W=[('o', oi)])
                        T.dma('g', S['ACTA'][rows:rows + 128, h * 128:(h + 1) * 128], o[:], 'oa%d' % oi, R=[('o', oi)])

                for n in range(min(DPIPE, len(its))):
                    emit_S(n)
                for n in range(len(its)):
                    if n + DPIPE < len(its):
                        emit_S(n + DPIPE)
                    emit_PV(n)
                    if its[n][1] == nkt - 1:
                        epilogue(its[n][0])
        T.barrier()


def phase_C(K, l):
    nc, T, I, S = K.nc, K.T, K.I, K.S
    Ls = K.cfg['Ls']
    NT = K.NT
    with ExitStack() as ph:
        a_t = K.sb(ph, [128, Ls])
        g_t = K.sb(ph, [128, Ls])
        zp = K.sb(ph, [128, Ls + 30])
        ac = K.sb(ph, [128, Ls])
        cw = K.sb(ph, [128, 8, 31])
        cb_ = K.sb(ph, [128, 8])
        stg = [K.sb(ph, [128, 4, 128]) for _ in range(2)]
        pst = [K.pb(ph) for _ in range(2)]
        T.dma('s', cw[:], I['conv_wT'][l].rearrange("(cb p) j -> p cb j", p=128), 'c0')
        T.dma('s', cb_[:], I['conv_bT'][l], 'c0')
        T.barrier()
        nt = 0
        for sq_ in K.seqs:
            r0, L = sq_['r0'], sq_['L']
            for cb in range(8):
                for c0_ in range(0, L, 2048):
                    c1_ = min(L, c0_ + 2048)
                    T.dma('s', a_t[:, c0_:c1_], S['PT'][cb * 128:(cb + 1) * 128, r0 + c0_:r0 + c1_], 'ca', W=['a_t'])
                    T.dma('s', g_t[:, c0_:c1_], S['PT'][1024 + cb * 128:1024 + (cb + 1) * 128, r0 + c0_:r0 + c1_], 'cg', W=['g_t'])
                T.op('a', lambda: nc.scalar.activation(out=g_t[:, :L], in_=g_t[:, :L], func=AF.Sigmoid), R=['g_t'], W=['g_t'])
                T.op('g', lambda: nc.gpsimd.memset(zp[:, 0:15], 0.0), W=['zp'])
                T.op('g', lambda: nc.gpsimd.memset(zp[:, 15 + L:30 + L], 0.0), W=['zp'])
                T.op('v', lambda: nc.vector.tensor_tensor(out=zp[:, 15:15 + L], in0=a_t[:, :L], in1=g_t[:, :L], op=ALU.mult),
                     R=['a_t', 'g_t'], W=['zp'])
                T.op('v', lambda: nc.vector.tensor_scalar(out=ac[:, :L], in0=zp[:, 0:L], scalar1=cw[:, cb, 0:1], scalar2=cb_[:, cb:cb + 1],
                                                          op0=ALU.mult, op1=ALU.add), R=['zp'], W=['ac'])
                for j in range(1, 31):
                    T.op('v', lambda: nc.vector.scalar_tensor_tensor(out=ac[:, :L], in0=zp[:, j:j + L], scalar=cw[:, cb, j:j + 1],
                                                                     in1=ac[:, :L], op0=ALU.mult, op1=ALU.add), R=['zp', 'ac'], W=['ac'])
                for t4 in range(L // 512):
                    pi = nt % 2
                    nt += 1
                    for j in range(4):
                        tt = t4 * 4 + j
                        T.op('p', lambda: nc.tensor.transpose(out=pst[pi][:, j * 128:(j + 1) * 128], in_=ac[:, tt * 128:(tt + 1) * 128],
                                                              identity=K.ident[:]), R=['ac'], W=[('pst', pi)])
                    T.op('a', lambda: nc.scalar.copy(out=stg[pi][:, :, :], in_=pst[pi][:, :].rearrange("p (j c) -> p j c", j=4)),
                         R=[('pst', pi)], W=[('stg', pi)])
                    rows = r0 + t4 * 512
                    T.dma('g', S['ZC'][rows:rows + 512, cb * 128:(cb + 1) * 128].rearrange("(j p) c -> p j c", p=128), stg[pi][:, :, :],
                          'cs%d' % pi, R=[('stg', pi)])
                if L % 512 != 0:
                    t0 = (L // 512) * 512
                    pi = nt % 2
                    nt += 1
                    nrem = (L - t0) // 128
                    for j in range(nrem):
                        tt = t0 // 128 + j
                        T.op('p', lambda: nc.tensor.transpose(out=pst[pi][:, j * 128:(j + 1) * 128], in_=ac[:, tt * 128:(tt + 1) * 128],
                                                              identity=K.ident[:]), R=['ac'], W=[('pst', pi)])
                    T.op('a', lambda: nc.scalar.copy(out=stg[pi][:, 0:nrem, :], in_=pst[pi][:, 0:nrem * 128].rearrange("p (j c) -> p j c", j=nrem)),
                         R=[('pst', pi)], W=[('stg', pi)])
                    rows = r0 + t0
                    T.dma('g', S['ZC'][rows:rows + nrem * 128, cb * 128:(cb + 1) * 128].rearrange("(j p) c -> p j c", p=128),
                          stg[pi][:, 0:nrem, :], 'cs%d' % pi, R=[('stg', pi)])
        T.barrier()
    with ExitStack() as ph:
        z = [K.sb(ph, [128, 1024]) for _ in range(2)]
        gb = [K.sb(ph, [128, 1024]) for _ in range(2)]
        lw = K.sb(ph, [128, 1024])
        lb = K.sb(ph, [128, 1024])
        bs = [K.sb(ph, [128, 12]) for _ in range(2)]
        mv = [K.sb(ph, [128, 4]) for _ in range(2)]
        T.dma('s', lw[:], I['conv_ln_w'][l:l + 1, :].to_broadcast([128, 1024]), 'c0')
        T.dma('s', lb[:], I['conv_ln_b'][l:l + 1, :].to_broadcast([128, 1024]), 'c0')
        T.barrier()
        for tt in range(NT // 128):
            s = tt % 2
            rows = tt * 128
            T.dma('s', z[s][:], S['ZC'][rows:rows + 128, :], 'z%d' % s, W=[('z', s)])
            T.dma('s', gb[s][:], S['P'][rows:rows + 128, 4096:5120], 'gb%d' % s, W=[('gb', s)])
            T.op('a', lambda: nc.scalar.activation(out=gb[s][:], in_=gb[s][:], func=AF.Silu), R=[('gb', s)], W=[('gb', s)])
            T.op('v', lambda: nc.vector.bn_stats(out=bs[s][:, 0:6], in_=z[s][:, 0:512]), R=[('z', s)], W=[('bs', s)])
            T.op('v', lambda: nc.vector.bn_stats(out=bs[s][:, 6:12], in_=z[s][:, 512:1024]), R=[('z', s)], W=[('bs', s)])
            T.op('v', lambda: nc.vector.bn_aggr(out=mv[s][:, 0:2], in_=bs[s][:, :]), R=[('bs', s)], W=[('mv', s)])
            T.op('a', lambda: nc.scalar.activation(out=mv[s][:, 2:3], in_=mv[s][:, 1:2], func=AF.Sqrt, bias=LN_EPS, scale=1.0),
                 R=[('mv', s)], W=[('mv', s)])
            T.op('v', lambda: nc.vector.reciprocal(out=mv[s][:, 3:4], in_=mv[s][:, 2:3]), R=[('mv', s)], W=[('mv', s)])
            T.op('v', lambda: nc.vector.tensor_scalar(out=z[s][:], in0=z[s][:], scalar1=mv[s][:, 0:1], scalar2=mv[s][:, 3:4],
                                                      op0=ALU.subtract, op1=ALU.mult), R=[('z', s), ('mv', s)], W=[('z', s)])
            T.op('v', lambda: nc.vector.tensor_tensor(out=z[s][:], in0=z[s][:], in1=lw[:], op=ALU.mult), R=[('z', s)], W=[('z', s)])
            T.op('v', lambda: nc.vector.tensor_tensor(out=z[s][:], in0=z[s][:], in1=lb[:], op=ALU.add), R=[('z', s)], W=[('z', s)])
            T.op('a', lambda: nc.scalar.activation(out=z[s][:], in_=z[s][:], func=AF.Silu), R=[('z', s)], W=[('z', s)])
            T.op('v', lambda: nc.vector.tensor_tensor(out=z[s][:], in0=z[s][:], in1=gb[s][:], op=ALU.mult), R=[('z', s), ('gb', s)], W=[('z', s)])
            T.dma('g', S['ACTB'][rows:rows + 128, :], z[s][:], 'zo%d' % s, R=[('z', s)])
        T.barrier()


def phase_D(K, l):
    nc, T, I, S = K.nc, K.T, K.I, K.S
    NT = K.NT
    with ExitStack() as ph:
        sb = lambda shape, dt=F32: K.sb(ph, shape, dt)
        kkb = sb([128, 1024]); kab = sb([128, 1024]); rkb = sb([128, 1024])
        w0b = sb([128, 1024]); a0b = sb([128, 1024])
        wup = sb([64, 1024]); aup = sb([64, 1024])
        mX = sb([128, 384]); mY = sb([128, 256]); tri = sb([128, 128]); nc0 = sb([128, 1])
        rkv = sb([128, 3072])
        xwT = sb([64, 128]); xaT = sb([64, 128])
        wsig = sb([128, 1024]); a_ = sb([128, 1024])
        kk = sb([128, 1024]); kd = sb([128, 1024]); b_ = sb([128, 1024])
        epos = sb([128, 1024]); eneg = sb([128, 1024]); eprv = sb([128, 1024])
        t1 = eprv; t2 = epos
        s16 = sb([128, 64])
        gC = sb([128, 8])
        bon = sb([128, 16])
        ZALL = sb([128, 16, 128], BF16 if K.cfg.get('scan_bf16', False) else F32)
        FT = [sb([128, 4, 128]) for _ in range(8)]
        SDT = BF16 if K.cfg.get('scan_bf16', False) else F32
        MXt = [sb([128, 384]) for _ in range(8)]
        MYt = [sb([128, 256]) for _ in range(8)]
        LV = [[sb([128, 384], SDT) for _ in range(2)] for _ in range(8)]
        AN0 = [sb([128, 256], SDT) for _ in range(8)]
        XW = [sb([128, 128]) for _ in range(8)]
        U0 = sb([128, 16, 64])
        WT = [sb([128, 128]) for _ in range(8)]
        UN = sb([128, 16, 64])
        STX = [sb([128, 128]) for _ in range(8)]
        ysb = sb([128, 1024])
        psb = [K.pb(ph) for _ in range(8)]
        r_ = rkv[:, 0:1024]
        k_ = rkv[:, 1024:2048]
        v_ = rkv[:, 2048:3072]

        def bc(src_row):
            return src_row.to_broadcast([128, 1024])

        T.dma('s', kkb[:], bc(I['rwkv_k_k'][l:l + 1, :]), 'c0')
        T.dma('s', kab[:], bc(I['rwkv_k_a'][l:l + 1, :]), 'c0')
        T.dma('s', rkb[:], bc(I['rwkv_r_k'][l:l + 1, :]), 'c0')
        T.op('v', lambda: nc.vector.memset(nc0[:], -C0))
        T.barrier()
        vv = lambda ap: ap.rearrange("p (h j) -> p h j", j=64)

        for d in range(2):
            T.dma('s', w0b[:], bc(I['rwkv_w0'][l, d:d + 1, :]), 'c0')
            T.dma('s', a0b[:], bc(I['rwkv_a0'][l, d:d + 1, :]), 'c0')
            T.dma('s', wup[:], I['rwkv_w_up'][l, d], 'c0')
            T.dma('s', aup[:], I['rwkv_a_up'][l, d], 'c0')
            T.dma('s', mX[:], I['maskX'][d], 'c0')
            T.dma('s', mY[:], I['maskY'][d], 'c0')
            T.dma('s', tri[:], I['tri'][d], 'c0')
            T.barrier()
            YC = S['YC0'] if d == 0 else S['YC1']
            for sq_ in K.seqs:
                r0, L, pi = sq_['r0'], sq_['L'], sq_['pi']
                nch = L // 128
                for p in range(8):
                    T.op('v', lambda: nc.vector.memset(STX[p][:], 0.0), W=[('STX', p)])
                if pi is None:
                    for p in range(8):
                        T.dma('s', STX[p][0:64, 0:64], I['st0T'][l, d, 2 * p], 'st0', W=[('STX', p)])
                        T.dma('s', STX[p][64:128, 64:128], I['st0T'][l, d, 2 * p + 1], 'st0', W=[('STX', p)])
                    T.barrier()
                order = range(nch) if d == 0 else range(nch - 1, -1, -1)
                for c in order:
                    rows = r0 + c * 128
                    for j3 in range(3):
                        T.dma('s', rkv[:, j3 * 1024:(j3 + 1) * 1024], S['P'][rows:rows + 128, 5120 + j3 * 1024:5120 + (j3 + 1) * 1024], 'rkv', W=['rkv'])
                    T.dma('s', xwT[:], S['PT'][2048 + d * 64:2048 + (d + 1) * 64, rows:rows + 128], 'xw', W=['xwT'])
                    T.dma('s', xaT[:], S['PT'][2176 + d * 64:2176 + (d + 1) * 64, rows:rows + 128], 'xa', W=['xaT'])
                    T.op('a', lambda: nc.scalar.activation(out=xwT[:], in_=xwT[:], func=AF.Tanh), R=['xwT'], W=['xwT'])
                    for hf in range(2):
                        T.op('p', lambda: nc.tensor.matmul(psb[0 + hf][:, :], lhsT=xwT[:, :], rhs=wup[:, hf * 512:(hf + 1) * 512],
                                                           start=True, stop=True), R=['xwT'], W=[('ps', hf)])
                        T.op('p', lambda: nc.tensor.matmul(psb[2 + hf][:, :], lhsT=xaT[:, :], rhs=aup[:, hf * 512:(hf + 1) * 512],
                                                           start=True, stop=True), R=['xaT'], W=[('ps', 2 + hf)])
                    for hf in range(2):
                        cs_ = slice(hf * 512, (hf + 1) * 512)
                        T.op('v', lambda: nc.vector.tensor_tensor(out=wsig[:, cs_], in0=psb[hf][:, :], in1=w0b[:, cs_], op=ALU.add),
                             W=[('ps', hf), 'wsig'])
                        T.op('v', lambda: nc.vector.tensor_tensor(out=a_[:, cs_], in0=psb[2 + hf][:, :], in1=a0b[:, cs_], op=ALU.add),
                             W=[('ps', 2 + hf), 'a'])
                    T.op('a', lambda: nc.scalar.activation(out=wsig[:], in_=wsig[:], func=AF.Sigmoid), R=['wsig'], W=['wsig'])
                    T.op('a', lambda: nc.scalar.activation(out=a_[:], in_=a_[:], func=AF.Sigmoid), R=['a'], W=['a'])
                    T.op('v', lambda: nc.vector.tensor_tensor(out=kk[:], in0=k_, in1=kkb[:], op=ALU.mult), R=['rkv'], W=['kk'])
                    T.op('g', lambda: nc.gpsimd.tensor_tensor(out=t1[:], in0=kk[:], in1=kk[:], op=ALU.mult), R=['kk'], W=['eprv'])
                    T.op('v', lambda: nc.vector.tensor_reduce(out=s16[:, 0:16], in_=vv(t1[:, :]), axis=AX.X, op=ALU.add), R=['eprv'], W=['s16'])
                    T.op('a', lambda: nc.scalar.activation(out=s16[:, 16:32], in_=s16[:, 0:16], func=AF.Sqrt), R=['s16'], W=['s16'])
                    T.op('v', lambda: nc.vector.tensor_scalar(out=s16[:, 16:32], in0=s16[:, 16:32], scalar1=1e-12, scalar2=None, op0=ALU.max),
                         R=['s16'], W=['s16'])
                    T.op('v', lambda: nc.vector.reciprocal(out=s16[:, 32:48], in_=s16[:, 16:32]), R=['s16'], W=['s16'])
                    T.op('v', lambda: nc.vector.tensor_tensor(out=vv(kk[:, :]), in0=vv(kk[:, :]),
                                                              in1=s16[:, 32:48, None].to_broadcast([128, 16, 64]), op=ALU.mult),
                         R=['kk', 's16'], W=['kk'])
                    T.op('v', lambda: nc.vector.scalar_tensor_tensor(out=t1[:], in0=a_[:], scalar=-1.0, in1=kab[:], op0=ALU.add, op1=ALU.mult),
                         R=['a', 'eprv'], W=['eprv'])
                    T.op('v', lambda: nc.vector.scalar_tensor_tensor(out=kd[:], in0=t1[:], scalar=1.0, in1=k_, op0=ALU.add, op1=ALU.mult),
                         R=['eprv', 'rkv'], W=['kd'])
                    T.op('g', lambda: nc.gpsimd.tensor_tensor(out=b_[:], in0=kk[:], in1=a_[:], op=ALU.mult), R=['kk', 'a'], W=['b'])
                    T.op('g', lambda: nc.gpsimd.tensor_tensor(out=t2[:], in0=r_, in1=rkb[:], op=ALU.mult), R=['rkv'], W=['epos'])
                    T.op('g', lambda: nc.gpsimd.tensor_tensor(out=t2[:], in0=t2[:], in1=kd[:], op=ALU.mult), R=['epos', 'kd'], W=['epos'])
                    T.op('v', lambda: nc.vector.tensor_reduce(out=bon[:], in_=vv(t2[:, :]), axis=AX.X, op=ALU.add), R=['epos'], W=['bon'])
                    T.dma('g', S['BON'][d, rows:rows + 128, :], bon[:], 'bon', R=['bon'])
                    for hf in range(2):
                        T.op('p', lambda: nc.tensor.matmul(psb[4 + hf][:, :], lhsT=tri[:, :], rhs=wsig[:, hf * 512:(hf + 1) * 512],
                                                           start=True, stop=True), R=['wsig'], W=[('ps', 4 + hf)])
                    for p in range(8):
                        T.op('p', lambda: nc.tensor.matmul(psb[6][:, p:p + 1], lhsT=wsig[:, p * 128:(p + 1) * 128], rhs=nc0[:, 0:1],
                                                           start=True, stop=True), R=['wsig'], W=[('ps', 6)])
                    T.op('a', lambda: nc.scalar.activation(out=gC[:], in_=psb[6][:, 0:8], func=AF.Exp), W=[('ps', 6), 'gC'])
                    for hf in range(2):
                        cs_ = slice(hf * 512, (hf + 1) * 512)
                        T.op('a', lambda: nc.scalar.activation(out=epos[:, cs_], in_=psb[4 + hf][:, :], func=AF.Exp), W=[('ps', 4 + hf), 'epos'])
                        T.op('a', lambda: nc.scalar.activation(out=eneg[:, cs_], in_=psb[4 + hf][:, :], func=AF.Exp, scale=-1.0),
                             W=[('ps', 4 + hf), 'eneg'])
                        T.op('v', lambda: nc.vector.scalar_tensor_tensor(out=eprv[:, cs_], in0=wsig[:, cs_], scalar=C0, in1=psb[4 + hf][:, :],
                                                                         op0=ALU.mult, op1=ALU.add), R=['wsig'], W=[('ps', 4 + hf), 'eprv'])
                    T.op('a', lambda: nc.scalar.activation(out=eprv[:], in_=eprv[:], func=AF.Exp), R=['eprv'], W=['eprv'])
                    T.op('v', lambda: nc.vector.tensor_tensor(out=epos[:], in0=epos[:], in1=r_, op=ALU.mult), R=['epos', 'rkv'], W=['epos'])
                    T.op('g', lambda: nc.gpsimd.tensor_tensor(out=eprv[:], in0=eprv[:], in1=kk[:], op=ALU.mult), R=['eprv', 'kk'], W=['eprv'])
                    T.op('v', lambda: nc.vector.tensor_tensor(out=b_[:], in0=b_[:], in1=eneg[:], op=ALU.mult), R=['b', 'eneg'], W=['b'])
                    T.op('g', lambda: nc.gpsimd.tensor_tensor(out=kd[:], in0=kd[:], in1=eneg[:], op=ALU.mult), R=['kd', 'eneg'], W=['kd'])
                    T.op('a', lambda: nc.scalar.copy(out=ZALL[:, :, 0:64], in_=vv(eprv[:, :])), R=['eprv'], W=['ZALLk'] + [('ZALL', h_) for h_ in range(16)])
                    for p in range(8):
                        pp = psb[p % 4]
                        cs_ = slice(p * 128, (p + 1) * 128)
                        for q, (src, key) in enumerate(((epos, 'epos'), (eprv, 'eprv'), (b_, 'b'), (kd, 'kd'))):
                            T.op('p', lambda: nc.tensor.transpose(out=pp[:, q * 128:(q + 1) * 128], in_=src[:, cs_], identity=K.ident[:]),
                                 R=[key], W=[('ps', p % 4)])
                        if p % 2 == 0:
                            T.op('v', lambda: nc.vector.tensor_copy(out=FT[p][:, :, :], in_=pp[:, :].rearrange("p (q t) -> p q t", q=4)),
                                 W=[('ps', p % 4), ('FT', p)])
                        else:
                            T.op('a', lambda: nc.scalar.copy(out=FT[p][:, :, :], in_=pp[:, :].rearrange("p (q t) -> p q t", q=4)),
                                 W=[('ps', p % 4), ('FT', p)])
                    for q2 in range(2):
                        grp = [(2 * q2, 0), (2 * q2 + 1, 4)]
                        hl = [(bo + i, 4 * hg + i) for hg, bo in grp for i in range(4)]
                        for bi, h in hl:
                            p = h // 2
                            sl = slice((h % 2) * 64, (h % 2) * 64 + 64)
                            rT_kpT = FT[p][sl, 0:2, :].rearrange("p a t -> p (a t)")
                            T.op('p', lambda: nc.tensor.matmul(psb[bi][:, 0:128], lhsT=FT[p][sl, 1, :], rhs=FT[p][sl, 2, :], start=True, stop=True),
                                 R=[('FT', p)], W=[('ps', bi)])
                            T.op('p', lambda: nc.tensor.matmul(psb[bi][:, 128:384], lhsT=FT[p][sl, 2, :], rhs=rT_kpT, start=True, stop=True),
                                 R=[('FT', p)], W=[('ps', bi)])
                        for bi, h in hl:
                            T.op('v', lambda: nc.vector.tensor_tensor(out=MXt[bi][:], in0=psb[bi][:, 0:384], in1=mX[:], op=ALU.mult),
                                 W=[('ps', bi), ('MX', bi)])
                            if SDT != F32:
                                T.op('g', lambda: nc.gpsimd.tensor_copy(out=AN0[bi][:, :].rearrange("p (a c) -> p a c", a=2),
                                                                        in_=MXt[bi][:, :].rearrange("p (a c) -> p a c", a=3)[:, 0:3:2, :]),
                                     R=[('MX', bi)], W=[('AN0', bi)])
                        for bi, h in hl:
                            p = h // 2
                            sl = slice((h % 2) * 64, (h % 2) * 64 + 64)
                            rT_kpT = FT[p][sl, 0:2, :].rearrange("p a t -> p (a t)")
                            T.op('p', lambda: nc.tensor.matmul(psb[bi][:, 0:256], lhsT=FT[p][sl, 3, :], rhs=rT_kpT, start=True, stop=True),
                                 R=[('FT', p)], W=[('ps', bi)])
                        for bi, h in hl:
                            T.op('v', lambda: nc.vector.tensor_tensor(out=MYt[bi][:], in0=psb[bi][:, 0:256], in1=mY[:], op=ALU.mult),
                                 W=[('ps', bi), ('MY', bi)])
                        for bi, h in hl:
                            T.op('p', lambda: nc.tensor.matmul(psb[bi][:, 0:64], lhsT=MYt[bi][:, 128:256], rhs=v_[:, h * 64:(h + 1) * 64],
                                                               start=True, stop=True), R=[('MY', bi), 'rkv'], W=[('ps', bi)])
                        for bi, h in hl:
                            T.op('a', lambda: nc.scalar.copy(out=ZALL[:, h, 64:128], in_=psb[bi][:, 0:64]), W=[('ps', bi), ('ZALL', h)])
                        v3 = lambda ap: ap.rearrange("p (a c) -> p a c", a=3)
                        for lv in range(7):
                            for hg, bo in grp:
                                hs = [4 * hg + i for i in range(4)]
                                for i, h in enumerate(hs):
                                    pb_ = psb[bo + i]
                                    if lv == 0:
                                        if SDT != F32:
                                            A_ap, N_ap, Akey = AN0[bo + i][:, 0:128], AN0[bo + i][:, 128:256], ('AN0', bo + i)
                                        else:
                                            A_ap, N_ap, Akey = MXt[bo + i][:, 0:128], MXt[bo + i][:, 256:384], ('MX', bo + i)
                                        T.op('p', lambda: nc.tensor.matmul(pb_[:, 128:256], lhsT=N_ap, rhs=ZALL[:, h, :], start=True, stop=True),
                                             R=[Akey, ('ZALL', h), 'ZALLk'], W=[('ps', bo + i)])
                                        T.op('p', lambda: nc.tensor.matmul(pb_[:, 0:128], lhsT=N_ap, rhs=A_ap, start=True, stop=True),
                                             R=[Akey], W=[('ps', bo + i)])
                                        T.op('p', lambda: nc.tensor.matmul(pb_[:, 256:384], lhsT=A_ap, rhs=N_ap, start=True, stop=True),
                                             R=[Akey], W=[('ps', bo + i)])
                                    else:
                                        lvt = LV[bo + i][(lv - 1) % 2]
                                        Lkey = ('LV', bo + i, (lv - 1) % 2)
                                        if lv < 6:
                                            T.op('p', lambda: nc.tensor.matmul(pb_[:, 0:256], lhsT=lvt[:, 256:384], rhs=lvt[:, 0:256], start=True, stop=True),
                                                 R=[Lkey], W=[('ps', bo + i)])
                                            T.op('p', lambda: nc.tensor.matmul(pb_[:, 256:384], lhsT=lvt[:, 0:128], rhs=lvt[:, 256:384], start=True, stop=True),
                                                 R=[Lkey], W=[('ps', bo + i)])
                                        else:
                                            T.op('p', lambda: nc.tensor.matmul(pb_[:, 128:256], lhsT=lvt[:, 256:384], rhs=lvt[:, 128:256], start=True, stop=True),
                                                 R=[Lkey], W=[('ps', bo + i)])
                            for hg, bo in grp:
                                hs = [4 * hg + i for i in range(4)]
                                for i, h in enumerate(hs):
                                    p = h // 2
                                    pb_ = psb[bo + i]
                                    if lv == 0:
                                        Xin, Xkey = ZALL[:, h, :], ('ZALL', h)
                                    else:
                                        Xin, Xkey = LV[bo + i][(lv - 1) % 2][:, 128:256], ('LV', bo + i, (lv - 1) % 2)
                                    opx = ALU.subtract if lv == 0 else ALU.add
                                    if lv < 6:
                                        lvo = LV[bo + i][lv % 2]
                                        T.op('v', lambda: nc.vector.tensor_tensor(out=lvo[:, 128:256], in0=Xin, in1=pb_[:, 128:256], op=opx),
                                             R=[Xkey, 'ZALLk'], W=[('ps', bo + i), ('LV', bo + i, lv % 2)])
                                        T.op('v', lambda: nc.vector.tensor_copy(out=v3(lvo[:, :])[:, 0:3:2, :], in_=v3(pb_[:, 0:384])[:, 0:3:2, :]),
                                             W=[('ps', bo + i), ('LV', bo + i, lv % 2)])
                                    else:
                                        T.op('v', lambda: nc.vector.tensor_tensor(out=XW[p][:, (h % 2) * 64:(h % 2) * 64 + 64], in0=Xin[:, 0:64],
                                                                                  in1=pb_[:, 128:192], op=opx),
                                             R=[Xkey], W=[('ps', bo + i), ('XW', p)])
                                        T.op('v', lambda: nc.vector.tensor_tensor(out=U0[:, h, :], in0=Xin[:, 64:128], in1=pb_[:, 192:256], op=opx),
                                             R=[Xkey], W=[('ps', bo + i), ('U0', h)])
                        pairs = [(hg * 2 + pi2, bo + 2 * pi2) for hg, bo in grp for pi2 in range(2)]
                        for p, be in pairs:
                            T.op('p', lambda: nc.tensor.transpose(out=psb[be][:, 0:128], in_=XW[p][:, :], identity=K.ident[:]),
                                 R=[('XW', p)], W=[('ps', be)])
                        for p, be in pairs:
                            T.op('a', lambda: nc.scalar.copy(out=WT[p][:, :], in_=psb[be][:, 0:128]), W=[('ps', be), ('WT', p)])
                        for p, be in pairs:
                            for j2 in range(2):
                                sl = slice(j2 * 64, j2 * 64 + 64)
                                T.op('p', lambda: nc.tensor.matmul(psb[be + j2][:, 0:64], lhsT=WT[p][sl, :], rhs=STX[p][sl, sl], start=True, stop=True),
                                     R=[('WT', p), ('STX', p)], W=[('ps', be + j2)])
                        for p, be in pairs:
                            for j2 in range(2):
                                h = 2 * p + j2
                                T.op('v', lambda: nc.vector.scalar_tensor_tensor(out=UN[:, h, :], in0=psb[be + j2][:, 0:64], scalar=-1.0, in1=U0[:, h, :],
                                                                                 op0=ALU.mult, op1=ALU.subtract),
                                     R=[('U0', h)], W=[('ps', be + j2), ('UN', h)])
                        for p, be in pairs:
                            for j2 in range(2):
                                h = 2 * p + j2
                                bi = be + j2
                                sl = slice(j2 * 64, j2 * 64 + 64)
                                yo = psb[be + 1][:, 256 + j2 * 64:256 + (j2 + 1) * 64]
                                T.op('p', lambda: nc.tensor.matmul(yo, lhsT=FT[p][sl, 0, :], rhs=STX[p][sl, sl], start=True, stop=False),
                                     R=[('FT', p), ('STX', p)], W=[('ps', be + 1)])
                                T.op('p', lambda: nc.tensor.matmul(yo, lhsT=MYt[bi][:, 0:128], rhs=v_[:, h * 64:(h + 1) * 64], start=False, stop=False),
                                     R=[('MY', bi), 'rkv'], W=[('ps', be + 1)])
                                T.op('p', lambda: nc.tensor.matmul(yo, lhsT=MXt[bi][:, 128:256], rhs=UN[:, h, :], start=False, stop=True),
                                     R=[('MX', bi), ('UN', h)], W=[('ps', be + 1)])
                            cs_ = slice(p * 128, (p + 1) * 128)
                            pS = psb[be][:, 128:256]
                            T.op('p', lambda: nc.tensor.matmul(pS, lhsT=kd[:, cs_], rhs=v_[:, cs_], start=True, stop=False),
                                 R=['kd', 'rkv'], W=[('ps', be)])
                            T.op('p', lambda: nc.tensor.matmul(pS, lhsT=b_[:, cs_], rhs=UN[:, 2 * p:2 * p + 2, :].rearrange("p a j -> p (a j)"),
                                                               start=False, stop=False), R=['b', ('UN', 2 * p), ('UN', 2 * p + 1)], W=[('ps', be)])
                            T.op('p', lambda: nc.tensor.matmul(pS, lhsT=K.ident[:, :], rhs=STX[p][:, :], start=False, stop=True),
                                 R=[('STX', p)], W=[('ps', be)])
                        for p, be in pairs:
                            T.op('a', lambda: nc.scalar.copy(out=ysb[:, p * 128:(p + 1) * 128], in_=psb[be + 1][:, 256:384]),
                                 W=[('ps', be + 1), ('ysb', p)])
                            T.op('v', lambda: nc.vector.tensor_scalar(out=STX[p][0:64, 0:64], in0=psb[be][0:64, 128:192], scalar1=gC[0:64, p:p + 1],
                                                                      scalar2=None, op0=ALU.mult), R=['gC'], W=[('ps', be), ('STX', p)])
                            T.op('v', lambda: nc.vector.tensor_scalar(out=STX[p][64:128, 64:128], in0=psb[be][64:128, 192:256], scalar1=gC[64:128, p:p + 1],
                                                                      scalar2=None, op0=ALU.mult), R=['gC'], W=[('ps', be), ('STX', p)])
                    T.dma('g', YC[rows:rows + 128, :], ysb[:], 'ysb', R=[('ysb', pp_) for pp_ in range(8)])
                if pi is not None:
                    for p in range(8):
                        T.dma('g', K.O['out_st'][pi, l, d, 2 * p], STX[p][0:64, 0:64], 'sto', R=[('STX', p)])
                        T.dma('g', K.O['out_st'][pi, l, d, 2 * p + 1], STX[p][64:128, 64:128], 'sto', R=[('STX', p)])
                    T.barrier()
            T.barrier()
    with ExitStack() as ph:
        y0 = [K.sb(ph, [128, 1024]) for _ in range(2)]
        y1 = [K.sb(ph, [128, 1024]) for _ in range(2)]
        vg = [K.sb(ph, [128, 2048]) for _ in range(2)]
        bo = [K.sb(ph, [128, 32]) for _ in range(2)]
        ysq = K.sb(ph, [128, 1024])
        gw = K.sb(ph, [128, 1024]); gbb = K.sb(ph, [128, 1024])
        sst = [K.sb(ph, [128, 96]) for _ in range(2)]
        T.dma('s', gw[:], I['rwkv_gn_w'][l:l + 1, :].to_broadcast([128, 1024]), 'c0')
        T.dma('s', gbb[:], I['rwkv_gn_b'][l:l + 1, :].to_broadcast([128, 1024]), 'c0')
        T.barrier()
        vv = lambda ap: ap.rearrange("p (h j) -> p h j", j=64)
        bc16 = lambda ap: ap[:, :, None].to_broadcast([128, 16, 64])
        for tt in range(NT // 128):
            s = tt % 2
            rows = tt * 128
            T.dma('s', y0[s][:], S['YC0'][rows:rows + 128, :], 'y0%d' % s, W=[('y0', s)])
            T.dma('s', y1[s][:], S['YC1'][rows:rows + 128, :], 'y1%d' % s, W=[('y1', s)])
            T.dma('s', vg[s][:], S['P'][rows:rows + 128, 7168:9216], 'vg%d' % s, W=[('vg', s)])
            T.dma('s', bo[s][:, 0:16], S['BON'][0, rows:rows + 128, :], 'bo%d' % s, W=[('bo', s)])
            T.dma('s', bo[s][:, 16:32], S['BON'][1, rows:rows + 128, :], 'bo%d' % s, W=[('bo', s)])
            y = y0[s]
            ss = sst[s]
            T.op('v', lambda: nc.vector.tensor_tensor(out=y[:], in0=y[:], in1=y1[s][:], op=ALU.add), R=[('y0', s), ('y1', s)], W=[('y0', s)])
            T.op('g', lambda: nc.gpsimd.tensor_tensor(out=ysq[:], in0=y[:], in1=y[:], op=ALU.mult), R=[('y0', s)], W=['ysq'])
            T.op('v', lambda: nc.vector.tensor_reduce(out=ss[:, 0:16], in_=vv(y[:, :]), axis=AX.X, op=ALU.add), R=[('y0', s)], W=[('ss', s)])
            T.op('v', lambda: nc.vector.tensor_reduce(out=ss[:, 16:32], in_=vv(ysq[:, :]), axis=AX.X, op=ALU.add), R=['ysq'], W=[('ss', s)])
            T.op('v', lambda: nc.vector.tensor_scalar(out=ss[:, 32:48], in0=ss[:, 0:16], scalar1=1.0 / 64, scalar2=None, op0=ALU.mult), R=[('ss', s)], W=[('ss', s)])
            T.op('v', lambda: nc.vector.tensor_tensor(out=ss[:, 48:64], in0=ss[:, 32:48], in1=ss[:, 32:48], op=ALU.mult), R=[('ss', s)], W=[('ss', s)])
            T.op('v', lambda: nc.vector.scalar_tensor_tensor(out=ss[:, 64:80], in0=ss[:, 16:32], scalar=1.0 / 64, in1=ss[:, 48:64],
                                                             op0=ALU.mult, op1=ALU.subtract), R=[('ss', s)], W=[('ss', s)])
            T.op('a', lambda: nc.scalar.activation(out=ss[:, 64:80], in_=ss[:, 64:80], func=AF.Sqrt, bias=GN_EPS, scale=1.0), R=[('ss', s)], W=[('ss', s)])
            T.op('v', lambda: nc.vector.reciprocal(out=ss[:, 80:96], in_=ss[:, 64:80]), R=[('ss', s)], W=[('ss', s)])
            T.op('v', lambda: nc.vector.tensor_tensor(out=vv(y[:, :]), in0=vv(y[:, :]), in1=bc16(ss[:, 32:48]), op=ALU.subtract),
                 R=[('y0', s), ('ss', s)], W=[('y0', s)])
            T.op('v', lambda: nc.vector.tensor_tensor(out=vv(y[:, :]), in0=vv(y[:, :]), in1=bc16(ss[:, 80:96]), op=ALU.mult),
                 R=[('y0', s), ('ss', s)], W=[('y0', s)])
            T.op('g', lambda: nc.gpsimd.tensor_tensor(out=y[:], in0=y[:], in1=gw[:], op=ALU.mult), R=[('y0', s)], W=[('y0', s)])
            T.op('g', lambda: nc.gpsimd.tensor_tensor(out=y[:], in0=y[:], in1=gbb[:], op=ALU.add), R=[('y0', s)], W=[('y0', s)])
            T.op('v', lambda: nc.vector.tensor_tensor(out=bo[s][:, 0:16], in0=bo[s][:, 0:16], in1=bo[s][:, 16:32], op=ALU.add), R=[('bo', s)], W=[('bo', s)])
            T.op('v', lambda: nc.vector.tensor_tensor(out=vv(ysq[:, :]), in0=vv(vg[s][:, 0:1024]), in1=bc16(bo[s][:, 0:16]), op=ALU.mult),
                 R=[('vg', s), ('bo', s)], W=['ysq'])
            T.op('v', lambda: nc.vector.tensor_tensor(out=y[:], in0=y[:], in1=ysq[:], op=ALU.add), R=[('y0', s), 'ysq'], W=[('y0', s)])
            T.op('a', lambda: nc.scalar.activation(out=vg[s][:, 1024:2048], in_=vg[s][:, 1024:2048], func=AF.Silu), R=[('vg', s)], W=[('vg', s)])
            T.op('v', lambda: nc.vector.tensor_tensor(out=y[:], in0=y[:], in1=vg[s][:, 1024:2048], op=ALU.mult), R=[('y0', s), ('vg', s)], W=[('y0', s)])
            T.dma('g', S['ACTC'][rows:rows + 128, :], y[:], 'yo%d' % s, R=[('y0', s)])
        T.barrier()


def phase_E(K, l):
    nc, T, I, S = K.nc, K.T, K.I, K.S
    NT, Ls = K.NT, K.cfg['Ls']
    TB = 256
    last = (l == DEPTH - 1)
    with ExitStack() as ph:
        ain = [K.sb(ph, [128, 1024]) for _ in range(2)]
        aT = [K.sb(ph, [128, 8, TB], BF16) for _ in range(3)]
        mT = K.sb(ph, [128, KT, TB], BF16)
        wbr = [K.sb(ph, [128, 8, 512], BF16) for _ in range(3)]
        wo = [K.sb(ph, [128, KT, 512], BF16) for _ in range(2)]
        gt = [K.sb(ph, [128, TB]) for _ in range(3)]
        mm = K.sb(ph, [128, TB])
        mt = K.sb(ph, [128, TB])
        xt = [K.sb(ph, [128, D]) for _ in range(2)]
        tmpo = K.sb(ph, [128, 512])
        sq = K.sb(ph, [128, D])
        st = K.sb(ph, [128, 4])
        fnw = K.sb(ph, [128, D])
        pst = [K.pb(ph) for _ in range(2)]
        psy = [K.pb(ph) for _ in range(3)]
        pso = [K.pb(ph) for _ in range(2)]
        T.dma('s', fnw[:], I['final_norm_w'][0:1, :].to_broadcast([128, D]), 'c0')
        GATE = [K.sb(ph, [128, D]) for _ in range(2)]
        for g in range(2):
            T.dma('s', GATE[g][:], S['ADAd'][g][:, 2 * D:3 * D], 'c0')
        T.barrier()
        acts = [S['ACTA'], S['ACTB'], S['ACTC']]
        wsrc = [S['Wa'][l].rearrange("(kt p) c -> p kt c", p=128), S['Wbb'][l].rearrange("(kt p) c -> p kt c", p=128),
                S['Wc'][l].rearrange("(kt p) c -> p kt c", p=128)]
        wov = S['Wo'][l].rearrange("(kt p) c -> p kt c", p=128)
        na = 0
        nt = 0
        no = 0
        for tb in range(NT // TB):
            g = 0 if tb * TB < Ls else 1
            gate_g = GATE[g][:, :]
            for br in range(3):
                for sub in range(TB // 128):
                    rows = tb * TB + sub * 128
                    s = na % 2
                    na += 1
                    T.dma('s', ain[s][:], acts[br][rows:rows + 128, :], 'ain%d' % s, W=[('ain', s)])
                    for q4 in range(2):
                        pi = nt % 2
                        nt += 1
                        for j in range(4):
                            kt = q4 * 4 + j
                            T.op('p', lambda: nc.tensor.transpose(out=pst[pi][:, j * 128:(j + 1) * 128], in_=ain[s][:, kt * 128:(kt + 1) * 128],
                                                                  identity=K.ident[:]), R=[('ain', s)], W=[('pst', pi)])
                        dst = aT[br][:, q4 * 4:q4 * 4 + 4, sub * 128:(sub + 1) * 128]
                        src = pst[pi][:, :].rearrange("p (j t) -> p j t", j=4)
                        if pi == 0:
                            T.op('a', lambda: nc.scalar.copy(out=dst, in_=src), R=[('pst', pi)], W=[('aT', br)])
                        else:
                            T.op('v', lambda: nc.vector.tensor_copy(out=dst, in_=src), R=[('pst', pi)], W=[('aT', br)])
            for cg in range(4):
                for br in range(3):
                    T.dma('s', wbr[br][:], wsrc[br][:, :, cg * 512:(cg + 1) * 512], 'wbr%d' % br, W=[('wbr', br)])
                for c4 in range(4):
                    ct = cg * 4 + c4
                    for br in range(3):
                        T.dma('s', gt[br][:], S['PT'][2304 + br * D + ct * 128:2304 + br * D + (ct + 1) * 128, tb * TB:(tb + 1) * TB],
                              'gt%d' % br, W=[('gt', br)])
                        T.op('a', lambda: nc.scalar.activation(out=gt[br][:], in_=gt[br][:], func=AF.Sigmoid), R=[('gt', br)], W=[('gt', br)])
                        for kt in range(8):
                            T.op('p', lambda: nc.tensor.matmul(psy[br][:, 0:TB], lhsT=wbr[br][:, kt, c4 * 128:(c4 + 1) * 128], rhs=aT[br][:, kt, :],
                                                               start=(kt == 0), stop=(kt == 7)), R=[('wbr', br), ('aT', br)], W=[('psy', br)])
                    T.op('v', lambda: nc.vector.tensor_tensor(out=mm[:], in0=psy[0][:, 0:TB], in1=gt[0][:], op=ALU.mult), R=[('psy', 0), ('gt', 0)], W=['mm'])
                    T.op('v', lambda: nc.vector.tensor_tensor(out=mt[:], in0=psy[1][:, 0:TB], in1=gt[1][:], op=ALU.mult), R=[('psy', 1), ('gt', 1)], W=['mt'])
                    T.op('g', lambda: nc.gpsimd.tensor_tensor(out=mm[:], in0=mm[:], in1=mt[:], op=ALU.add), R=['mm', 'mt'], W=['mm'])
                    T.op('v', lambda: nc.vector.tensor_tensor(out=mt[:], in0=psy[2][:, 0:TB], in1=gt[2][:], op=ALU.mult), R=[('psy', 2), ('gt', 2)], W=['mt'])
                    T.op('v', lambda: nc.vector.tensor_tensor(out=mT[:, ct, :], in0=mm[:], in1=mt[:], op=ALU.add), R=['mm', 'mt'], W=['mT'])
            for sub in range(TB // 128):
                rows = tb * TB + sub * 128
                T.dma('s', xt[sub][:], S['X'][rows:rows + 128, :], 'xt%d' % sub, W=[('xt', sub)])
            for cb in range(4):
                ws = no % 2
                no += 1
                T.dma('s', wo[ws][:], wov[:, :, cb * 512:(cb + 1) * 512], 'wo%d' % ws, W=[('wo', ws)])
                for sub in range(TB // 128):
                    pi = (cb * 2 + sub) % 2
                    for kt in range(KT):
                        T.op('p', lambda: nc.tensor.matmul(pso[pi][:, :], lhsT=mT[:, kt, sub * 128:(sub + 1) * 128], rhs=wo[ws][:, kt, :],
                                                           start=(kt == 0), stop=(kt == KT - 1)), R=['mT', ('wo', ws)], W=[('pso', pi)])
                    cs_ = slice(cb * 512, (cb + 1) * 512)
                    T.op('v', lambda: nc.vector.tensor_tensor(out=tmpo[:], in0=pso[pi][:, :], in1=gate_g[:, cs_], op=ALU.mult),
                         R=[('pso', pi)], W=['tmpo'])
                    T.op('v', lambda: nc.vector.tensor_tensor(out=xt[sub][:, cs_], in0=xt[sub][:, cs_], in1=tmpo[:], op=ALU.add),
                         R=['tmpo', ('xt', sub)], W=[('xt', sub)])
            for sub in range(TB // 128):
                rows = tb * TB + sub * 128
                if not last:
                    T.dma('g', S['X'][rows:rows + 128, :], xt[sub][:], 'xo%d' % sub, R=[('xt', sub)])
                else:
                    T.op('v', lambda: nc.vector.tensor_tensor(out=sq[:], in0=xt[sub][:], in1=xt[sub][:], op=ALU.mult), R=[('xt', sub)], W=['sq'])
                    T.op('v', lambda: nc.vector.tensor_reduce(out=st[:, 0:1], in_=sq[:], axis=AX.X, op=ALU.add), R=['sq'], W=['st'])
                    T.op('a', lambda: nc.scalar.activation(out=st[:, 1:2], in_=st[:, 0:1], func=AF.Sqrt, bias=EPS, scale=1.0 / D), R=['st'], W=['st'])
                    T.op('v', lambda: nc.vector.reciprocal(out=st[:, 2:3], in_=st[:, 1:2]), R=['st'], W=['st'])
                    T.op('v', lambda: nc.vector.scalar_tensor_tensor(out=xt[sub][:], in0=xt[sub][:], scalar=st[:, 2:3], in1=fnw[:],
                                                                     op0=ALU.mult, op1=ALU.mult), R=[('xt', sub), 'st'], W=[('xt', sub)])
                    T.dma('g', K.O['y_all'][rows:rows + 128, :], xt[sub][:], 'xo%d' % sub, R=[('xt', sub)])
        T.barrier()


def host_consts(cfg):
    Ls = cfg['Ls']
    GRID_W = 64
    t = np.arange(Ls)
    row = (t // GRID_W).astype(np.float32)
    col = (t % GRID_W).astype(np.float32)
    inv = (10000.0 ** (-np.arange(16, dtype=np.float32) / 16)).astype(np.float32)
    ang_r = row[:, None] * inv[None, :]
    ang_c = col[:, None] * inv[None, :]
    cos4 = np.stack([np.cos(ang_r), np.cos(ang_c), np.cos(ang_r), np.cos(ang_c)], axis=1)
    sin4 = np.stack([np.sin(ang_r), np.sin(ang_c), np.sin(ang_r), np.sin(ang_c)], axis=1)
    ropetab = np.concatenate([cos4.reshape(Ls, 64), sin4.reshape(Ls, 64)], axis=1).astype(np.float32)
    i = np.arange(128)
    P_, F_ = i[:, None], i[None, :]
    SL = (F_ < P_).astype(np.float32)
    SU = (F_ > P_).astype(np.float32)
    LI = (F_ <= P_).astype(np.float32)
    UI = (F_ >= P_).astype(np.float32)
    maskX = np.stack([np.concatenate([SL, UI, SU], 1), np.concatenate([SU, LI, SL], 1)])
    maskY = np.stack([np.concatenate([UI, SU], 1), np.concatenate([LI, SL], 1)])
    tri = np.stack([-C0 * UI, -C0 * LI]).astype(np.float32)
    return dict(ident=np.eye(128, dtype=np.float32), ropetab=ropetab, maskX=maskX.astype(np.float32),
                maskY=maskY.astype(np.float32), tri=tri)


_CACHE = {}


def run(cfg, inp):
    Ls, Lp, NP, PAST = cfg['Ls'], cfg['Lp'], cfg['NP'], cfg['PAST']
    key = (Ls, Lp, NP, PAST, cfg.get('scan_bf16', False), cfg.get('stop', 'Z'), cfg.get('nl', DEPTH), tuple(cfg.get('taps', ())), cfg.get('debug', False))
    if key not in _CACHE:
        _CACHE[key] = build(cfg)
    nc = _CACHE[key]
    f = lambda a: np.ascontiguousarray(np.asarray(a, dtype=np.float32))
    hc = host_consts(cfg)
    shared = dict(
        w_ada=f(inp['w_ada']), b_ada=f(inp['b_ada']), norm_w=f(inp['norm_w']), w_in=f(inp['w_in']),
        lam4=f(np.concatenate([inp['lambda_q1'], inp['lambda_k1'], inp['lambda_q2'], inp['lambda_k2']], axis=1)),
        subln_w=f(inp['subln_w']),
        conv_wT=f(np.transpose(inp['conv_w'], (0, 2, 1))),
        conv_bT=f(np.transpose(np.asarray(inp['conv_b']).reshape(DEPTH, 8, 128), (0, 2, 1))),
        conv_ln_w=f(inp['conv_ln_w']), conv_ln_b=f(inp['conv_ln_b']),
        rwkv_w0=f(inp['rwkv_w0']), rwkv_w_up=f(inp['rwkv_w_up']), rwkv_a0=f(inp['rwkv_a0']), rwkv_a_up=f(inp['rwkv_a_up']),
        rwkv_k_k=f(inp['rwkv_k_k']), rwkv_k_a=f(inp['rwkv_k_a']), rwkv_r_k=f(np.asarray(inp['rwkv_r_k']).reshape(DEPTH, 1024)),
        rwkv_gn_w=f(inp['rwkv_gn_w']), rwkv_gn_b=f(inp['rwkv_gn_b']),
        w_br_a=f(inp['w_br_a']), w_br_b=f(inp['w_br_b']), w_br_c=f(inp['w_br_c']), w_out=f(inp['w_out']),
        final_norm_w=f(np.asarray(inp['final_norm_w']).reshape(1, D)),
        **hc)
    xs, xp = np.asarray(inp['x_sample']), np.asarray(inp['x_prompt'])
    in_maps = []
    ncores = cfg.get('ncores', 8)
    for c in range(ncores):
        b = c // 4
        m = dict(shared)
        m['x_all'] = f(np.concatenate([xs[b], xp[c * NP:(c + 1) * NP].reshape(NP * Lp, D)], axis=0))
        cv2 = np.stack([np.asarray(inp['c'])[b], np.asarray(inp['c_ctx'])], axis=0)
        m['cvecT'] = f(np.transpose(cv2.reshape(2, KT, 128), (2, 0, 1)).reshape(128, 32))
        m['ck'] = f(np.asarray(inp['cache_k'])[b].reshape(DEPTH, PAST, 1024))
        m['cv'] = f(np.asarray(inp['cache_v'])[b].reshape(DEPTH, PAST, 1024))
        m['st0T'] = f(np.transpose(np.asarray(inp['state_rwkv'])[b], (0, 1, 2, 4, 3)))
        in_maps.append(m)
    res = run_bass_kernel_spmd(nc, in_maps, core_ids=list(range(ncores)))
    R = res.results
    if cfg.get('raw'):
        return R
    nb = xs.shape[0]
    y_sample = np.stack([R[4 * b]['y_all'][:Ls] for b in range(nb)], axis=0).astype(np.float32)
    y_prompt = np.concatenate([R[c]['y_all'][Ls:].reshape(NP, Lp, D) for c in range(8)], axis=0).astype(np.float32)
    nk = np.concatenate([R[c]['out_k'] for c in range(8)], axis=0).reshape(8 * NP, DEPTH, Lp, 8, 2, 64).astype(np.float32)
    nv = np.concatenate([R[c]['out_v'] for c in range(8)], axis=0).reshape(8 * NP, DEPTH, Lp, 8, 128).astype(np.float32)
    ns = np.concatenate([R[c]['out_st'] for c in range(8)], axis=0)
    ns = np.ascontiguousarray(np.transpose(ns, (0, 1, 2, 3, 5, 4))).astype(np.float32)
    return (y_prompt, y_sample, nk, nv, ns)


def kernel(**inputs):
    cfg = dict(Ls=4096, Lp=256, NP=4, PAST=512)
    return run(cfg, inputs)
```
